# Optimizing a Trainium2 kernel written in Bass

```python
import math
import jax
import jax.numpy as jnp
from jax import lax
import numpy as np

D_MODEL = 1024
BATCH = 8
SEQ = 2048
DEPTH = 4

GRID_W = 64
CTX_LEN = 256
CHUNK = 64
NORM_EPS = 1e-6
N_MOD = 9
D_FF = 2816
SHORT_CONV = 5

A_HEADS = 4
A_DK = 128
A_DV = 128
A_QKV = A_HEADS * (2 * A_DK + A_DV)

B_CH = 512
B_SHORT = 3
B_EMB = 33
B_FFN = 64
B_INNER_MLPS = 2
B_WINDOW_SHIFT = 0.05
B_DECAY_SHORT_PCT = 0.3
B_DECAY_LONG_PCT = 1.5
B_DECAY_TARGET = 1e-2

C_HEADS = 8
C_HEADDIM = 64
C_GROUPS = 2
C_STATE = 64
C_INNER = C_HEADS * C_HEADDIM
C_XBC = C_INNER + 2 * C_GROUPS * C_STATE

D_HEADS = 4
D_DK = 64
D_DV = 128
D_RANK = 16
D_GATE_NORM = 16.0

N_BRANCH = 4
BRANCH_W = 512

IN_SIZES = (A_QKV, A_HEADS * A_DV, 2 * A_HEADS, 2 * A_HEADS,
            3 * B_CH,
            C_INNER, C_XBC, 2 * C_HEADS,
            D_HEADS * (2 * D_DK + D_DV), D_HEADS * D_DV, 2 * D_RANK,
            N_BRANCH * D_MODEL)
N_IN = sum(IN_SIZES)

kernel_name = 'hybrid_quad_mixer_prefix_dit'


def rms_norm(x, w):
    xf = x.astype(jnp.float32)
    y = xf * lax.rsqrt(jnp.mean(xf * xf, axis=-1, keepdims=True) + NORM_EPS)
    return (y * w.astype(jnp.float32)).astype(x.dtype)


def l2_normalize(x):
    return x * lax.rsqrt(jnp.sum(x * x, axis=-1, keepdims=True) + NORM_EPS)


def modulate(h, shift, scale):
    return h * (1.0 + scale) + shift


def pre_norm(t, gain, m, i):
    return modulate(rms_norm(t, gain), m[..., 3 * i, :], m[..., 3 * i + 1, :])


def swiglu(h, w_up, w_down):
    a, g = jnp.split(h @ w_up, 2, axis=-1)
    return (jax.nn.silu(a) * g) @ w_down


def split_cols(z, sizes):
    idx, acc = [], 0
    for s in sizes[:-1]:
        acc += s
        idx.append(acc)
    return jnp.split(z, idx, axis=-1)


def dwconv_centred(x, w, b=None):
    k_w, l = w.shape[0], x.shape[1]
    left = (k_w - 1) // 2
    xp = jnp.pad(x, ((0, 0), (left, k_w - 1 - left), (0, 0)))
    y = xp[:, 0:l] * w[0]
    for i in range(1, k_w):
        y = y + xp[:, i:i + l] * w[i]
    return y if b is None else y + b


def lower_masks():
    i = jnp.arange(CHUNK)[:, None]
    j = jnp.arange(CHUNK)[None, :]
    return i >= j, i > j


def to_chunks(t):
    b, l, h = t.shape[:3]
    t = t.reshape((b, l // CHUNK, CHUNK, h) + t.shape[3:])
    return jnp.moveaxis(t, 3, 1)


def from_chunks(t):
    t = jnp.moveaxis(t, 1, 3)
    b, n, cl, h = t.shape[:4]
    return t.reshape((b, n * cl, h) + t.shape[4:])


def scan_chunks(step, s0, xs):
    xs = tuple(jnp.moveaxis(a, 2, 0) for a in xs)
    s_final, ys = lax.scan(step, s0, xs)
    return jnp.moveaxis(ys, 0, 2), s_final


def gated_delta_chunked(q, k, v, beta, g, s0):
    q, k, v, beta, g = map(to_chunks, (q, k, v, beta, g))
    incl, strict = lower_masks()
    gc = jnp.cumsum(g, axis=-1)
    diff = gc[..., :, None] - gc[..., None, :]
    dmask = jnp.where(incl, jnp.exp(jnp.where(incl, diff, 0.0)), 0.0)
    kb = k * beta[..., None]
    low = jnp.where(strict, jnp.einsum('bhncd,bhnsd->bhncs', kb, k) * dmask, 0.0)
    eye = jnp.eye(CHUNK, dtype=low.dtype)
    tinv = lax.linalg.triangular_solve(eye + low, jnp.broadcast_to(eye, low.shape),
                                       left_side=True, lower=True, unit_diagonal=True)
    u = tinv @ (v * beta[..., None])
    w = tinv @ (kb * jnp.exp(gc)[..., None])
    aqk = jnp.einsum('bhncd,bhnsd->bhncs', q, k) * dmask
    qg = q * jnp.exp(gc)[..., None]
    glast = gc[..., -1:]
    kg = k * jnp.exp(glast - gc)[..., None]
    dlast = jnp.exp(glast[..., 0])

    def step(s, inp):
        u_c, w_c, aqk_c, qg_c, kg_c, d_c = inp
        v_new = u_c - jnp.einsum('bhcd,bhde->bhce', w_c, s)
        o_c = jnp.einsum('bhcd,bhde->bhce', qg_c, s) + jnp.einsum('bhcs,bhse->bhce', aqk_c, v_new)
        s = d_c[..., None, None] * s + jnp.einsum('bhcd,bhce->bhde', kg_c, v_new)
        return s, o_c

    o, s_final = scan_chunks(step, s0, (u, w, aqk, qg, kg, dlast))
    return from_chunks(o), s_final


def gla_chunked(q, k, v, g, s0):
    q, k, v, g = map(to_chunks, (q, k, v, g))
    incl, _ = lower_masks()
    gc = jnp.cumsum(g, axis=3)
    glast = gc[..., -1:, :]
    kd = k * jnp.exp(glast - gc)
    qd = q * jnp.exp(gc)
    qr = q * jnp.exp(gc - glast)
    aqk = jnp.where(incl, jnp.einsum('bhncd,bhnsd->bhncs', qr, kd), 0.0)
    o_intra = aqk @ v
    dlast = jnp.exp(glast[..., 0, :])

    def step(s, inp):
        qd_c, kd_c, v_c, d_c = inp
        o_c = jnp.einsum('bhcd,bhde->bhce', qd_c, s)
        s = d_c[..., None] * s + jnp.einsum('bhcd,bhce->bhde', kd_c, v_c)
        return s, o_c

    o_inter, s_final = scan_chunks(step, s0, (qd, kd, v, dlast))
    return from_chunks(o_intra + o_inter), s_final


def ssd_chunked(x, a, bm, cm, s0):
    x, a, bm, cm = map(to_chunks, (x, a, bm, cm))
    incl, _ = lower_masks()
    ac = jnp.cumsum(a, axis=-1)
    diff = ac[..., :, None] - ac[..., None, :]
    seg = jnp.where(incl, jnp.exp(jnp.where(incl, diff, 0.0)), 0.0)
    scores = jnp.einsum('bhkcn,bhksn->bhkcs', cm, bm) * seg
    y_diag = jnp.einsum('bhkcs,bhksp->bhkcp', scores, x)
    alast = ac[..., -1:]
    cd = cm * jnp.exp(ac)[..., None]
    bd = bm * jnp.exp(alast - ac)[..., None]
    dl = jnp.exp(alast[..., 0])

    def step(s, inp):
        cd_c, bd_c, x_c, d_c = inp
        y_c = jnp.einsum('bhcn,bhpn->bhcp', cd_c, s)
        s = d_c[..., None, None] * s + jnp.einsum('bhcp,bhcn->bhpn', x_c, bd_c)
        return s, y_c

    y_off, s_final = scan_chunks(step, s0, (cd, bd, x, dl))
    return from_chunks(y_diag + y_off), s_final


def bidirectional_scan(scan_fn, ctx_dirs, lat_dirs, s0):
    out_c, out_x = None, None
    for d in range(2):
        args_c, args_x = ctx_dirs[d], lat_dirs[d]
        if d == 1:
            args_c = tuple(jnp.flip(t, axis=1) for t in args_c)
            args_x = tuple(jnp.flip(t, axis=1) for t in args_x)
        oc, s_ctx = scan_fn(*args_c, s0)
        ox, _ = scan_fn(*args_x, s_ctx)
        if d == 1:
            oc, ox = jnp.flip(oc, axis=1), jnp.flip(ox, axis=1)
        out_c = oc if out_c is None else out_c + oc
        out_x = ox if out_x is None else out_x + ox
    return out_c, out_x


def head_gate_norm(o, gate_raw, norm_w):
    b, l, h, dv = o.shape
    y = rms_norm(o, norm_w) * jax.nn.silu(gate_raw.astype(jnp.float32)).reshape(b, l, h, dv)
    return y.reshape(b, l, h * dv)


def gdn_prep(qkv_raw, beta_raw, decay_raw, conv_w, a_log, dt_bias):
    b, l, _ = qkv_raw.shape
    f32 = jnp.float32
    qkv = jax.nn.silu(dwconv_centred(qkv_raw, conv_w)).astype(f32)
    q, k, v = jnp.split(qkv, [A_HEADS * A_DK, 2 * A_HEADS * A_DK], axis=-1)
    q = l2_normalize(q.reshape(b, l, A_HEADS, A_DK)) * (A_DK ** -0.5)
    k = l2_normalize(k.reshape(b, l, A_HEADS, A_DK))
    v = v.reshape(b, l, A_HEADS, A_DV)
    beta = jax.nn.sigmoid(beta_raw.astype(f32)).reshape(b, l, 2, A_HEADS)
    g = -jnp.exp(a_log.astype(f32)) * jax.nn.softplus(
        decay_raw.astype(f32).reshape(b, l, 2, A_HEADS) + dt_bias.astype(f32))
    return [(q, k, v, beta[:, :, d], g[:, :, d]) for d in range(2)]


def gdn_branch(pc, px, conv_w, a_log, dt_bias, norm_w, need_ctx):
    dc = gdn_prep(pc[0], pc[2], pc[3], conv_w, a_log, dt_bias)
    dx = gdn_prep(px[0], px[2], px[3], conv_w, a_log, dt_bias)
    s0 = jnp.zeros((px[0].shape[0], A_HEADS, A_DK, A_DV), jnp.float32)
    oc, ox = bidirectional_scan(gated_delta_chunked, dc, dx, s0)
    yx = head_gate_norm(ox, px[1], norm_w)
    yc = head_gate_norm(oc, pc[1], norm_w) if need_ctx else None
    return yc, yx


def hyena_filters(l, w1, b1, w2, b2, w_out, freq):
    f32 = jnp.float32
    t = jnp.linspace(0.0, 1.0, l, dtype=f32)[:, None]
    bands = (B_EMB - 1) // 2
    ang = 2.0 * math.pi * jnp.arange(l, dtype=f32)[:, None] / l
    fr = jnp.linspace(1e-4, bands - 1, bands, dtype=f32)[None, :]
    z = jnp.concatenate([t, jnp.cos(fr * ang), -jnp.sin(fr * ang)], axis=-1)
    sfreq = freq.astype(f32)
    h = jnp.sin(sfreq * (z @ w1.astype(f32) + b1.astype(f32)))
    for i in range(B_INNER_MLPS):
        h = jnp.sin(sfreq * (h @ w2[i].astype(f32) + b2[i].astype(f32)))
    h = (h @ w_out.astype(f32)).reshape(l, 2, 2, B_CH)
    max_decay = math.log(B_DECAY_TARGET) / B_DECAY_SHORT_PCT
    min_decay = math.log(B_DECAY_TARGET) / B_DECAY_LONG_PCT
    deltas = jnp.abs(jnp.linspace(min_decay, max_decay, B_CH, dtype=f32))
    h = h * (jnp.exp(-t * deltas) + B_WINDOW_SHIFT)[:, None, None, :]
    hf, hb = h[:, :, 0], h[:, :, 1]
    g = jnp.concatenate([hf, jnp.zeros_like(hf[:1]), hb[:0:-1]], axis=0)
    return jnp.fft.rfft(jnp.moveaxis(g, 1, 0), axis=1)


def fft_conv(u, g_freq):
    l = u.shape[1]
    u_f = jnp.fft.rfft(u, n=2 * l, axis=1)
    return jnp.fft.irfft(u_f * g_freq[None], n=2 * l, axis=1)[:, :l]


def hyena_op(z, conv_w, conv_b, g_freq, bias):
    b, l, _, ch = z.shape
    u = dwconv_centred(z.reshape(b, l, 3 * ch), conv_w.reshape(B_SHORT, 3 * ch), conv_b.reshape(3 * ch))
    u = u.astype(jnp.float32).reshape(b, l, 3, ch)
    y = u[:, :, 0]
    for order in range(2):
        y = u[:, :, 1 + order] * (fft_conv(y, g_freq[order]) + y * bias[order].astype(jnp.float32))
    return y


def raster_to_columns(t, rows):
    b = t.shape[0]
    t = t.reshape((b, rows, GRID_W) + t.shape[2:])
    return jnp.swapaxes(t, 1, 2).reshape((b, rows * GRID_W) + t.shape[3:])


def columns_to_raster(t, rows):
    b = t.shape[0]
    t = t.reshape((b, GRID_W, rows) + t.shape[2:])
    return jnp.swapaxes(t, 1, 2).reshape((b, rows * GRID_W) + t.shape[3:])


def hyena_branch(zc, zx, conv_w, conv_b, w1, b1, w2, b2, w_out, freq, bias, need_ctx):
    b, l, _ = zx.shape
    rows = l // GRID_W
    half = B_CH // 2
    zx = zx.reshape(b, l, 3, B_CH)
    gx = hyena_filters(l, w1, b1, w2, b2, w_out, freq)
    y_row = hyena_op(zx[..., :half], conv_w[..., :half], conv_b[..., :half], gx[..., :half], bias[..., :half])
    y_col = columns_to_raster(
        hyena_op(raster_to_columns(zx[..., half:], rows), conv_w[..., half:], conv_b[..., half:],
                 gx[..., half:], bias[..., half:]), rows)
    yx = jnp.concatenate([y_row, y_col], axis=-1)
    yc = None
    if need_ctx:
        lc = zc.shape[1]
        gcf = hyena_filters(lc, w1, b1, w2, b2, w_out, freq)
        yc = hyena_op(zc.reshape(b, lc, 3, B_CH), conv_w, conv_b, gcf, bias)
    return yc, yx


def ssd_prep(xbc_raw, dt_raw, conv_w, conv_b, a_log, dt_bias):
    b, l, _ = xbc_raw.shape
    f32 = jnp.float32
    xbc = jax.nn.silu(dwconv_centred(xbc_raw, conv_w, conv_b)).astype(f32)
    xs, bm, cm = jnp.split(xbc, [C_INNER, C_INNER + C_GROUPS * C_STATE], axis=-1)
    xs = xs.reshape(b, l, C_HEADS, C_HEADDIM)
    rep = C_HEADS // C_GROUPS
    bm = jnp.repeat(bm.reshape(b, l, C_GROUPS, C_STATE), rep, axis=2)
    cm = jnp.repeat(cm.reshape(b, l, C_GROUPS, C_STATE), rep, axis=2)
    dt = jax.nn.softplus(dt_raw.astype(f32).reshape(b, l, 2, C_HEADS) + dt_bias.astype(f32))
    a = -jnp.exp(a_log.astype(f32)) * dt
    dirs = [(xs * dt[:, :, d, :, None], a[:, :, d], bm, cm) for d in range(2)]
    return xs, dirs


def ssd_branch(pc, px, conv_w, conv_b, a_log, dt_bias, d_skip, norm_w, need_ctx):
    f32 = jnp.float32
    xs_c, dc = ssd_prep(pc[1], pc[2], conv_w, conv_b, a_log, dt_bias)
    xs_x, dx = ssd_prep(px[1], px[2], conv_w, conv_b, a_log, dt_bias)
    s0 = jnp.zeros((px[0].shape[0], C_HEADS, C_HEADDIM, C_STATE), f32)
    oc, ox = bidirectional_scan(ssd_chunked, dc, dx, s0)

    def ssd_out(y, xs, z):
        b, l = y.shape[:2]
        y = (y + d_skip.astype(f32)[:, None] * xs).reshape(b, l, C_INNER) * jax.nn.silu(z.astype(f32))
        y = rms_norm(y.reshape(b, l, C_GROUPS, C_INNER // C_GROUPS),
                     norm_w.reshape(C_GROUPS, C_INNER // C_GROUPS))
        return y.reshape(b, l, C_INNER)

    yx = ssd_out(ox, xs_x, px[0])
    yc = ssd_out(oc, xs_c, pc[0]) if need_ctx else None
    return yc, yx


def gla_prep(qkv_raw, lr_raw, gk_w, gk_b):
    b, l, _ = qkv_raw.shape
    f32 = jnp.float32
    q, k, v = jnp.split(qkv_raw.astype(f32), [D_HEADS * D_DK, 2 * D_HEADS * D_DK], axis=-1)
    q = q.reshape(b, l, D_HEADS, D_DK) * (D_DK ** -0.5)
    k = k.reshape(b, l, D_HEADS, D_DK)
    v = v.reshape(b, l, D_HEADS, D_DV)
    lr = lr_raw.astype(f32).reshape(b, l, 2, D_RANK)
    gk = jnp.einsum('bldr,drk->bldk', lr, gk_w.astype(f32)) + gk_b.astype(f32)
    g = (jax.nn.log_sigmoid(gk) / D_GATE_NORM).reshape(b, l, 2, D_HEADS, D_DK)
    return [(q, k, v, g[:, :, d]) for d in range(2)]


def gla_branch(pc, px, gk_w, gk_b, norm_w, need_ctx):
    dc = gla_prep(pc[0], pc[2], gk_w, gk_b)
    dx = gla_prep(px[0], px[2], gk_w, gk_b)
    s0 = jnp.zeros((px[0].shape[0], D_HEADS, D_DK, D_DV), jnp.float32)
    oc, ox = bidirectional_scan(gla_chunked, dc, dx, s0)
    yx = head_gate_norm(ox, px[1], norm_w)
    yc = head_gate_norm(oc, pc[1], norm_w) if need_ctx else None
    return yc, yx


def merge_branches(ys, gate_raw, w_branch, w_out, dtype):
    b, l = gate_raw.shape[:2]
    yb = jnp.stack([y.astype(dtype) for y in ys], axis=2)
    proj = jnp.einsum('blkw,kwd->blkd', yb, w_branch)
    gates = jax.nn.sigmoid(gate_raw.astype(jnp.float32)).astype(dtype).reshape(b, l, N_BRANCH, D_MODEL)
    return jnp.sum(gates * proj, axis=2) @ w_out


def token_mix(hc, hx, w_in, gdn_conv, gdn_a_log, gdn_dt_bias, gdn_norm,
              hy_conv_w, hy_conv_b, hy_w1, hy_b1, hy_w2, hy_b2, hy_wout, hy_freq, hy_bias,
              ssd_conv_w, ssd_conv_b, ssd_a_log, ssd_dt_bias, ssd_d, ssd_norm,
              gla_gk_w, gla_gk_b, gla_norm, w_branch, w_out, need_ctx):
    dtype = hx.dtype
    pc = split_cols(hc @ w_in, IN_SIZES)
    px = split_cols(hx @ w_in, IN_SIZES)
    ya_c, ya_x = gdn_branch(pc[0:4], px[0:4], gdn_conv, gdn_a_log, gdn_dt_bias, gdn_norm, need_ctx)
    yb_c, yb_x = hyena_branch(pc[4], px[4], hy_conv_w, hy_conv_b, hy_w1, hy_b1, hy_w2, hy_b2,
                              hy_wout, hy_freq, hy_bias, need_ctx)
    yc_c, yc_x = ssd_branch(pc[5:8], px[5:8], ssd_conv_w, ssd_conv_b, ssd_a_log, ssd_dt_bias,
                            ssd_d, ssd_norm, need_ctx)
    yd_c, yd_x = gla_branch(pc[8:11], px[8:11], gla_gk_w, gla_gk_b, gla_norm, need_ctx)
    out_x = merge_branches([ya_x, yb_x, yc_x, yd_x], px[11], w_branch, w_out, dtype)
    out_c = merge_branches([ya_c, yb_c, yc_c, yd_c], pc[11], w_branch, w_out, dtype) if need_ctx else None
    return out_c, out_x


def setup_inputs(seed: int = 0) -> dict:
    key = jax.random.key(seed)
    keys = jax.random.split(key, 40)
    counter = iter(range(40))
    f32 = jnp.float32

    def nk():
        return keys[next(counter)]

    def normal(shape, scale):
        return jax.random.normal(nk(), shape, f32) * scale

    def gain(shape):
        return 1.0 + normal(shape, 0.1)

    def a_log_init(shape):
        return jnp.log(jax.random.uniform(nk(), shape, f32, 1.0, 16.0))

    def dt_bias_init(shape):
        dt = jnp.exp(jax.random.uniform(nk(), shape, f32, math.log(1e-3), math.log(1e-1)))
        return dt + jnp.log(-jnp.expm1(-dt))

    L = DEPTH
    return {
        'x': normal((BATCH, SEQ, D_MODEL), 1.0),
        'c': normal((BATCH, D_MODEL), 1.0),
        'ctx': normal((BATCH, CTX_LEN, D_MODEL), 1.0),
        'c_ctx': normal((D_MODEL,), 0.5),
        'w_ada': normal((L, D_MODEL, N_MOD * D_MODEL), 0.5 * D_MODEL ** -0.5),
        'b_ada': normal((L, N_MOD * D_MODEL), 0.02),
        'norm_w': gain((L, 3, D_MODEL)),
        'ffn_up': normal((L, 2, D_MODEL, 2 * D_FF), D_MODEL ** -0.5),
        'ffn_down': normal((L, 2, D_FF, D_MODEL), D_FF ** -0.5),
        'w_in': normal((L, D_MODEL, N_IN), D_MODEL ** -0.5),
        'gdn_conv': normal((L, SHORT_CONV, A_QKV), SHORT_CONV ** -0.5),
        'gdn_a_log': a_log_init((L, 2, A_HEADS)),
        'gdn_dt_bias': dt_bias_init((L, 2, A_HEADS)),
        'gdn_norm': gain((L, A_DV)),
        'hy_conv_w': normal((L, B_SHORT, 3, B_CH), B_SHORT ** -0.5),
        'hy_conv_b': normal((L, 3, B_CH), 0.02),
        'hy_w1': normal((L, B_EMB, B_FFN), B_EMB ** -0.5),
        'hy_b1': normal((L, B_FFN), 0.1),
        'hy_w2': normal((L, B_INNER_MLPS, B_FFN, B_FFN), B_FFN ** -0.5),
        'hy_b2': normal((L, B_INNER_MLPS, B_FFN), 0.1),
        'hy_wout': normal((L, B_FFN, 4 * B_CH), 0.05 * B_FFN ** -0.5),
        'hy_freq': gain((L, B_FFN)),
        'hy_bias': normal((L, 2, B_CH), 0.5),
        'ssd_conv_w': normal((L, SHORT_CONV, C_XBC), SHORT_CONV ** -0.5),
        'ssd_conv_b': normal((L, C_XBC), 0.02),
        'ssd_a_log': a_log_init((L, 2, C_HEADS)),
        'ssd_dt_bias': dt_bias_init((L, 2, C_HEADS)),
        'ssd_d': gain((L, C_HEADS)),
        'ssd_norm': gain((L, C_INNER)),
        'gla_gk_w': normal((L, 2, D_RANK, D_HEADS * D_DK), D_RANK ** -0.5),
        'gla_gk_b': normal((L, 2, D_HEADS * D_DK), 0.1),
        'gla_norm': gain((L, D_DV)),
        'w_branch': normal((L, N_BRANCH, BRANCH_W, D_MODEL), BRANCH_W ** -0.5),
        'w_out': normal((L, D_MODEL, D_MODEL), D_MODEL ** -0.5),
        'final_norm': gain((D_MODEL,)),
    }


def reference(x, c, ctx, c_ctx, w_ada, b_ada, norm_w, ffn_up, ffn_down, w_in,
              gdn_conv, gdn_a_log, gdn_dt_bias, gdn_norm,
              hy_conv_w, hy_conv_b, hy_w1, hy_b1, hy_w2, hy_b2, hy_wout, hy_freq, hy_bias,
              ssd_conv_w, ssd_conv_b, ssd_a_log, ssd_dt_bias, ssd_d, ssd_norm,
              gla_gk_w, gla_gk_b, gla_norm, w_branch, w_out, final_norm):
    b = x.shape[0]
    xc = ctx
    cond_x = jax.nn.silu(c)
    cond_c = jax.nn.silu(c_ctx)
    for layer in range(DEPTH):
        last = layer == DEPTH - 1
        mx = (cond_x @ w_ada[layer] + b_ada[layer]).reshape(b, 1, N_MOD, D_MODEL)
        mc = (cond_c @ w_ada[layer] + b_ada[layer]).reshape(N_MOD, D_MODEL)
        nw = norm_w[layer]
        x = x + 0.5 * mx[..., 2, :] * swiglu(pre_norm(x, nw[0], mx, 0), ffn_up[layer, 0], ffn_down[layer, 0])
        xc = xc + 0.5 * mc[..., 2, :] * swiglu(pre_norm(xc, nw[0], mc, 0), ffn_up[layer, 0], ffn_down[layer, 0])
        yc, yx = token_mix(pre_norm(xc, nw[1], mc, 1), pre_norm(x, nw[1], mx, 1), w_in[layer],
                           gdn_conv[layer], gdn_a_log[layer], gdn_dt_bias[layer], gdn_norm[layer],
                           hy_conv_w[layer], hy_conv_b[layer], hy_w1[layer], hy_b1[layer], hy_w2[layer],
                           hy_b2[layer], hy_wout[layer], hy_freq[layer], hy_bias[layer],
                           ssd_conv_w[layer], ssd_conv_b[layer], ssd_a_log[layer], ssd_dt_bias[layer],
                           ssd_d[layer], ssd_norm[layer],
                           gla_gk_w[layer], gla_gk_b[layer], gla_norm[layer],
                           w_branch[layer], w_out[layer], not last)
        x = x + mx[..., 5, :] * yx
        x = x + 0.5 * mx[..., 8, :] * swiglu(pre_norm(x, nw[2], mx, 2), ffn_up[layer, 1], ffn_down[layer, 1])
        if not last:
            xc = xc + mc[..., 5, :] * yc
            xc = xc + 0.5 * mc[..., 8, :] * swiglu(pre_norm(xc, nw[2], mc, 2), ffn_up[layer, 1], ffn_down[layer, 1])
    return rms_norm(x, final_norm)
```

```python
import numpy as np
from contextlib import ExitStack
import concourse.bass as bass
import concourse.mybir as mybir
from concourse.bass_utils import run_bass_kernel_spmd

F32 = mybir.dt.float32
BF16 = mybir.dt.bfloat16
ALU = mybir.AluOpType
AF = mybir.ActivationFunctionType

D = 1024
DEPTH = 4
SEQ = 2048
CTX = 256
NT = SEQ + CTX
DFF = 2816
NJ = DFF // 128
EPS = 1e-6
import os
CUT = int(os.environ.get('KCUT', '0'))
KCH = int(os.environ.get('KCH', '99'))
KSUB = int(os.environ.get('KSUB', '9'))
EVE = os.environ.get('EVE', None)
TT = [(0, 256), (256, 512), (768, 512), (1280, 512), (1792, 512)]
GROUPS = [[0, 1, 2], [3, 4]]


class Tl:
    __slots__ = ("w", "r", "name", "excl")

    def __init__(self, name=""):
        self.w = None
        self.r = {}
        self.name = name
        self.excl = False


class Prog:
    NSLOT = 12

    def __init__(self, nc, es):
        self.nc = nc
        self.eng = {"pe": nc.tensor, "act": nc.scalar, "dve": nc.vector, "pool": nc.gpsimd, "sp": nc.sync}
        self.banks = [{e: es.enter_context(nc.semaphore("s%d_%s" % (b, e))) for e in ("pe", "act", "dve", "pool")} for b in range(2)]
        self.bank = 0
        self.sem = self.banks[0]
        self.epoch_sem = es.enter_context(nc.semaphore("s_epoch"))
        self.epoch = 0
        self.cnt = {e: 0 for e in self.eng}
        self.known = {e: {} for e in self.eng}
        self.dsem = {}
        self.dtarget = {}
        self.drr = {}
        for q in ("sp", "pool"):
            self.dsem[q] = [es.enter_context(nc.semaphore("d_%s%d" % (q, i))) for i in range(self.NSLOT)]
            self.dtarget[q] = [0] * self.NSLOT
            self.drr[q] = 0
        self.tiles = {}
        self.nops = 0

    def T(self, *key):
        t = self.tiles.get(key)
        if t is None:
            t = Tl(str(key))
            self.tiles[key] = t
        return t

    def _wait(self, e, ref):
        if ref[0] == "dma":
            _, q, slot, target = ref
            k = ("dma", q, slot)
            if self.known[e].get(k, 0) >= target:
                return
            self.eng[e].wait_ge(self.dsem[q][slot], target)
            self.known[e][k] = target
        else:
            f, idx = ref
            if self.known[e].get(f, 0) >= idx:
                return
            self.eng[e].wait_ge(self.sem[f], idx)
            self.known[e][f] = idx

    def _deps(self, e, r, w):
        for t in r:
            if t.w is not None:
                ref = t.w
                if ref[0] == e and e == "pe":
                    continue
                self._wait(e, ref)
            if t.excl:
                for f, ref in t.r.items():
                    if ref[0] != e:
                        self._wait(e, ref)
        for t in w:
            if t.w is not None:
                ref = t.w
                if not (ref[0] == e and e == "pe"):
                    self._wait(e, ref)
            for f, ref in t.r.items():
                if not (ref[0] == e and e == "pe"):
                    self._wait(e, ref)

    def _commit(self, ref, r, w):
        for t in w:
            t.w = ref
            t.r = {}
        for t in r:
            key = ref[0] if ref[0] != "dma" else ref[:3]
            t.r[key] = ref

    def op(self, e, fn, r=(), w=()):
        self._deps(e, r, w)
        ins = fn(self.eng[e])
        self.cnt[e] += 1
        ins.then_inc(self.sem[e], 1)
        ref = (e, self.cnt[e])
        self._commit(ref, r, w)
        self.nops += 1
        return ref

    def dma(self, q, out, in_, r=(), w=(), **kw):
        self._deps(q, r, w)
        slot = self.drr[q]
        self.drr[q] = (slot + 1) % self.NSLOT
        if self.dtarget[q][slot] > 0:
            self._wait(q, ("dma", q, slot, self.dtarget[q][slot]))
        self.dtarget[q][slot] += 16
        self.eng[q].dma_start(out=out, in_=in_, **kw).then_inc(self.dsem[q][slot], 16)
        ref = ("dma", q, slot, self.dtarget[q][slot])
        self._commit(ref, r, w)
        self.nops += 1
        return ref

    def full_barrier(self):
        for e in self.eng:
            self.barrier_all(e)
        old = 1 - self.bank
        for e in ("pe", "act", "dve", "pool"):
            self.nc.vector.sem_clear(self.banks[old][e])
        self.nc.vector.sem_inc(self.epoch_sem, 1)
        self.epoch += 1
        for e in ("pe", "act", "pool", "sp"):
            self.eng[e].wait_ge(self.epoch_sem, self.epoch)
        self.bank = old
        self.sem = self.banks[old]
        for e in ("pe", "act", "dve", "pool"):
            self.cnt[e] = 0
        for e in self.known:
            self.known[e] = {k: v for k, v in self.known[e].items() if isinstance(k, tuple)}
        for t in self.tiles.values():
            t.w = None
            t.r = {}

    def barrier_all(self, e_final="sp"):
        for f in ("pe", "act", "dve", "pool"):
            if self.cnt[f] > 0 and f != e_final:
                self._wait(e_final, (f, self.cnt[f]))
        for q in ("sp", "pool"):
            for slot in range(self.NSLOT):
                if self.dtarget[q][slot] > 0:
                    self._wait(e_final, ("dma", q, slot, self.dtarget[q][slot]))


def build_nc(nlayers=DEPTH, stage="full"):
    nc = bass.Bass("TRN2", target_bir_lowering=False)
    dt_in = {}

    def din(name, shape, dt=F32):
        dt_in[name] = nc.dram_tensor(name, list(shape), dt, kind="ExternalInput").ap()
        return dt_in[name]

    xT = din("xT", [128, 8, NT])
    cond = din("cond", [128, 8, 2])
    w_ada = din("w_ada", [nlayers, 9, 128, 8, 1024])
    b_ada = din("b_ada", [nlayers, 128, 72])
    norm_w = din("norm_w", [nlayers, 128, 3, 8])
    ffn_up = din("ffn_up", [nlayers, 2, NJ, 128, 8, 256])
    ffn_dn = din("ffn_dn", [nlayers, 2, 8, 128, NJ, 128])
    fin_w = din("fin_w", [128, 8])
    consts = din("consts", [128, 9 * 128])
    w_ssd = din("w_ssd", [nlayers, 128, 8, 1296])
    lp_ssd = din("lp_ssd", [nlayers, 128, 588])
    w_gla = din("w_gla", [nlayers, 2, 128, 8, 768])
    w_glr = din("w_glr", [nlayers, 128, 8, 128])
    w_hy = din("w_hy", [nlayers, 2, 128, 8, 768])
    lp_hy = din("lp_hy", [nlayers, 128, 436])
    hy_wout = din("hy_wout", [nlayers, 128, 2048])
    hy_biasr = din("hy_biasr", [nlayers, 128, 1024])
    hy_zl = din("hy_zl", [128, 2048])
    hy_zc = din("hy_zc", [128, 256])
    hy_fc_l = din("hy_fc_l", [17, 128, 16, 128], BF16)
    hy_fs_l = din("hy_fs_l", [17, 128, 16, 128], BF16)
    hy_ic_l = din("hy_ic_l", [16, 128, 17, 128], BF16)
    hy_is_l = din("hy_is_l", [16, 128, 17, 128], BF16)
    hy_fc_c = din("hy_fc_c", [3, 128, 2, 128], BF16)
    hy_fs_c = din("hy_fs_c", [3, 128, 2, 128], BF16)
    hy_ic_c = din("hy_ic_c", [2, 128, 3, 128], BF16)
    hy_is_c = din("hy_is_c", [2, 128, 3, 128], BF16)
    hy_win_l = din("hy_win_l", [16, 128, 512])
    hy_win0_l = din("hy_win0_l", [128, 512])
    hy_win_c = din("hy_win_c", [2, 128, 512])
    hy_win0_c = din("hy_win0_c", [128, 512])
    w_gdn = din("w_gdn", [nlayers, 4, 128, 8, 512])
    w_gbd = din("w_gbd", [nlayers, 128, 8, 16])
    lp_gdn = din("lp_gdn", [nlayers, 128, 588])
    lp_gla = din("lp_gla", [nlayers, 128, 1536])
    h1s = nc.dram_tensor("h1s", [128, 8, NT], BF16, kind="Internal").ap()
    w_mg = din("w_mg", [nlayers, 4, 128, 8, 1024])
    w_br = din("w_br", [nlayers, 4, 128, 4, 1024])
    w_o = din("w_o", [nlayers, 128, 8, 1024])
    dbg = nc.dram_tensor("dbg", [128, 4, NT], BF16, kind="ExternalOutput").ap() if stage.startswith("dbg") else None
    outT = nc.dram_tensor("outT", [128, 8, SEQ], F32, kind="ExternalOutput").ap()

    _uc = [0]

    def sbt(name, shape, dt):
        _uc[0] += 1
        return nc.sbuf_tensor("%s_%d" % (name, _uc[0]), shape, dt)

    with ExitStack() as es:
        P = Prog(nc, es)

        def sb(name, shape, dt=F32):
            return es.enter_context(sbt(name, list(shape), dt))

        xs = sb("xs", [128, 8, NT])
        ones_bf = sb("ones_bf", [128, 128], BF16)
        condf = sb("condf", [128, 8, 2])
        condb = sb("condb", [128, 8, 2], BF16)
        mod = sb("mod", [128, 72, 2])
        bada = sb("bada", [128, 72])
        nw = sb("nw", [128, 3, 8])
        seff = sb("seff", [128, 3, 8, 2])
        gate = sb("gate", [128, 3, 8, 2])
        finw = sb("finw", [128, 8])
        sq = sb("sq", [128, 8, 512], BF16)
        lnv = sb("lnv", [128, 512])
        rstd = sb("rstd", [128, 512])
        tmpn = [sb("tmpn%d" % i, [128, 512]) for i in range(2)]
        epsb = sb("epsb", [128, 1])
        psum = [es.enter_context(nc.psum_tensor("ps%d" % i, [128, 512], F32)) for i in range(8)]
        pst = [P.T("ps", i) for i in range(8)]
        for t_ in pst:
            t_.excl = True
        prr = [0]

        def next_ps():
            i = prr[0]
            prr[0] = (i + 1) % 8
            return psum[i], pst[i]

        P.op("dve", lambda e: e.memset(ones_bf[:, :], 1.0), w=[P.T("ones_bf")])
        P.op("dve", lambda e: e.memset(epsb[:, :], EPS), w=[P.T("epsb")])
        for d in range(8):
            P.dma("sp", xs[:, d, :], xT[:, d, :], w=[P.T("xs", d, t) for t in range(5)])
        P.dma("sp", condf[:, :, :], cond[:, :, :], w=[P.T("condf")])
        P.dma("sp", finw[:, :], fin_w[:, :], w=[P.T("finw")])
        P.op("act", lambda e: e.activation(out=condb[:, :, :], in_=condf[:, :, :], func=AF.Silu),
             r=[P.T("condf")], w=[P.T("condb")])

        def ada_layer(l):
          with ExitStack() as ph:
            adar = [ph.enter_context(sbt("adar%d" % i, [128, 8, 1024], BF16)) for i in range(2)]
            ada_layer_(l, adar)
            P.full_barrier()

        def ada_layer_(l, adar):
            P.dma("sp", bada[:, :], b_ada[l], w=[P.T("bada")])
            P.dma("sp", nw[:, :, :], norm_w[l], w=[P.T("nw")])
            for j in range(9):
                slab = adar[j % 2]
                st = P.T("adar", j % 2)
                P.dma("pool", slab[:, :, :], w_ada[l, j], w=[st])
                ps, pt = next_ps()
                for dch in range(8):
                    for k in range(8):
                        P.op("pe", lambda e, dch=dch, k=k: e.matmul(
                            ps[:, dch * 2:dch * 2 + 2], slab[:, k, dch * 128:(dch + 1) * 128], condb[:, k, :],
                            start=(k == 0), stop=(k == 7)), r=[st, P.T("condb")], w=[pt])
                P.op("dve", lambda e, j=j: e.tensor_tensor(
                    out=mod[:, j * 8:(j + 1) * 8, :],
                    in0=ps[:, 0:16].rearrange("p (d c) -> p d c", c=2),
                    in1=bada[:, j * 8:(j + 1) * 8].unsqueeze(2).to_broadcast([128, 8, 2]),
                    op=ALU.add), r=[pt, P.T("bada")], w=[P.T("mod", j)])
            for i in range(3):
                P.op("dve", lambda e, i=i: e.scalar_tensor_tensor(
                    out=seff[:, i, :, :], in0=mod[:, (3 * i + 1) * 8:(3 * i + 2) * 8, :], scalar=1.0,
                    in1=nw[:, i, :].unsqueeze(2).to_broadcast([128, 8, 2]), op0=ALU.add, op1=ALU.mult),
                    r=[P.T("mod", 3 * i + 1), P.T("nw")], w=[P.T("seff", i)])
                gsc = 1.0 if i == 1 else 0.5
                P.op("dve", lambda e, i=i, gsc=gsc: e.tensor_scalar(
                    out=gate[:, i, :, :], in0=mod[:, (3 * i + 2) * 8:(3 * i + 3) * 8, :], scalar1=gsc,
                    scalar2=None, op0=ALU.mult), r=[P.T("mod", 3 * i + 2)], w=[P.T("gate", i)])

        def norm_tile(i, tt, dst, dst_off, dst_key, key_tt=True):
            t0, n = TT[tt]
            c = 1 if tt == 0 else 0
            for d in range(8):
                P.op("act", lambda e, d=d: e.activation(out=sq[:, d, :n], in_=xs[:, d, t0:t0 + n], func=AF.Square),
                     r=[P.T("xs", d, tt)], w=[P.T("sq", d)])
            ps, pt = next_ps()
            for d in range(8):
                P.op("pe", lambda e, d=d: e.matmul(ps[:, :n], ones_bf[:, :], sq[:, d, :n], start=(d == 0), stop=(d == 7)),
                     r=[P.T("ones_bf"), P.T("sq", d)], w=[pt])
            P.op("act", lambda e: e.activation(out=lnv[:, :n], in_=ps[:, :n], func=AF.Ln, bias=epsb[:, 0:1], scale=1.0 / D),
                 r=[pt, P.T("epsb")], w=[P.T("lnv")])
            P.op("act", lambda e: e.activation(out=rstd[:, :n], in_=lnv[:, :n], func=AF.Exp, scale=-0.5),
                 r=[P.T("lnv")], w=[P.T("rstd")])
            for d in range(8):
                tb = tmpn[d % 2]
                tk = P.T("tmpn", d % 2)
                P.op("dve", lambda e, d=d, tb=tb: e.tensor_tensor(out=tb[:, :n], in0=xs[:, d, t0:t0 + n], in1=rstd[:, :n], op=ALU.mult),
                     r=[P.T("xs", d, tt), P.T("rstd")], w=[tk])
                if i is None:
                    P.op("act", lambda e, d=d, tb=tb: e.activation(
                        out=dst[:, d, dst_off:dst_off + n], in_=tb[:, :n], func=AF.Identity, scale=finw[:, d:d + 1]),
                        r=[tk, P.T("finw")], w=[P.T(dst_key, d, tt) if key_tt else P.T(dst_key, d)])
                else:
                    P.op("act", lambda e, d=d, tb=tb: e.activation(
                        out=dst[:, d, dst_off:dst_off + n], in_=tb[:, :n], func=AF.Identity,
                        bias=mod[:, 3 * i * 8 + d, c:c + 1], scale=seff[:, i, d, c:c + 1]),
                        r=[tk, P.T("mod", 3 * i), P.T("seff", i)], w=[P.T(dst_key, d, tt) if key_tt else P.T(dst_key, d)])

        def ffn(l, s, i):
          with ExitStack() as ph:
            hb = ph.enter_context(sbt("hb", [128, 8, 1280], BF16))
            actb = ph.enter_context(sbt("actb", [128, NJ, 1280], BF16))
            upr = [ph.enter_context(sbt("upr%d" % i, [128, 8, 256], BF16)) for i in range(3)]
            dnr = [ph.enter_context(sbt("dnr%d" % i, [128, NJ, 128], BF16)) for i in range(2)]
            tmps = [ph.enter_context(sbt("tmps%d" % i, [128, 512], F32)) for i in range(2)]
            ffn_(l, s, i, hb, actb, upr, dnr, tmps)
            P.full_barrier()

        def ffn_(l, s, i, hb, actb, upr, dnr, tmps):
            for grp in GROUPS:
                offs = {}
                o = 0
                for tt in grp:
                    offs[tt] = o
                    o += TT[tt][1]
                for tt in grp:
                    norm_tile(i, tt, hb, offs[tt], "hb")
                for j in range(NJ):
                    slab = upr[j % 3]
                    st = P.T("upr", j % 3)
                    P.dma("pool", slab[:, :, :], ffn_up[l, s, j], w=[st])
                    for tt in grp:
                        n = TT[tt][1]
                        o = offs[tt]
                        pa, pat = next_ps()
                        for k in range(8):
                            P.op("pe", lambda e, k=k, pa=pa: e.matmul(pa[:, :n], slab[:, k, 0:128], hb[:, k, o:o + n],
                                                                  start=(k == 0), stop=(k == 7)),
                                 r=[st, P.T("hb", k, tt)], w=[pat])
                        pg, pgt = next_ps()
                        for k in range(8):
                            P.op("pe", lambda e, k=k, pg=pg: e.matmul(pg[:, :n], slab[:, k, 128:256], hb[:, k, o:o + n],
                                                                  start=(k == 0), stop=(k == 7)),
                                 r=[st, P.T("hb", k, tt)], w=[pgt])
                        tb = tmps[(j + tt) % 2]
                        tk = P.T("tmps", (j + tt) % 2)
                        P.op("act", lambda e, pa=pa, tb=tb: e.activation(out=tb[:, :n], in_=pa[:, :n], func=AF.Silu),
                             r=[pat], w=[tk])
                        P.op("dve", lambda e, pg=pg, tb=tb, j=j: e.tensor_tensor(out=actb[:, j, o:o + n], in0=pg[:, :n], in1=tb[:, :n], op=ALU.mult),
                             r=[pgt, tk], w=[P.T("actb", j, tt)])
                for d in range(8):
                    slab = dnr[d % 2]
                    st = P.T("dnr", d % 2)
                    P.dma("pool", slab[:, :, :], ffn_dn[l, s, d], w=[st])
                    for tt in grp:
                        t0, n = TT[tt]
                        o = offs[tt]
                        c = 1 if tt == 0 else 0
                        py, pyt = next_ps()
                        for j in range(NJ):
                            P.op("pe", lambda e, j=j, py=py: e.matmul(py[:, :n], slab[:, j, :], actb[:, j, o:o + n],
                                                                  start=(j == 0), stop=(j == NJ - 1)),
                                 r=[st, P.T("actb", j, tt)], w=[pyt])
                        P.op("dve", lambda e, py=py, d=d: e.scalar_tensor_tensor(
                            out=xs[:, d, t0:t0 + n], in0=py[:, :n], scalar=gate[:, i, d, c:c + 1], in1=xs[:, d, t0:t0 + n],
                            op0=ALU.mult, op1=ALU.add), r=[pyt, P.T("gate", i), P.T("xs", d, tt)], w=[P.T("xs", d, tt)])


        T = P.T
        consts_sb = sb("consts_sb", [128, 9 * 128])
        P.dma("sp", consts_sb[:, :], consts[:, :], w=[T("consts")])
        ident = consts_sb[:, 0:128]
        tri = [consts_sb[:, 128:256], consts_sb[:, 256:384]]
        negm = [consts_sb[:, 384:512], consts_sb[:, 512:640]]
        msk = [consts_sb[:, 640:768], consts_sb[:, 768:896]]
        smsk = [consts_sb[:, 896:1024], consts_sb[:, 1024:1152]]
        ones_f = sb("ones_f", [128, 128])
        one_col = sb("one_col", [128, 1])
        P.op("dve", lambda e: e.memset(ones_f[:, :], 1.0), w=[T("ones_f")])
        P.op("dve", lambda e: e.memset(one_col[:, :], 1.0), w=[T("one_col")])
        CK = [T("consts"), T("ones_f"), T("one_col"), T("epsb")]
        evt = [0]

        def evac(dst, src, r, w, func=None, eng=None):
            if func is not None or eng == "act" or (eng is None and evt[0] % 2 == 0 and eng != "dve"):
                P.op("act", lambda e: e.activation(out=dst, in_=src, func=(func or AF.Copy)), r=r, w=w)
            else:
                P.op("dve", lambda e: e.tensor_copy(out=dst, in_=src), r=r, w=w)
            evt[0] += 1

        def store_h1():
            with ExitStack() as p0:
                hst = p0.enter_context(sbt("hst", [128, 8, 512], BF16))
                for tt in range(5):
                    t0, n = TT[tt]
                    norm_tile(1, tt, hst, 0, "hst", key_tt=False)
                    P.dma("sp", h1s[:, :, t0:t0 + n], hst[:, :, :n], r=[T("hst", d) for d in range(8)], w=[T("h1s", tt)])
                P.full_barrier()

        def load_h1(tt, hb):
            t0, n = TT[tt]
            P.dma("sp", hb[:, :, :n], h1s[:, :, t0:t0 + n], r=[T("h1s", tt)], w=[T("hbm", k) for k in range(8)])

        def bc(ap, axis, shape):
            return ap.unsqueeze(axis).to_broadcast(list(shape))

        def conv_fm(src, dst, taps, bias, K, rk, wk, segs=((0, CTX), (CTX, NT))):
            left = (K - 1) // 2
            for (a, b) in segs:
                if bias is not None:
                    P.op("dve", lambda e: e.tensor_scalar(out=dst[:, a:b], in0=src[:, a:b], scalar1=taps[left], scalar2=bias,
                                                       op0=ALU.mult, op1=ALU.add), r=rk, w=wk)
                else:
                    P.op("dve", lambda e: e.tensor_scalar(out=dst[:, a:b], in0=src[:, a:b], scalar1=taps[left], scalar2=None,
                                                       op0=ALU.mult), r=rk, w=wk)
                for i in range(K):
                    if i == left:
                        continue
                    sft = i - left
                    lo = max(a, a - sft)
                    hi = min(b, b - sft)
                    P.op("dve", lambda e: e.scalar_tensor_tensor(out=dst[:, lo:hi], in0=src[:, lo + sft:hi + sft], scalar=taps[i],
                                                              in1=dst[:, lo:hi], op0=ALU.mult, op1=ALU.add), r=rk + wk, w=wk)

        def tr_to_tok(src_fm, rk, dstfn, wk, nchunks=18, c0=0):
            for g0 in range(0, nchunks, 4):
                nn = min(4, nchunks - g0)
                ps, pt = next_ps()
                for q in range(nn):
                    P.op("pe", lambda e: e.transpose(out=ps[:, q * 128:(q + 1) * 128],
                                                     in_=src_fm[:, (g0 + q) * 128:(g0 + q + 1) * 128], identity=ident),
                         r=rk + [T("consts")], w=[pt])
                evac(dstfn(c0 + g0, nn), ps[:, :nn * 128].rearrange("p (q c) -> p q c", c=128), r=[pt], w=wk)

        def proj_fm(wsrc, ncols, dst, dstkey, hb, slab, slabkey):
            P.dma("pool", slab[:, :, :ncols], wsrc, w=[slabkey])
            for tt in range(5):
                t0, n = TT[tt]
                load_h1(tt, hb)
                for cc in range(ncols // 128):
                    ps, pt = next_ps()
                    for k in range(8):
                        P.op("pe", lambda e: e.matmul(ps[:, :n], slab[:, k, cc * 128:(cc + 1) * 128], hb[:, k, :n],
                                                      start=(k == 0), stop=(k == 7)), r=[slabkey, T("hbm", k)], w=[pt])
                    evac(dst[:, cc, t0:t0 + n], ps[:, :n], r=[pt], w=[T(dstkey, cc)])

        def proj_tm(wsrc, ncols, hb, slab, slabkey, consume):
            P.dma("pool", slab[:, :, :ncols], wsrc, w=[slabkey])
            for tt in range(5):
                t0, n = TT[tt]
                load_h1(tt, hb)
                for sub in range(n // 128):
                    tc = t0 // 128 + sub
                    ps, pt = next_ps()
                    for k in range(8):
                        P.op("pe", lambda e: e.matmul(ps[:, :ncols], hb[:, k, sub * 128:(sub + 1) * 128], slab[:, k, :ncols],
                                                      start=(k == 0), stop=(k == 7)), r=[slabkey, T("hbm", k)], w=[pt])
                    consume(tc, ps, pt)

        def softplus_inplace(buf, key, tmp, tmpkey):
            P.op("act", lambda e: e.activation(out=tmp, in_=buf, func=AF.Exp), r=[key], w=[tmpkey])
            P.op("act", lambda e: e.activation(out=buf, in_=tmp, func=AF.Ln, bias=one_col[:, 0:1], scale=1.0),
                 r=[tmpkey, T("one_col")], w=[key])

        def finish(l, wg_src, oaccfn, ybr, pre_gate, ngrp, nrm, lpkey, ph, skipfn=None, W=512, ch0=0):
            al = lambda n_, s_, dt=F32: ph.enter_context(sbt(n_, list(s_), dt))
            hb = al("fin_hb", [128, 8, 512], BF16)
            wg = al("fin_wg", [128, 8, W], BF16)
            zs = tmpn[1]
            tq = al("fin_t", [128, W])
            sqv = lnv
            ssq = al("fin_ssq", [128, 8])
            gs = W // ngrp

            def consume(tc, ps, pt):
                evac(zs[:, :W], ps[:, :W], r=[pt], w=[T("tmpn", 1)], func=AF.Silu)
                src, rk = oaccfn(tc)
                if skipfn is not None:
                    skipfn(tc, tq)
                    src = tq[:, :]
                    rk = [T("fin_t")]
                if pre_gate:
                    P.op("dve", lambda e: e.tensor_tensor(out=tq[:, :], in0=src, in1=zs[:, :W], op=ALU.mult),
                         r=rk + [T("tmpn", 1)], w=[T("fin_t")])
                    src = tq[:, :]
                    rk = [T("fin_t")]
                P.op("dve", lambda e: e.tensor_tensor(out=sqv[:, :W], in0=src, in1=src, op=ALU.mult), r=rk, w=[T("lnv")])
                P.op("dve", lambda e: e.tensor_reduce(out=ssq[:, :ngrp], in_=sqv[:, :W].rearrange("p (g c) -> p g c", c=gs),
                                                   axis=mybir.AxisListType.X, op=ALU.add), r=[T("lnv")], w=[T("fin_ssq")])
                P.op("act", lambda e: e.activation(out=ssq[:, :ngrp], in_=ssq[:, :ngrp], func=AF.Ln, bias=epsb[:, 0:1], scale=1.0 / gs),
                     r=[T("fin_ssq"), T("epsb")], w=[T("fin_ssq")])
                P.op("act", lambda e: e.activation(out=ssq[:, :ngrp], in_=ssq[:, :ngrp], func=AF.Exp, scale=-0.5),
                     r=[T("fin_ssq")], w=[T("fin_ssq")])
                P.op("dve", lambda e: e.tensor_tensor(out=tq[:, :].rearrange("p (g c) -> p g c", c=gs),
                                                   in0=src.rearrange("p (g c) -> p g c", c=gs),
                                                   in1=bc(ssq[:, :ngrp], 2, [128, ngrp, gs]), op=ALU.mult),
                     r=rk + [T("fin_ssq")], w=[T("fin_t")])
                P.op("dve", lambda e: e.tensor_tensor(out=tq[:, :], in0=tq[:, :], in1=nrm, op=ALU.mult),
                     r=[T("fin_t"), lpkey], w=[T("fin_t")])
                if not pre_gate:
                    P.op("dve", lambda e: e.tensor_tensor(out=tq[:, :], in0=tq[:, :], in1=zs[:, :W], op=ALU.mult),
                         r=[T("fin_t"), T("tmpn", 1)], w=[T("fin_t")])
                ps2, pt2 = next_ps()
                nq = W // 128
                for cc in range(nq):
                    P.op("pe", lambda e: e.transpose(out=ps2[:, cc * 128:(cc + 1) * 128], in_=tq[:, cc * 128:(cc + 1) * 128],
                                                     identity=ident), r=[T("fin_t"), T("consts")], w=[pt2])
                evac(ybr[:, ch0:ch0 + nq, tc * 128:(tc + 1) * 128], ps2[:, :W].rearrange("p (q c) -> p q c", c=128), r=[pt2],
                     w=[T("ybr", tc)])

            proj_tm(wg_src, W, hb, wg, T("fin_wg"), consume)

        def decay_mats(a_c, d, H, wk, bufs, rk):
            X, gca, egl, ewl, Dm, E = bufs
            P.op("dve", lambda e: e.tensor_tensor(out=X[:, :H, :], in0=bc(tri[d], 1, [128, H, 128]), in1=bc(a_c, 2, [128, H, 128]),
                                               op=ALU.mult), r=rk + [T("consts")], w=[T(wk, "X")])
            pg, pgt = next_ps()
            P.op("pe", lambda e: e.matmul(pg[:, 0:H], tri[d], a_c, start=True, stop=True), r=rk + [T("consts")], w=[pgt])
            P.op("pe", lambda e: e.matmul(pg[:, H:2 * H], ones_f[:, :], a_c, start=True, stop=True), r=rk + [T("ones_f")], w=[pgt])
            evac(gca[:, :2 * H], pg[:, :2 * H], r=[pgt], w=[T(wk, "gca")], eng="act")
            P.op("act", lambda e: e.activation(out=egl[:, :2 * H], in_=gca[:, :2 * H], func=AF.Exp), r=[T(wk, "gca")], w=[T(wk, "egl")])
            P.op("dve", lambda e: e.tensor_tensor(out=ewl[:, :H], in0=gca[:, H:2 * H], in1=gca[:, 0:H], op=ALU.subtract),
                 r=[T(wk, "gca")], w=[T(wk, "ewl")])
            P.op("act", lambda e: e.activation(out=ewl[:, :H], in_=ewl[:, :H], func=AF.Exp), r=[T(wk, "ewl")], w=[T(wk, "ewl")])
            hb_ = min(4, H)
            for h0 in range(0, H, hb_):
                pr, prt = next_ps()
                P.op("pe", lambda e: e.matmul(pr[:, :hb_ * 128], ones_f[:, :], X[:, h0:h0 + hb_, :].rearrange("p h c -> p (h c)"),
                                              start=True, stop=True), r=[T(wk, "X"), T("ones_f")], w=[prt])
                P.op("dve", lambda e: e.tensor_tensor(out=Dm[:, h0:h0 + hb_, :], in0=pr[:, :hb_ * 128].rearrange("p (h c) -> p h c", c=128),
                                                   in1=bc(gca[:, h0:h0 + hb_], 2, [128, hb_, 128]), op=ALU.subtract),
                     r=[prt, T(wk, "gca")], w=[T(wk, "Dm", h0), T(wk, "E", h0)])
                P.op("dve", lambda e: e.scalar_tensor_tensor(out=Dm[:, h0:h0 + hb_, :], in0=Dm[:, h0:h0 + hb_, :], scalar=0.0,
                                                          in1=bc(negm[d], 1, [128, hb_, 128]), op0=ALU.min, op1=ALU.add),
                     r=[T(wk, "Dm", h0), T("consts")], w=[T(wk, "Dm", h0)])
                P.op("act", lambda e: e.activation(out=E[:, h0:h0 + hb_, :], in_=Dm[:, h0:h0 + hb_, :], func=AF.Exp),
                     r=[T(wk, "Dm", h0)], w=[T(wk, "E", h0), T(wk, "Dm", h0)])

        def chunk_order(d):
            return list(range(18)) if d == 0 else [1, 0] + list(range(17, 1, -1))

        def ssd_branch(l, ybr):
            with ExitStack() as ph:
                al = lambda n_, s_, dt=F32: ph.enter_context(sbt(n_, list(s_), dt))
                lp = al("lp_ssd", [128, 588])
                LK = T("lp_ssd")
                P.dma("sp", lp[:, :], lp_ssd[l], w=[LK])
                xtok = al("xtok", [128, 18, 512], BF16)
                btok = al("btok", [128, 18, 128], BF16)
                bcT = al("bcT", [128, 3, NT], BF16)
                dtt = al("dtt", [128, 18, 16])
                att = al("att", [128, 18, 16])
                with ExitStack() as p2:
                    al2 = lambda n_, s_, dt=F32: p2.enter_context(sbt(n_, list(s_), dt))
                    craw = al2("craw", [128, 2, NT])
                    cvo = al2("cvo", [128, NT])
                    slab = al2("pslab", [128, 8, 256], BF16)
                    hb = al2("hbm", [128, 8, 512], BF16)
                    for pss in range(3):
                        proj_fm(w_ssd[l, :, :, pss * 256:(pss + 1) * 256], 256, craw, "craw", hb, slab, T("pslab"))
                        for cc in range(2):
                            ch = 2 * pss + cc
                            conv_fm(craw[:, cc, :], cvo, [lp[:, ch * 5 + i:ch * 5 + i + 1] for i in range(5)], lp[:, 30 + ch:31 + ch], 5,
                                    [T("craw", cc), LK], [T("cvo")])
                            P.op("act", lambda e: e.activation(out=cvo[:, :], in_=cvo[:, :], func=AF.Silu), r=[T("cvo")], w=[T("cvo")])
                            if ch < 4:
                                tr_to_tok(cvo, [T("cvo")], lambda g0, nn: xtok[:, g0:g0 + nn, ch * 128:(ch + 1) * 128], [T("xtok", ch)])
                            elif ch == 4:
                                evac(bcT[:, 0, :], cvo[:, :], r=[T("cvo")], w=[T("bcT", 0)])
                                tr_to_tok(cvo, [T("cvo")], lambda g0, nn: btok[:, g0:g0 + nn, :], [T("btok")])
                            else:
                                P.op("dve", lambda e: e.memset(bcT[:, 1:3, :], 0.0), w=[T("bcT", 1)])
                                evac(bcT[0:64, 1, :], cvo[0:64, :], r=[T("cvo")], w=[T("bcT", 1)])
                                evac(bcT[64:128, 2, :], cvo[64:128, :], r=[T("cvo")], w=[T("bcT", 1)])
                    proj_tm(w_ssd[l, :, :, 1280:1296], 16, hb, slab, T("pslab"),
                            lambda tc, ps, pt: evac(dtt[:, tc, :], ps[:, :16], r=[pt], w=[T("dtt")]))
                    P.full_barrier()
                if CUT == 1:
                    return
                oacc = al("oacc", [128, 18, 512])
                with ExitStack() as p3:
                    al3 = lambda n_, s_, dt=F32: p3.enter_context(sbt(n_, list(s_), dt))
                    tmpd = al3("tmpd", [128, 18, 16])
                    negA = al3("negA", [128, 16])
                    P.op("dve", lambda e: e.tensor_tensor(out=dtt[:, :, :], in0=dtt[:, :, :], in1=bc(lp[:, 36:52], 1, [128, 18, 16]), op=ALU.add),
                         r=[T("dtt"), LK], w=[T("dtt")])
                    softplus_inplace(dtt[:, :, :], T("dtt"), tmpd[:, :, :], T("tmpd"))
                    P.op("act", lambda e: e.activation(out=negA[:, :], in_=lp[:, 52:68], func=AF.Exp), r=[LK], w=[T("negA")])
                    P.op("dve", lambda e: e.scalar_tensor_tensor(out=att[:, :, :], in0=dtt[:, :, :], scalar=-1.0, in1=bc(negA[:, :], 1, [128, 18, 16]),
                                                              op0=ALU.mult, op1=ALU.mult), r=[T("dtt"), T("negA")], w=[T("att")])
                    P.full_barrier()
                if CUT == 2:
                    return
                with ExitStack() as p4:
                    al4 = lambda n_, s_, dt=F32: p4.enter_context(sbt(n_, list(s_), dt))
                    X = al4("dX", [128, 8, 128])
                    gca = al4("gca", [128, 16])
                    egl = al4("egl", [128, 16])
                    ewl = al4("ewl", [128, 8])
                    Dm = al4("Dm", [128, 8, 128])
                    E = Dm
                    AT = al4("AT", [128, 8, 128], BF16)
                    xdt = al4("xdt", [128, 8, 64], BF16)
                    xw = al4("xw", [128, 8, 64], BF16)
                    t1 = al4("t1", [128, 8, 64])
                    S = al4("S", [128, 8, 64])
                    Sb = al4("Sb", [128, 512], BF16)
                    for d in range(2):
                        P.op("dve", lambda e: e.memset(S[:, :, :], 0.0), w=[T("S")])
                        P.op("dve", lambda e: e.memset(Sb[:, :], 0.0), w=[T("Sb")])
                        for tc in chunk_order(d)[:KCH]:
                            tok = slice(tc * 128, (tc + 1) * 128)
                            a_c = att[:, tc, d * 8:(d + 1) * 8]
                            dt_c = dtt[:, tc, d * 8:(d + 1) * 8]
                            decay_mats(a_c, d, 8, "ssd", (X, gca, egl, ewl, Dm, E), [T("att")])
                            if KSUB == 1:
                                continue
                            psc, psct = next_ps()
                            for g in range(2):
                                P.op("pe", lambda e: e.matmul(psc[:, g * 128:(g + 1) * 128], bcT[:, 0, tok],
                                                              bcT[:, 1 + g, tok], start=True, stop=True),
                                     r=[T("bcT", 0), T("bcT", 1)], w=[psct])
                            if KSUB == 11:
                                continue
                            for g in range(2):
                                P.op("dve", lambda e: e.tensor_tensor(out=AT[:, g * 4:(g + 1) * 4, :], in0=E[:, g * 4:(g + 1) * 4, :],
                                                                   in1=bc(psc[:, g * 128:(g + 1) * 128], 1, [128, 4, 128]), op=ALU.mult),
                                     r=[T("ssd", "E", g * 4), psct], w=[T("AT", g)])
                            if KSUB == 12:
                                continue
                            P.op("dve", lambda e: e.tensor_tensor(out=xdt[:, :, :], in0=xtok[:, tc, :].rearrange("p (h q) -> p h q", q=64),
                                                               in1=bc(dt_c, 2, [128, 8, 64]), op=ALU.mult),
                                 r=[T("xtok", i) for i in range(4)] + [T("dtt")], w=[T("xdt")])
                            P.op("dve", lambda e: e.tensor_tensor(out=xw[:, :, :], in0=xdt[:, :, :], in1=bc(ewl[:, :8], 2, [128, 8, 64]), op=ALU.mult),
                                 r=[T("xdt"), T("ssd", "ewl")], w=[T("xw")])
                            if KSUB == 2:
                                continue
                            po, pot = next_ps()
                            for h in range(8):
                                P.op("pe", lambda e: e.matmul(po[:, h * 64:(h + 1) * 64], AT[:, h, :], xdt[:, h, :], start=True, stop=True),
                                     r=[T("AT", h // 4), T("xdt")], w=[pot])
                            pf, pft = next_ps()
                            for g in range(2):
                                P.op("pe", lambda e: e.matmul(pf[:, g * 256:(g + 1) * 256], bcT[:, 1 + g, tok],
                                                              Sb[:, g * 256:(g + 1) * 256], start=True, stop=True),
                                     r=[T("bcT", 1), T("Sb")], w=[pft])
                            P.op("dve", lambda e: e.tensor_tensor(out=t1[:, :, :], in0=pf[:, :].rearrange("p (h q) -> p h q", q=64),
                                                               in1=bc(egl[:, 0:8], 2, [128, 8, 64]), op=ALU.mult),
                                 r=[pft, T("ssd", "egl")], w=[T("t1")])
                            if d == 0:
                                P.op("dve", lambda e: e.tensor_tensor(out=oacc[:, tc, :], in0=po[:, :], in1=t1[:, :, :].rearrange("p h q -> p (h q)"),
                                                                   op=ALU.add), r=[pot, T("t1")], w=[T("oacc", tc)])
                            else:
                                P.op("dve", lambda e: e.tensor_tensor(out=t1[:, :, :].rearrange("p h q -> p (h q)"), in0=po[:, :],
                                                                   in1=t1[:, :, :].rearrange("p h q -> p (h q)"), op=ALU.add),
                                     r=[pot, T("t1")], w=[T("t1")])
                                P.op("dve", lambda e: e.tensor_tensor(out=oacc[:, tc, :], in0=oacc[:, tc, :],
                                                                   in1=t1[:, :, :].rearrange("p h q -> p (h q)"), op=ALU.add),
                                     r=[T("oacc", tc), T("t1")], w=[T("oacc", tc)])
                            if KSUB == 3:
                                continue
                            pS, pSt = next_ps()
                            P.op("pe", lambda e: e.matmul(pS[:, :], btok[:, tc, :], xw[:, :, :].rearrange("p h q -> p (h q)"), start=True, stop=True),
                                 r=[T("btok"), T("xw")], w=[pSt])
                            P.op("dve", lambda e: e.tensor_tensor(out=S[:, :, :], in0=S[:, :, :], in1=bc(egl[:, 8:16], 2, [128, 8, 64]), op=ALU.mult),
                                 r=[T("S"), T("ssd", "egl")], w=[T("S")])
                            P.op("dve", lambda e: e.tensor_tensor(out=S[:, :, :], in0=S[:, :, :], in1=pS[:, :].rearrange("p (h q) -> p h q", q=64),
                                                               op=ALU.add), r=[T("S"), pSt], w=[T("S")])
                            evac(Sb[:, :], S[:, :, :].rearrange("p h q -> p (h q)"), r=[T("S")], w=[T("Sb")], eng="act")
                    P.full_barrier()
                if CUT == 3:
                    return
                with ExitStack() as p5:
                    def skipfn(tc, tq):
                        P.op("dve", lambda e: e.tensor_tensor(out=tq[:, :].rearrange("p (h q) -> p h q", q=64),
                                                           in0=xtok[:, tc, :].rearrange("p (h q) -> p h q", q=64),
                                                           in1=bc(lp[:, 68:76], 2, [128, 8, 64]), op=ALU.mult),
                             r=[T("xtok", i) for i in range(4)] + [LK], w=[T("fin_t")])
                        P.op("dve", lambda e: e.tensor_tensor(out=tq[:, :], in0=tq[:, :], in1=oacc[:, tc, :], op=ALU.add),
                             r=[T("fin_t"), T("oacc", tc)], w=[T("fin_t")])
                    finish(l, w_ssd[l, :, :, 768:1280], lambda tc: (oacc[:, tc, :], [T("oacc", tc)]), ybr, True, 2, lp[:, 76:588], LK, p5, skipfn)
                    P.full_barrier()


        def gla_branch(l, ybr):
            for pair in range(2):
                gla_pair(l, ybr, pair)

        def gla_pair(l, ybr, pair):
            with ExitStack() as ph:
                al = lambda n_, s_, dt=F32: ph.enter_context(sbt(n_, list(s_), dt))
                lp = al("lp_gla", [128, 1536])
                LK = T("lp_gla")
                P.dma("sp", lp[:, :], lp_gla[l], w=[LK])
                qk = al("g_qk", [128, 18, 256], BF16)
                vt = al("g_v", [128, 18, 256], BF16)
                lrT = al("g_lrT", [128, NT])
                wp = w_gla[l, pair]
                with ExitStack() as p2:
                    al2 = lambda n_, s_, dt=F32: p2.enter_context(sbt(n_, list(s_), dt))
                    slab = al2("pslab", [128, 8, 512], BF16)
                    hb = al2("hbm", [128, 8, 512], BF16)

                    def cons(tc, ps, pt):
                        evac(qk[:, tc, :], ps[:, 0:256], r=[pt], w=[T("g_qk", tc)], eng=EVE)
                        evac(vt[:, tc, :], ps[:, 256:512], r=[pt], w=[T("g_v", tc)], eng=EVE)
                    if KSUB == 21:
                        cons = lambda tc, ps, pt: None
                    proj_tm(wp[:, :, 0:512], 512, hb, slab, T("pslab"), cons)
                    if KSUB in (21, 22):
                        P.full_barrier()
                        return
                    P.dma("pool", slab[:, :, :128], w_glr[l], w=[T("pslab")])
                    for tt in range(5):
                        t0, n = TT[tt]
                        load_h1(tt, hb)
                        ps, pt = next_ps()
                        for k in range(8):
                            P.op("pe", lambda e: e.matmul(ps[:, :n], slab[:, k, 0:128], hb[:, k, :n], start=(k == 0), stop=(k == 7)),
                                 r=[T("pslab"), T("hbm", k)], w=[pt])
                        evac(lrT[:, t0:t0 + n], ps[:, :n], r=[pt], w=[T("g_lrT")])
                    P.full_barrier()
                if CUT == 1:
                    return
                oacc = al("oacc", [128, 18, 256])
                with ExitStack() as p4:
                    al4 = lambda n_, s_, dt=F32: p4.enter_context(sbt(n_, list(s_), dt))
                    sp_ = al4("g_sp", [128, 128])
                    glsb = al4("g_gl", [128, 128])
                    e1 = al4("g_e1", [128, 128])
                    e2 = al4("g_e2", [128, 128])
                    e3 = al4("g_e3", [128, 128])
                    qd = al4("g_qd", [128, 128])
                    qr = al4("g_qr", [128, 128])
                    kdf = al4("g_kdf", [128, 128])
                    kdb = al4("g_kdb", [128, 128], BF16)
                    kdTz = al4("g_kdTz", [128, 2, 128], BF16)
                    qdTz = al4("g_qdTz", [128, 2, 128], BF16)
                    qrT = al4("g_qrT", [128, 128], BF16)
                    AT = al4("g_AT", [128, 2, 128], BF16)
                    t1 = al4("g_t1", [128, 256])
                    dl = al4("g_dl", [128, 1])
                    S = al4("g_S", [128, 256])
                    Sb = al4("g_Sb", [128, 256], BF16)
                    SC = 64.0 ** -0.5
                    for d in range(2):
                        P.op("dve", lambda e: e.memset(S[:, :], 0.0), w=[T("g_S")])
                        P.op("dve", lambda e: e.memset(Sb[:, :], 0.0), w=[T("g_Sb")])
                        P.op("dve", lambda e: e.memset(kdTz[:, :, :], 0.0), w=[T("g_kdTz")])
                        P.op("dve", lambda e: e.memset(qdTz[:, :, :], 0.0), w=[T("g_qdTz")])
                        gkw = lp[:, 1024 + d * 256 + pair * 128:1024 + d * 256 + (pair + 1) * 128]
                        gkb = lp[:, d * 256 + pair * 128:d * 256 + (pair + 1) * 128]
                        for tc in chunk_order(d)[:KCH]:
                            tok = slice(tc * 128, (tc + 1) * 128)
                            pgk, pgkt = next_ps()
                            P.op("pe", lambda e: e.matmul(pgk[:, :128], lrT[:, tok], gkw, start=True, stop=True), r=[T("g_lrT"), LK], w=[pgkt])
                            P.op("dve", lambda e: e.tensor_tensor(out=sp_[:, :], in0=pgk[:, :128], in1=gkb, op=ALU.add), r=[pgkt, LK], w=[T("g_sp")])
                            P.op("act", lambda e: e.activation(out=sp_[:, :], in_=sp_[:, :], func=AF.Exp, scale=-1.0), r=[T("g_sp")], w=[T("g_sp")])
                            P.op("act", lambda e: e.activation(out=sp_[:, :], in_=sp_[:, :], func=AF.Ln, bias=one_col[:, 0:1], scale=1.0),
                                 r=[T("g_sp"), T("one_col")], w=[T("g_sp")])
                            pc, pct = next_ps()
                            P.op("pe", lambda e: e.matmul(pc[:, 0:128], tri[d], sp_[:, :], start=True, stop=True), r=[T("g_sp"), T("consts")], w=[pct])
                            P.op("pe", lambda e: e.matmul(pc[:, 128:256], ones_f[:, :], sp_[:, :], start=True, stop=True), r=[T("g_sp"), T("ones_f")], w=[pct])
                            P.op("pe", lambda e: e.matmul(pc[:, 256:257], sp_[:, :], ones_f[:, 0:1], start=True, stop=True),
                                 r=[T("g_sp"), T("ones_f")], w=[pct])
                            P.op("act", lambda e: e.activation(out=dl[:, :], in_=pc[:, 256:257], func=AF.Exp, scale=-1.0 / 16), r=[pct], w=[T("g_dl")])
                            P.op("act", lambda e: e.activation(out=e1[:, :], in_=pc[:, 0:128], func=AF.Exp, scale=-1.0 / 16), r=[pct], w=[T("g_e1")])
                            evac(glsb[:, :], pc[:, 128:256], r=[pct], w=[T("g_gl")], eng="act")
                            P.op("dve", lambda e: e.tensor_tensor(out=glsb[:, :], in0=pc[:, 0:128], in1=glsb[:, :], op=ALU.subtract),
                                 r=[pct, T("g_gl")], w=[T("g_gl")])
                            P.op("act", lambda e: e.activation(out=e2[:, :], in_=glsb[:, :], func=AF.Exp, scale=1.0 / 16), r=[T("g_gl")], w=[T("g_e2")])
                            P.op("act", lambda e: e.activation(out=e3[:, :], in_=glsb[:, :], func=AF.Exp, scale=-1.0 / 16), r=[T("g_gl")], w=[T("g_e3")])
                            if KSUB == 1:
                                continue
                            P.op("dve", lambda e: e.scalar_tensor_tensor(out=qd[:, :], in0=qk[:, tc, 0:128], scalar=SC, in1=e1[:, :], op0=ALU.mult, op1=ALU.mult),
                                 r=[T("g_qk", tc), T("g_e1")], w=[T("g_qd")])
                            P.op("dve", lambda e: e.scalar_tensor_tensor(out=qr[:, :], in0=qk[:, tc, 0:128], scalar=SC, in1=e3[:, :], op0=ALU.mult, op1=ALU.mult),
                                 r=[T("g_qk", tc), T("g_e3")], w=[T("g_qr")])
                            P.op("dve", lambda e: e.tensor_tensor(out=kdf[:, :], in0=qk[:, tc, 128:256], in1=e2[:, :], op=ALU.mult),
                                 r=[T("g_qk", tc), T("g_e2")], w=[T("g_kdf")])
                            evac(kdb[:, :], kdf[:, :], r=[T("g_kdf")], w=[T("g_kdb")], eng="act")
                            ptr, ptrt = next_ps()
                            P.op("pe", lambda e: e.transpose(out=ptr[:, 0:128], in_=kdf[:, :], identity=ident), r=[T("g_kdf"), T("consts")], w=[ptrt])
                            P.op("pe", lambda e: e.transpose(out=ptr[:, 128:256], in_=qd[:, :], identity=ident), r=[T("g_qd"), T("consts")], w=[ptrt])
                            P.op("pe", lambda e: e.transpose(out=ptr[:, 256:384], in_=qr[:, :], identity=ident), r=[T("g_qr"), T("consts")], w=[ptrt])
                            for hh in range(2):
                                rows = slice(hh * 64, (hh + 1) * 64)
                                evac(kdTz[rows, hh, :], ptr[rows, 0:128], r=[ptrt], w=[T("g_kdTz")])
                                evac(qdTz[rows, hh, :], ptr[rows, 128:256], r=[ptrt], w=[T("g_qdTz")])
                            evac(qrT[:, :], ptr[:, 256:384], r=[ptrt], w=[T("g_qrT")])
                            if KSUB == 2:
                                continue
                            psc, psct = next_ps()
                            for hh in range(2):
                                P.op("pe", lambda e: e.matmul(psc[:, hh * 128:(hh + 1) * 128], kdTz[:, hh, :], qrT[:, :], start=True, stop=True),
                                     r=[T("g_kdTz"), T("g_qrT")], w=[psct])
                            P.op("dve", lambda e: e.tensor_tensor(out=AT[:, :, :], in0=psc[:, 0:256].rearrange("p (h c) -> p h c", c=128),
                                                               in1=bc(msk[d], 1, [128, 2, 128]), op=ALU.mult), r=[psct, T("consts")], w=[T("g_AT")])
                            po, pot = next_ps()
                            for hh in range(2):
                                P.op("pe", lambda e: e.matmul(po[:, hh * 128:(hh + 1) * 128], AT[:, hh, :], vt[:, tc, hh * 128:(hh + 1) * 128], start=True, stop=True),
                                     r=[T("g_AT"), T("g_v", tc)], w=[pot])
                            pf, pft = next_ps()
                            for hh in range(2):
                                P.op("pe", lambda e: e.matmul(pf[:, hh * 128:(hh + 1) * 128], qdTz[:, hh, :], Sb[:, hh * 128:(hh + 1) * 128],
                                                              start=True, stop=True), r=[T("g_qdTz"), T("g_Sb")], w=[pft])
                            evac(t1[:, :], pf[:, 0:256], r=[pft], w=[T("g_t1")], eng="act")
                            if d == 0:
                                P.op("dve", lambda e: e.tensor_tensor(out=oacc[:, tc, :], in0=po[:, 0:256], in1=t1[:, :], op=ALU.add),
                                     r=[pot, T("g_t1")], w=[T("oacc", tc)])
                            else:
                                P.op("dve", lambda e: e.tensor_tensor(out=t1[:, :], in0=po[:, 0:256], in1=t1[:, :], op=ALU.add), r=[pot, T("g_t1")], w=[T("g_t1")])
                                P.op("dve", lambda e: e.tensor_tensor(out=oacc[:, tc, :], in0=oacc[:, tc, :], in1=t1[:, :], op=ALU.add),
                                     r=[T("oacc", tc), T("g_t1")], w=[T("oacc", tc)])
                            if KSUB == 3:
                                continue
                            pS, pSt = next_ps()
                            P.op("pe", lambda e: e.matmul(pS[:, :256], kdb[:, :], vt[:, tc, :], start=True, stop=True),
                                 r=[T("g_kdb"), T("g_v", tc)], w=[pSt])
                            P.op("dve", lambda e: e.scalar_tensor_tensor(out=S[:, :], in0=S[:, :], scalar=dl[:, 0:1], in1=pS[:, :256],
                                                                      op0=ALU.mult, op1=ALU.add), r=[T("g_S"), T("g_dl"), pSt], w=[T("g_S")])
                            evac(Sb[:, :], S[:, :], r=[T("g_S")], w=[T("g_Sb")], eng="act")
                    P.full_barrier()
                if CUT == 3:
                    return
                with ExitStack() as p5:
                    finish(l, wp[:, :, 512:768], lambda tc: (oacc[:, tc, :], [T("oacc", tc)]), ybr, False, 2, lp[:, 512:768], LK, p5,
                           W=256, ch0=pair * 2)
                    P.full_barrier()

        def gdn_branch(l, ybr):
            with ExitStack() as ph:
                al = lambda n_, s_, dt=F32: ph.enter_context(sbt(n_, list(s_), dt))
                lp = al("lp_gdn", [128, 588])
                LK = T("lp_gdn")
                P.dma("sp", lp[:, :], lp_gdn[l], w=[LK])
                beta = al("d_beta", [128, 18, 8])
                nbeta = al("d_nbeta", [128, 18, 8])
                gg = al("d_gg", [128, 18, 8])
                with ExitStack() as p2:
                    al2 = lambda n_, s_, dt=F32: p2.enter_context(sbt(n_, list(s_), dt))
                    slab = al2("pslab", [128, 8, 16], BF16)
                    hb = al2("hbm", [128, 8, 512], BF16)
                    tmpd = al2("tmpd", [128, 18, 8])
                    negA = al2("negA", [128, 8])

                    def cons(tc, ps, pt):
                        evac(beta[:, tc, :], ps[:, 0:8], r=[pt], w=[T("d_beta")], eng="act")
                        evac(gg[:, tc, :], ps[:, 8:16], r=[pt], w=[T("d_gg")], eng="act")
                    proj_tm(w_gbd[l], 16, hb, slab, T("pslab"), cons)
                    P.op("act", lambda e: e.activation(out=beta[:, :, :], in_=beta[:, :, :], func=AF.Sigmoid), r=[T("d_beta")], w=[T("d_beta")])
                    P.op("dve", lambda e: e.tensor_scalar(out=nbeta[:, :, :], in0=beta[:, :, :], scalar1=-1.0, scalar2=None, op0=ALU.mult),
                         r=[T("d_beta")], w=[T("d_nbeta")])
                    P.op("dve", lambda e: e.tensor_tensor(out=gg[:, :, :], in0=gg[:, :, :], in1=bc(lp[:, 68:76], 1, [128, 18, 8]), op=ALU.add),
                         r=[T("d_gg"), LK], w=[T("d_gg")])
                    softplus_inplace(gg[:, :, :], T("d_gg"), tmpd[:, :, :], T("tmpd"))
                    P.op("act", lambda e: e.activation(out=negA[:, :], in_=lp[:, 60:68], func=AF.Exp), r=[LK], w=[T("negA")])
                    P.op("dve", lambda e: e.scalar_tensor_tensor(out=gg[:, :, :], in0=gg[:, :, :], scalar=-1.0, in1=bc(negA[:, :], 1, [128, 18, 8]),
                                                              op0=ALU.mult, op1=ALU.mult), r=[T("d_gg"), T("negA")], w=[T("d_gg")])
                    P.full_barrier()
                for h in range(4):
                    gdn_head(l, ybr, h, lp, LK, beta, nbeta, gg)

        def gdn_head(l, ybr, h, lp, LK, beta, nbeta, gg):
            with ExitStack() as ph:
                al = lambda n_, s_, dt=F32: ph.enter_context(sbt(n_, list(s_), dt))
                QT = al("d_QT", [128, NT], BF16)
                KT = al("d_KT", [128, NT], BF16)
                Ktok = al("d_Ktok", [128, 18, 128], BF16)
                Vtok = al("d_Vtok", [128, 18, 128])
                wh = w_gdn[l, h]
                with ExitStack() as p2:
                    al2 = lambda n_, s_, dt=F32: p2.enter_context(sbt(n_, list(s_), dt))
                    craw = al2("craw", [128, 3, NT])
                    cvo = al2("cvo", [128, NT])
                    slab = al2("pslab", [128, 8, 384], BF16)
                    hb = al2("hbm", [128, 8, 512], BF16)
                    proj_fm(wh[:, :, 0:384], 384, craw, "craw", hb, slab, T("pslab"))
                    for cc in range(3):
                        ch = cc * 4 + h
                        conv_fm(craw[:, cc, :], cvo, [lp[:, ch * 5 + i:ch * 5 + i + 1] for i in range(5)], None, 5, [T("craw", cc), LK], [T("cvo")])
                        P.op("act", lambda e: e.activation(out=cvo[:, :], in_=cvo[:, :], func=AF.Silu), r=[T("cvo")], w=[T("cvo")])
                        if cc == 2:
                            tr_to_tok(cvo, [T("cvo")], lambda g0, nn: Vtok[:, g0:g0 + nn, :], [T("d_Vtok")])
                            continue
                        for tt in range(5):
                            t0, n = TT[tt]
                            P.op("dve", lambda e: e.tensor_tensor(out=tmpn[0][:, :n], in0=cvo[:, t0:t0 + n], in1=cvo[:, t0:t0 + n], op=ALU.mult),
                                 r=[T("cvo")], w=[T("tmpn", 0)])
                            ps, pt = next_ps()
                            P.op("pe", lambda e: e.matmul(ps[:, :n], ones_f[:, :], tmpn[0][:, :n], start=True, stop=True),
                                 r=[T("tmpn", 0), T("ones_f")], w=[pt])
                            P.op("act", lambda e: e.activation(out=rstd[:, :n], in_=ps[:, :n], func=AF.Ln, bias=epsb[:, 0:1], scale=1.0),
                                 r=[pt, T("epsb")], w=[T("rstd")])
                            P.op("act", lambda e: e.activation(out=rstd[:, :n], in_=rstd[:, :n], func=AF.Exp, scale=-0.5), r=[T("rstd")], w=[T("rstd")])
                            if cc == 0:
                                P.op("dve", lambda e: e.scalar_tensor_tensor(out=QT[:, t0:t0 + n], in0=cvo[:, t0:t0 + n], scalar=128.0 ** -0.5,
                                                                          in1=rstd[:, :n], op0=ALU.mult, op1=ALU.mult),
                                     r=[T("cvo"), T("rstd")], w=[T("d_QT")])
                            else:
                                P.op("dve", lambda e: e.tensor_tensor(out=cvo[:, t0:t0 + n], in0=cvo[:, t0:t0 + n], in1=rstd[:, :n], op=ALU.mult),
                                     r=[T("cvo"), T("rstd")], w=[T("cvo")])
                        if cc == 1:
                            evac(KT[:, :], cvo[:, :], r=[T("cvo")], w=[T("d_KT")])
                            tr_to_tok(cvo, [T("cvo")], lambda g0, nn: Ktok[:, g0:g0 + nn, :], [T("d_Ktok")])
                    P.full_barrier()
                oacc = al("oacc", [128, 18, 128])
                with ExitStack() as p4:
                    al4 = lambda n_, s_, dt=F32: p4.enter_context(sbt(n_, list(s_), dt))
                    X = al4("dX", [128, 1, 128])
                    gca = al4("gca", [128, 2])
                    egl = al4("egl", [128, 2])
                    ewl = al4("ewl", [128, 1])
                    neg = al4("d_neg", [128, 1])
                    Dm = al4("Dm", [128, 1, 128])
                    E = Dm
                    tmpA = al4("d_tmpA", [128, 128])
                    MT = al4("d_MT", [128, 128])
                    Ab = [al4("d_A%d" % i, [128, 128]) for i in range(2)]
                    ATb = [al4("d_AT%d" % i, [128, 128]) for i in range(2)]
                    TTm = al4("d_TT", [128, 128])
                    Aqk = al4("d_Aqk", [128, 128], BF16)
                    r0 = al4("d_r0", [128, 128])
                    vn = al4("d_vn", [128, 128], BF16)
                    kgt = al4("d_kgt", [128, 128], BF16)
                    t1 = al4("d_t1", [128, 128])
                    S = al4("d_S", [128, 128])
                    Sb = al4("d_Sb", [128, 128], BF16)
                    for d in range(2):
                        col = d * 4 + h
                        P.op("dve", lambda e: e.memset(S[:, :], 0.0), w=[T("d_S")])
                        P.op("dve", lambda e: e.memset(Sb[:, :], 0.0), w=[T("d_Sb")])
                        for tc in chunk_order(d)[:KCH]:
                            tok = slice(tc * 128, (tc + 1) * 128)
                            a_c = gg[:, tc, col:col + 1]
                            decay_mats(a_c, d, 1, "gdn", (X, gca, egl, ewl, Dm, E), [T("d_gg")])
                            pkk, pkkt = next_ps()
                            P.op("pe", lambda e: e.matmul(pkk[:, 0:128], KT[:, tok], KT[:, tok], start=True, stop=True), r=[T("d_KT")], w=[pkkt])
                            P.op("pe", lambda e: e.matmul(pkk[:, 128:256], KT[:, tok], QT[:, tok], start=True, stop=True), r=[T("d_KT"), T("d_QT")], w=[pkkt])
                            P.op("dve", lambda e: e.tensor_tensor(out=tmpA[:, :], in0=pkk[:, 0:128], in1=E[:, 0, :], op=ALU.mult),
                                 r=[pkkt, T("gdn", "E", 0)], w=[T("d_tmpA")])
                            P.op("dve", lambda e: e.scalar_tensor_tensor(out=MT[:, :], in0=tmpA[:, :], scalar=nbeta[:, tc, col:col + 1], in1=smsk[d],
                                                                      op0=ALU.mult, op1=ALU.mult), r=[T("d_tmpA"), T("d_nbeta"), T("consts")], w=[T("d_MT")])
                            P.op("dve", lambda e: e.tensor_tensor(out=Aqk[:, :], in0=pkk[:, 128:256], in1=E[:, 0, :], op=ALU.mult),
                                 r=[pkkt, T("gdn", "E", 0)], w=[T("d_Aqk")])
                            pA, pAt = next_ps()
                            P.op("pe", lambda e: e.transpose(out=pA[:, 0:128], in_=MT[:, :], identity=ident), r=[T("d_MT"), T("consts")], w=[pAt])
                            evac(Ab[0][:, :], pA[:, 0:128], r=[pAt], w=[T("d_A", 0)])
                            P.op("dve", lambda e: e.tensor_tensor(out=TTm[:, :], in0=MT[:, :], in1=ident, op=ALU.add), r=[T("d_MT"), T("consts")], w=[T("d_TT")])
                            Ap, ATp, Apk, ATpk = Ab[0], MT, T("d_A", 0), T("d_MT")
                            for j in range(1, 7):
                                An, ATn = Ab[j % 2], ATb[j % 2]
                                Ank, ATnk = T("d_A", j % 2), T("d_AT", j % 2)
                                p1, p1t = next_ps()
                                P.op("pe", lambda e: e.matmul(p1[:, 0:128], ATp[:, :], Ap[:, :], start=True, stop=True), r=[ATpk, Apk], w=[p1t])
                                if j < 6:
                                    P.op("pe", lambda e: e.matmul(p1[:, 128:256], Ap[:, :], ATp[:, :], start=True, stop=True), r=[ATpk, Apk], w=[p1t])
                                evac(An[:, :], p1[:, 0:128], r=[p1t], w=[Ank], eng="act")
                                if j < 6:
                                    evac(ATn[:, :], p1[:, 128:256], r=[p1t], w=[ATnk], eng="act")
                                p2_, p2t = next_ps()
                                P.op("pe", lambda e: e.matmul(p2_[:, 0:128], An[:, :], TTm[:, :], start=True, stop=True), r=[Ank, T("d_TT")], w=[p2t])
                                P.op("dve", lambda e: e.tensor_tensor(out=TTm[:, :], in0=TTm[:, :], in1=p2_[:, 0:128], op=ALU.add),
                                     r=[T("d_TT"), p2t], w=[T("d_TT")])
                                Ap, ATp, Apk, ATpk = An, ATn, Ank, ATnk
                            pks, pkst = next_ps()
                            P.op("pe", lambda e: e.matmul(pks[:, 0:128], KT[:, tok], Sb[:, :], start=True, stop=True), r=[T("d_KT"), T("d_Sb")], w=[pkst])
                            P.op("pe", lambda e: e.matmul(pks[:, 128:256], QT[:, tok], Sb[:, :], start=True, stop=True), r=[T("d_QT"), T("d_Sb")], w=[pkst])
                            P.op("dve", lambda e: e.tensor_scalar(out=neg[:, :], in0=egl[:, 0:1], scalar1=-1.0, scalar2=None, op0=ALU.mult),
                                 r=[T("gdn", "egl")], w=[T("d_neg")])
                            P.op("dve", lambda e: e.scalar_tensor_tensor(out=r0[:, :], in0=pks[:, 0:128], scalar=neg[:, 0:1], in1=Vtok[:, tc, :],
                                                                      op0=ALU.mult, op1=ALU.add), r=[pkst, T("d_neg"), T("d_Vtok")], w=[T("d_r0")])
                            pX, pXt = next_ps()
                            P.op("pe", lambda e: e.matmul(pX[:, 0:128], TTm[:, :], r0[:, :], start=True, stop=True), r=[T("d_TT"), T("d_r0")], w=[pXt])
                            P.op("dve", lambda e: e.tensor_scalar(out=vn[:, :], in0=pX[:, 0:128], scalar1=beta[:, tc, col:col + 1], scalar2=None, op0=ALU.mult),
                                 r=[pXt, T("d_beta")], w=[T("d_vn")])
                            po, pot = next_ps()
                            P.op("pe", lambda e: e.matmul(po[:, 0:128], Aqk[:, :], vn[:, :], start=True, stop=True), r=[T("d_Aqk"), T("d_vn")], w=[pot])
                            P.op("dve", lambda e: e.tensor_scalar(out=t1[:, :], in0=pks[:, 128:256], scalar1=egl[:, 0:1], scalar2=None, op0=ALU.mult),
                                 r=[pkst, T("gdn", "egl")], w=[T("d_t1")])
                            if d == 0:
                                P.op("dve", lambda e: e.tensor_tensor(out=oacc[:, tc, :], in0=po[:, 0:128], in1=t1[:, :], op=ALU.add),
                                     r=[pot, T("d_t1")], w=[T("oacc", tc)])
                            else:
                                P.op("dve", lambda e: e.tensor_tensor(out=t1[:, :], in0=po[:, 0:128], in1=t1[:, :], op=ALU.add), r=[pot, T("d_t1")], w=[T("d_t1")])
                                P.op("dve", lambda e: e.tensor_tensor(out=oacc[:, tc, :], in0=oacc[:, tc, :], in1=t1[:, :], op=ALU.add),
                                     r=[T("oacc", tc), T("d_t1")], w=[T("oacc", tc)])
                            P.op("dve", lambda e: e.tensor_scalar(out=kgt[:, :], in0=Ktok[:, tc, :], scalar1=ewl[:, 0:1], scalar2=None, op0=ALU.mult),
                                 r=[T("d_Ktok"), T("gdn", "ewl")], w=[T("d_kgt")])
                            pS, pSt = next_ps()
                            P.op("pe", lambda e: e.matmul(pS[:, 0:128], kgt[:, :], vn[:, :], start=True, stop=True), r=[T("d_kgt"), T("d_vn")], w=[pSt])
                            P.op("dve", lambda e: e.scalar_tensor_tensor(out=S[:, :], in0=S[:, :], scalar=egl[:, 1:2], in1=pS[:, 0:128],
                                                                      op0=ALU.mult, op1=ALU.add), r=[T("d_S"), T("gdn", "egl"), pSt], w=[T("d_S")])
                            evac(Sb[:, :], S[:, :], r=[T("d_S")], w=[T("d_Sb")], eng="act")
                    P.full_barrier()
                if CUT == 3:
                    return
                with ExitStack() as p5:
                    finish(l, wh[:, :, 384:512], lambda tc: (oacc[:, tc, :], [T("oacc", tc)]), ybr, False, 1, lp[:, 76:204], LK, p5,
                           W=128, ch0=h)
                    P.full_barrier()

        TWO_PI = 2.0 * np.pi

        def hyena_branch(l, ybr):
            with ExitStack() as ph:
                al = lambda n_, s_, dt=F32: ph.enter_context(sbt(n_, list(s_), dt))
                lp = al("lp_hy", [128, 436])
                LK = T("lp_hy")
                P.dma("sp", lp[:, :], lp_hy[l], w=[LK])
                h3 = {2048: al("h3l", [128, 2048], BF16), 256: al("h3c", [128, 256], BF16)}
                negpi = al("negpi", [128, 1])
                P.op("dve", lambda e: e.memset(negpi[:, :], -float(np.pi)), w=[T("negpi")])
                with ExitStack() as p1:
                    al1 = lambda n_, s_, dt=F32: p1.enter_context(sbt(n_, list(s_), dt))
                    zT = al1("hy_zT", [128, 2048])
                    hA = al1("hy_hA", [128, 2048])
                    hB = al1("hy_hB", [128, 2048])
                    zr = al1("hy_zr", [128, 512])
                    for L, zsrc in ((2048, hy_zl), (256, hy_zc)):
                        P.dma("sp", zT[:, :L], zsrc[:, :], w=[T("hy_zT")])
                        cur, curk = zT, T("hy_zT")
                        for li in range(3):
                            dst, dstk = (hA, T("hy_hA")) if li % 2 == 0 else (hB, T("hy_hB"))
                            wap = lp[:, li * 128:(li + 1) * 128]
                            bcol = lp[:, 384 + li:385 + li]
                            fcol = lp[:, 387:388]
                            for c0 in range(0, L, 512):
                                n = min(512, L - c0)
                                ps, pt = next_ps()
                                P.op("pe", lambda e: e.matmul(ps[:, :n], wap, cur[:, c0:c0 + n], start=True, stop=True), r=[LK, curk], w=[pt])
                                P.op("dve", lambda e: e.tensor_scalar(out=dst[:, c0:c0 + n], in0=ps[:, :n], scalar1=bcol, scalar2=fcol,
                                                                   op0=ALU.add, op1=ALU.mult), r=[pt, LK], w=[dstk])
                                MAGIC = 12582912.0
                                P.op("dve", lambda e: e.tensor_scalar(out=zr[:, :n], in0=dst[:, c0:c0 + n], scalar1=float(1.0 / TWO_PI), scalar2=MAGIC,
                                                                   op0=ALU.mult, op1=ALU.add), r=[dstk], w=[T("hy_zr")])
                                P.op("dve", lambda e: e.tensor_scalar(out=zr[:, :n], in0=zr[:, :n], scalar1=-MAGIC, scalar2=None, op0=ALU.add),
                                     r=[T("hy_zr")], w=[T("hy_zr")])
                                P.op("dve", lambda e: e.scalar_tensor_tensor(out=dst[:, c0:c0 + n], in0=zr[:, :n], scalar=float(-TWO_PI), in1=dst[:, c0:c0 + n],
                                                                          op0=ALU.mult, op1=ALU.add), r=[T("hy_zr"), dstk], w=[dstk])
                                P.op("dve", lambda e: e.tensor_scalar(out=dst[:, c0:c0 + n], in0=dst[:, c0:c0 + n], scalar1=3.1415925, scalar2=-3.1415925,
                                                                   op0=ALU.min, op1=ALU.max), r=[dstk], w=[dstk])
                                if li < 2:
                                    P.op("act", lambda e: e.activation(out=dst[:, c0:c0 + n], in_=dst[:, c0:c0 + n], func=AF.Sin),
                                         r=[dstk], w=[dstk])
                                else:
                                    P.op("act", lambda e: e.activation(out=h3[L][:, c0:c0 + n], in_=dst[:, c0:c0 + n], func=AF.Sin),
                                         r=[dstk], w=[T("h3", L)])
                            cur, curk = dst, dstk
                    P.full_barrier()
                for half in range(2):
                    hyena_half(l, ybr, half, lp, LK, h3)

        def hyena_half(l, ybr, half, lp, LK, h3):
            with ExitStack() as ph:
                al = lambda n_, s_, dt=F32: ph.enter_context(sbt(n_, list(s_), dt))
                yb = al("hy_y", [128, 18, 256], BF16)
                uu = [None, al("hy_u1", [128, 18, 256], BF16), al("hy_u2", [128, 18, 256], BF16)]
                ukeys = [T("hy_y"), T("hy_u1"), T("hy_u2")]
                ubufs = [yb, uu[1], uu[2]]
                with ExitStack() as p2:
                    al2 = lambda n_, s_, dt=F32: p2.enter_context(sbt(n_, list(s_), dt))
                    craw = al2("craw", [128, 3, NT])
                    cvo = al2("cvo", [128, NT])
                    slab = al2("pslab", [128, 8, 384], BF16)
                    hb = al2("hbm", [128, 8, 512], BF16)
                    for pss in range(2):
                        P.dma("pool", slab[:, :, :], w_hy[l, half, :, :, pss * 384:(pss + 1) * 384], w=[T("pslab")])
                        for tt in range(5):
                            t0, n = TT[tt]
                            load_h1(tt, hb)
                            for c3 in range(3):
                                ps, pt = next_ps()
                                for k in range(8):
                                    P.op("pe", lambda e: e.matmul(ps[:, :n], slab[:, k, c3 * 128:(c3 + 1) * 128], hb[:, k, :n],
                                                                  start=(k == 0), stop=(k == 7)), r=[T("pslab"), T("hbm", k)], w=[pt])
                                if half == 1 and tt > 0:
                                    r0_ = (t0 - CTX) // 64
                                    dstv = craw[:, c3, CTX:NT].rearrange("p (c r) -> p r c", r=32)[:, r0_:r0_ + 8, :]
                                    evac(dstv, ps[:, :512].rearrange("p (r c) -> p r c", c=64), r=[pt], w=[T("craw", c3)])
                                else:
                                    evac(craw[:, c3, t0:t0 + n], ps[:, :n], r=[pt], w=[T("craw", c3)])
                        for c3 in range(3):
                            cc = pss * 3 + c3
                            grp, sub = cc // 2, cc % 2
                            ch = grp * 4 + half * 2 + sub
                            conv_fm(craw[:, c3, :], cvo, [lp[:, 388 + ch * 3 + i:388 + ch * 3 + i + 1] for i in range(3)], lp[:, 424 + ch:425 + ch], 3,
                                    [T("craw", c3), LK], [T("cvo")])
                            tr_to_tok(cvo, [T("cvo")], lambda g0, nn: ubufs[grp][:, g0:g0 + nn, sub * 128:(sub + 1) * 128], [ukeys[grp]])
                    P.full_barrier()
                for (L, c0, nN, nK, fc, fs, ic, isn, wn, wn0) in ((2048, 2, 16, 17, hy_fc_l, hy_fs_l, hy_ic_l, hy_is_l, hy_win_l, hy_win0_l),
                                                                (256, 0, 2, 3, hy_fc_c, hy_fs_c, hy_ic_c, hy_is_c, hy_win_c, hy_win0_c)):
                    with ExitStack() as p3:
                        al3 = lambda n_, s_, dt=F32: p3.enter_context(sbt(n_, list(s_), dt))
                        Fp = al3("hy_Fp", [128, nN, 256], BF16)
                        Fm = al3("hy_Fm", [128, nN, 256], BF16)
                        Zc = al3("hy_Zc", [128, nK, 256], BF16)
                        Zs = al3("hy_Zs", [128, nK, 256], BF16)
                        wo = al3("hy_wo", [128, 512], BF16)
                        bia = al3("hy_bias", [128, 256])
                        win = al3("hy_win", [128, 256])
                        winb = al3("hy_winb", [128, 256])
                        cs = al3("hy_cs", [128, nK, 128], BF16)
                        sn = al3("hy_sn", [128, nK, 128], BF16)
                        gsb = al3("hy_gsb", [128, 512])
                        ta = al3("hy_ta", [128, 256])
                        tb = al3("hy_tb", [128, 256])
                        yn = al3("hy_yn", [128, 256])
                        for order in range(2):
                            P.dma("pool", wo[:, 0:256], hy_wout[l, :, order * 1024 + half * 256:order * 1024 + half * 256 + 256], w=[T("hy_wo")])
                            P.dma("pool", wo[:, 256:512], hy_wout[l, :, order * 1024 + 512 + half * 256:order * 1024 + 512 + half * 256 + 256], w=[T("hy_wo")])
                            P.dma("sp", bia[:, :], hy_biasr[l, :, order * 512 + half * 256:order * 512 + half * 256 + 256], w=[T("hy_bias")])
                            for n_ in range(nN):
                                P.dma("sp", win[:, :], wn[n_, :, half * 256:(half + 1) * 256], w=[T("hy_win")])
                                if n_ == 0:
                                    P.dma("sp", winb[:, :], wn0[:, half * 256:(half + 1) * 256], w=[T("hy_winb")])
                                ps, pt = next_ps()
                                P.op("pe", lambda e: e.matmul(ps[:, :], h3[L][:, n_ * 128:(n_ + 1) * 128], wo[:, :], start=True, stop=True),
                                     r=[T("h3", L), T("hy_wo")], w=[pt])
                                P.op("dve", lambda e: e.tensor_tensor(out=ta[:, :], in0=ps[:, 0:256], in1=win[:, :], op=ALU.mult), r=[pt, T("hy_win")], w=[T("hy_ta")])
                                if n_ == 0:
                                    P.op("dve", lambda e: e.tensor_tensor(out=tb[:, :], in0=ps[:, 256:512], in1=winb[:, :], op=ALU.mult),
                                         r=[pt, T("hy_winb")], w=[T("hy_tb")])
                                else:
                                    P.op("dve", lambda e: e.tensor_tensor(out=tb[:, :], in0=ps[:, 256:512], in1=win[:, :], op=ALU.mult),
                                         r=[pt, T("hy_win")], w=[T("hy_tb")])
                                P.op("dve", lambda e: e.tensor_tensor(out=Fp[:, n_, :], in0=ta[:, :], in1=tb[:, :], op=ALU.add), r=[T("hy_ta"), T("hy_tb")], w=[T("hy_Fp")])
                                P.op("dve", lambda e: e.tensor_tensor(out=Fm[:, n_, :], in0=ta[:, :], in1=tb[:, :], op=ALU.subtract), r=[T("hy_ta"), T("hy_tb")], w=[T("hy_Fm")])
                            for kc in range(nK):
                                P.dma("sp", cs[:, :nN, :], fc[kc], w=[T("hy_cs")])
                                P.dma("sp", sn[:, :nN, :], fs[kc], w=[T("hy_sn")])
                                py, pyt = next_ps()
                                pg, pgt = next_ps()
                                for (pp, ppt, c_, mat, mk, rhs_, rk_) in ((py, pyt, 0, cs, "hy_cs", yb, T("hy_y")), (py, pyt, 256, sn, "hy_sn", yb, T("hy_y")),
                                                                         (pg, pgt, 0, cs, "hy_cs", Fp, T("hy_Fp")), (pg, pgt, 256, sn, "hy_sn", Fm, T("hy_Fm"))):
                                    for n_ in range(nN):
                                        rr = rhs_[:, c0 + n_, :] if rhs_ is yb else rhs_[:, n_, :]
                                        P.op("pe", lambda e: e.matmul(pp[:, c_:c_ + 256], mat[:, n_, :], rr, start=(n_ == 0), stop=(n_ == nN - 1)),
                                             r=[T(mk), rk_], w=[ppt])
                                evac(gsb[:, :], pg[:, :], r=[pgt], w=[T("hy_gsb")], eng="act")
                                P.op("dve", lambda e: e.tensor_tensor(out=ta[:, :], in0=py[:, 0:256], in1=gsb[:, 0:256], op=ALU.mult), r=[pyt, T("hy_gsb")], w=[T("hy_ta")])
                                P.op("dve", lambda e: e.tensor_tensor(out=tb[:, :], in0=py[:, 256:512], in1=gsb[:, 256:512], op=ALU.mult), r=[pyt, T("hy_gsb")], w=[T("hy_tb")])
                                P.op("dve", lambda e: e.tensor_tensor(out=Zc[:, kc, :], in0=ta[:, :], in1=tb[:, :], op=ALU.subtract), r=[T("hy_ta"), T("hy_tb")], w=[T("hy_Zc")])
                                P.op("dve", lambda e: e.tensor_tensor(out=ta[:, :], in0=py[:, 0:256], in1=gsb[:, 256:512], op=ALU.mult), r=[pyt, T("hy_gsb")], w=[T("hy_ta")])
                                P.op("dve", lambda e: e.tensor_tensor(out=tb[:, :], in0=py[:, 256:512], in1=gsb[:, 0:256], op=ALU.mult), r=[pyt, T("hy_gsb")], w=[T("hy_tb")])
                                P.op("dve", lambda e: e.tensor_tensor(out=Zs[:, kc, :], in0=ta[:, :], in1=tb[:, :], op=ALU.add), r=[T("hy_ta"), T("hy_tb")], w=[T("hy_Zs")])
                            for tn in range(nN):
                                P.dma("sp", cs[:, :, :], ic[tn], w=[T("hy_cs")])
                                P.dma("sp", sn[:, :, :], isn[tn], w=[T("hy_sn")])
                                pv, pvt = next_ps()
                                for kc in range(nK):
                                    P.op("pe", lambda e: e.matmul(pv[:, 0:256], cs[:, kc, :], Zc[:, kc, :], start=(kc == 0), stop=False), r=[T("hy_cs"), T("hy_Zc")], w=[pvt])
                                for kc in range(nK):
                                    P.op("pe", lambda e: e.matmul(pv[:, 0:256], sn[:, kc, :], Zs[:, kc, :], start=False, stop=(kc == nK - 1)), r=[T("hy_sn"), T("hy_Zs")], w=[pvt])
                                P.op("dve", lambda e: e.tensor_tensor(out=ta[:, :], in0=yb[:, c0 + tn, :], in1=bia[:, :], op=ALU.mult), r=[T("hy_y"), T("hy_bias")], w=[T("hy_ta")])
                                P.op("dve", lambda e: e.tensor_tensor(out=ta[:, :], in0=ta[:, :], in1=pv[:, 0:256], op=ALU.add), r=[T("hy_ta"), pvt], w=[T("hy_ta")])
                                if order == 0:
                                    P.op("dve", lambda e: e.tensor_tensor(out=yb[:, c0 + tn, :], in0=ta[:, :], in1=uu[1][:, c0 + tn, :], op=ALU.mult),
                                         r=[T("hy_ta"), T("hy_u1")], w=[T("hy_y")])
                                else:
                                    P.op("dve", lambda e: e.tensor_tensor(out=yn[:, :], in0=ta[:, :], in1=uu[2][:, c0 + tn, :], op=ALU.mult),
                                         r=[T("hy_ta"), T("hy_u2")], w=[T("hy_yn")])
                                    pt_, ptt = next_ps()
                                    for q in range(2):
                                        P.op("pe", lambda e: e.transpose(out=pt_[:, q * 128:(q + 1) * 128], in_=yn[:, q * 128:(q + 1) * 128], identity=ident),
                                             r=[T("hy_yn"), T("consts")], w=[ptt])
                                    if L == 2048 and half == 1:
                                        for q in range(2):
                                            dstv = ybr[:, half * 2 + q, CTX:NT].rearrange("p (r c) -> p c r", c=64)[:, 4 * tn:4 * tn + 4, :]
                                            evac(dstv, pt_[:, q * 128:(q + 1) * 128].rearrange("p (c r) -> p c r", r=32), r=[ptt], w=[T("ybr", 0)])
                                    else:
                                        tg = c0 + tn
                                        evac(ybr[:, half * 2:half * 2 + 2, tg * 128:(tg + 1) * 128], pt_[:, 0:256].rearrange("p (q c) -> p q c", c=128),
                                             r=[ptt], w=[T("ybr", 0)])
                        P.full_barrier()

        def merge_branch(l, br, ybr):
            with ExitStack() as ph:
                al = lambda n_, s_, dt=F32: ph.enter_context(sbt(n_, list(s_), dt))
                wg = al("m_wg", [128, 8, 1024], BF16)
                wb = al("m_wb", [128, 4, 1024], BF16)
                wo = al("m_wo", [128, 8, 1024], BF16)
                hb = al("m_hb", [128, 8, 512], BF16)
                sg = al("m_sg", [128, 512])
                mg = al("m_mg", [128, 8, 512], BF16)
                P.dma("pool", wg[:, :, :], w_mg[l, br], w=[T("m_wg")])
                P.dma("pool", wb[:, :, :], w_br[l, br], w=[T("m_wb")])
                P.dma("pool", wo[:, :, :], w_o[l], w=[T("m_wo")])
                for tt in range(5):
                    t0, n = TT[tt]
                    c = 1 if tt == 0 else 0
                    load_h1(tt, hb)
                    for d in range(8):
                        pg, pgt = next_ps()
                        for k in range(8):
                            P.op("pe", lambda e: e.matmul(pg[:, :n], wg[:, k, d * 128:(d + 1) * 128], hb[:, k, :n], start=(k == 0), stop=(k == 7)),
                                 r=[T("m_wg"), T("hbm", k)], w=[pgt])
                        P.op("act", lambda e: e.activation(out=sg[:, :n], in_=pg[:, :n], func=AF.Sigmoid), r=[pgt], w=[T("m_sg")])
                        pp, ppt = next_ps()
                        for cc in range(4):
                            P.op("pe", lambda e: e.matmul(pp[:, :n], wb[:, cc, d * 128:(d + 1) * 128], ybr[:, cc, t0:t0 + n], start=(cc == 0), stop=(cc == 3)),
                                 r=[T("m_wb")] + [T("ybr", tc) for tc in range(t0 // 128, (t0 + n) // 128)] + [T("ybr", 0)], w=[ppt])
                        P.op("dve", lambda e: e.tensor_tensor(out=mg[:, d, :n], in0=pp[:, :n], in1=sg[:, :n], op=ALU.mult),
                             r=[ppt, T("m_sg")], w=[T("m_mg", d)])
                    for d2 in range(8):
                        po, pot = next_ps()
                        for d in range(8):
                            P.op("pe", lambda e: e.matmul(po[:, :n], wo[:, d, d2 * 128:(d2 + 1) * 128], mg[:, d, :n], start=(d == 0), stop=(d == 7)),
                                 r=[T("m_wo"), T("m_mg", d)], w=[pot])
                        P.op("dve", lambda e: e.scalar_tensor_tensor(out=xs[:, d2, t0:t0 + n], in0=po[:, :n], scalar=gate[:, 1, d2, c:c + 1],
                                                                  in1=xs[:, d2, t0:t0 + n], op0=ALU.mult, op1=ALU.add),
                             r=[pot, T("gate", 1), T("xs", d2, tt)], w=[T("xs", d2, tt)])
                P.full_barrier()

        def mixer(l, which):
            store_h1()
            with ExitStack() as mp:
                ybr = mp.enter_context(sbt("ybr", [128, 4, NT], BF16))
                dbgmode = which.startswith("dbg_") or which.startswith("dbgn_")
                for br, (nm, fn) in enumerate((("gdn", gdn_branch), ("hy", hyena_branch), ("ssd", ssd_branch), ("gla", gla_branch))):
                    if dbgmode and nm not in which:
                        continue
                    fn(l, ybr)
                    if not dbgmode:
                        merge_branch(l, br, ybr)
                if dbgmode:
                    P.dma("sp", dbg[:, :, :], ybr[:, :, :], r=[T("ybr", tc) for tc in range(18)])
                    P.full_barrier()

        for l in range(nlayers):
            ada_layer(l)
            if not stage.startswith("dbgn"):
                ffn(l, 0, 0)
            if stage == "ffn0":
                break
            if stage.startswith("dbg"):
                mixer(l, stage)
                break
            mixer(l, "all")
            if stage == "mix":
                break
            ffn(l, 1, 2)

        if stage == "full":
            for tt in range(1, 5):
                t0, n = TT[tt]
                norm_tile(None, tt, xs, t0, "xs")
        for d in range(8):
            P.dma("sp", outT[:, d, :], xs[:, d, CTX:NT], r=[P.T("xs", d, t) for t in range(1, 5)])
        P.barrier_all("sp")
    return nc


def _prep_shared(inp, NL_=DEPTH):
    inp = {k: (np.asarray(v)[:NL_] if k in _LAYERED else v) for k, v in inp.items()}
    f = np.float32
    sh = {}
    w_ada = np.asarray(inp["w_ada"], f)
    sh["w_ada"] = np.ascontiguousarray(w_ada.reshape(NL_, 8, 128, 9, 1024).transpose(0, 3, 2, 1, 4))
    sh["b_ada"] = np.ascontiguousarray(np.asarray(inp["b_ada"], f).reshape(NL_, 72, 128).transpose(0, 2, 1))
    sh["norm_w"] = np.ascontiguousarray(np.asarray(inp["norm_w"], f).reshape(NL_, 3, 8, 128).transpose(0, 3, 1, 2))
    up = np.asarray(inp["ffn_up"], f).reshape(NL_, 2, 8, 128, 2, NJ, 128)
    sh["ffn_up"] = np.ascontiguousarray(up.transpose(0, 1, 5, 3, 2, 4, 6)).reshape(NL_, 2, NJ, 128, 8, 256)
    dn = np.asarray(inp["ffn_down"], f).reshape(NL_, 2, NJ, 128, 8, 128)
    sh["ffn_dn"] = np.ascontiguousarray(dn.transpose(0, 1, 4, 3, 2, 5))
    sh["fin_w"] = np.ascontiguousarray(np.asarray(inp["final_norm"], f).reshape(8, 128).T)
    wbr = np.asarray(inp["w_branch"], f).reshape(NL_, 4, 4, 128, 1024)
    sh["w_br"] = np.ascontiguousarray(wbr.transpose(0, 1, 3, 2, 4))
    sh["w_o"] = np.ascontiguousarray(np.asarray(inp["w_out"], f).reshape(NL_, 8, 128, 1024).transpose(0, 2, 1, 3))
    sh["consts"] = _consts()
    w_in = np.asarray(inp["w_in"], f)

    def wslab(cols):
        return np.ascontiguousarray(w_in[:, :, cols].reshape(NL_, 8, 128, len(cols)).transpose(0, 2, 1, 3))

    def rep(a):
        a = np.asarray(a, f).reshape(NL_, -1)
        return np.broadcast_to(a[:, None, :], (NL_, 128, a.shape[1]))

    def chanmajor(a, nch):
        a = np.asarray(a, f)
        K = a.shape[1]
        return a.reshape(NL_, K, nch, 128).transpose(0, 3, 2, 1).reshape(NL_, 128, nch * K)

    ar = np.arange
    sh["w_ssd"] = wslab(np.concatenate([4112 + ar(768), 3600 + ar(512), 4880 + ar(16)]))
    sh["w_gla"] = np.ascontiguousarray(np.stack([wslab(np.concatenate([4896 + pr * 128 + ar(128), 5152 + pr * 128 + ar(128),
                                                                      5408 + pr * 256 + ar(256), 5920 + pr * 256 + ar(256)]))
                                                  for pr in range(2)], axis=1))
    _glr = np.zeros((NL_, 128, 8, 128), f)
    _glr[:, :, :, 0:32] = wslab(6432 + ar(32))
    sh["w_glr"] = _glr
    gkw = np.asarray(inp["gla_gk_w"], f)
    gkw_pad = np.zeros((NL_, 128, 2, 256), f)
    gkw_pad[:, 0:16, 0, :] = gkw[:, 0]
    gkw_pad[:, 16:32, 1, :] = gkw[:, 1]
    sh["lp_gla"] = np.ascontiguousarray(np.concatenate([
        rep(inp["gla_gk_b"]), rep(np.tile(np.asarray(inp["gla_norm"], f), (1, 4))), gkw_pad.reshape(NL_, 128, 512)], axis=2))
    sh.update(_hy_consts())
    sh["w_hy"] = np.ascontiguousarray(np.stack([wslab(np.concatenate([2064 + g_ * 512 + hf * 256 + ar(256) for g_ in range(3)]))
                                                 for hf in range(2)], axis=1))
    lph = np.zeros((NL_, 128, 436), f)
    lph[:, 0:33, 0:64] = np.asarray(inp["hy_w1"], f)
    w2 = np.asarray(inp["hy_w2"], f)
    lph[:, 0:64, 128:192] = w2[:, 0]
    lph[:, 0:64, 256:320] = w2[:, 1]
    lph[:, 0:64, 384] = np.asarray(inp["hy_b1"], f)
    b2 = np.asarray(inp["hy_b2"], f)
    lph[:, 0:64, 385] = b2[:, 0]
    lph[:, 0:64, 386] = b2[:, 1]
    lph[:, 0:64, 387] = np.asarray(inp["hy_freq"], f)
    lph[:, :, 388:424] = chanmajor(np.asarray(inp["hy_conv_w"], f).reshape(NL_, 3, 1536), 12)
    lph[:, :, 424:436] = chanmajor(np.asarray(inp["hy_conv_b"], f).reshape(NL_, 1, 1536), 12)
    sh["lp_hy"] = lph
    wo = np.zeros((NL_, 128, 2048), f)
    wo[:, 0:64, :] = np.asarray(inp["hy_wout"], f)
    sh["hy_wout"] = wo
    sh["hy_biasr"] = np.ascontiguousarray(rep(inp["hy_bias"]))
    sh["w_gdn"] = np.ascontiguousarray(np.stack([wslab(np.concatenate([hh * 128 + ar(128), 512 + hh * 128 + ar(128),
                                                                      1024 + hh * 128 + ar(128), 1536 + hh * 128 + ar(128)]))
                                                  for hh in range(4)], axis=1))
    sh["w_gbd"] = wslab(2048 + ar(16))
    sh["lp_gdn"] = np.ascontiguousarray(np.concatenate([
        chanmajor(inp["gdn_conv"], 12), rep(inp["gdn_a_log"]), rep(inp["gdn_dt_bias"]),
        rep(np.tile(np.asarray(inp["gdn_norm"], f), (1, 4)))], axis=2))
    sh["w_mg"] = np.ascontiguousarray(np.stack([wslab(6464 + b_ * 1024 + ar(1024)) for b_ in range(4)], axis=1))
    sh["lp_ssd"] = np.ascontiguousarray(np.concatenate([
        chanmajor(inp["ssd_conv_w"], 6), chanmajor(np.asarray(inp["ssd_conv_b"], f)[:, None, :], 6),
        rep(inp["ssd_dt_bias"]), rep(inp["ssd_a_log"]), rep(inp["ssd_d"]), rep(inp["ssd_norm"])], axis=2))
    return sh


_LAYERED = ("w_ada", "b_ada", "norm_w", "ffn_up", "ffn_down", "w_in", "gdn_conv", "gdn_a_log", "gdn_dt_bias", "gdn_norm",
            "hy_conv_w", "hy_conv_b", "hy_w1", "hy_b1", "hy_w2", "hy_b2", "hy_wout", "hy_freq", "hy_bias",
            "ssd_conv_w", "ssd_conv_b", "ssd_a_log", "ssd_dt_bias", "ssd_d", "ssd_norm",
            "gla_gk_w", "gla_gk_b", "gla_norm", "w_branch", "w_out")


_HYC = {}


def _hy_consts():
    if _HYC:
        return _HYC
    import ml_dtypes
    bf = ml_dtypes.bfloat16
    out = {}
    for L, tag in ((2048, "l"), (256, "c")):
        N = 2 * L
        nN = L // 128
        nK = (L + 1 + 127) // 128
        t = np.linspace(0.0, 1.0, L, dtype=np.float32)[:, None].astype(np.float64)
        ang = 2.0 * np.pi * np.arange(L, dtype=np.float64)[:, None] / L
        fr = np.linspace(1e-4, 15, 16, dtype=np.float32)[None, :].astype(np.float64)
        z = np.concatenate([t, np.cos(fr * ang), -np.sin(fr * ang)], axis=-1)
        zT = np.zeros((128, L), np.float32)
        zT[0:33] = z.T
        out["hy_z" + tag] = zT
        max_decay = np.log(1e-2) / 0.3
        min_decay = np.log(1e-2) / 1.5
        deltas = np.abs(np.linspace(min_decay, max_decay, 512, dtype=np.float32)).astype(np.float64)
        win = (np.exp(-t * deltas[None, :]) + 0.05).astype(np.float32)
        out["hy_win_" + tag] = np.ascontiguousarray(win.reshape(nN, 128, 512))
        w0 = win[0:128].copy()
        w0[0, :] = 0.0
        out["hy_win0_" + tag] = w0
        n = np.arange(L, dtype=np.float64)
        k = np.arange(nK * 128, dtype=np.float64)
        th = 2.0 * np.pi / N
        valid = (k <= L)
        ph_ = th * np.outer(n, k)
        C = np.cos(ph_) * valid[None, :]
        S_ = np.sin(ph_) * valid[None, :]
        out["hy_fc_" + tag] = np.ascontiguousarray(C.reshape(nN, 128, nK, 128).transpose(2, 1, 0, 3)).astype(bf)
        out["hy_fs_" + tag] = np.ascontiguousarray(S_.reshape(nN, 128, nK, 128).transpose(2, 1, 0, 3)).astype(bf)
        wk = np.where((k == 0) | (k == L), 1.0 / N, 2.0 / N) * valid
        Ci = (C * wk[None, :]).T
        Si = (S_ * wk[None, :]).T
        out["hy_ic_" + tag] = np.ascontiguousarray(Ci.reshape(nK, 128, nN, 128).transpose(2, 1, 0, 3)).astype(bf)
        out["hy_is_" + tag] = np.ascontiguousarray(Si.reshape(nK, 128, nN, 128).transpose(2, 1, 0, 3)).astype(bf)
    _HYC.update(out)
    return _HYC


def _consts():
    i = np.arange(128)
    t, s_ = i[:, None], i[None, :]
    ident = (t == s_)
    tri0 = (t <= s_)
    tri1 = (t >= s_)
    v0 = (s_ >= t)
    v1 = (s_ <= t)
    st0 = (s_ > t)
    st1 = (s_ < t)
    parts = [ident, tri0, tri1, np.where(v0, 0.0, -1e5), np.where(v1, 0.0, -1e5), v0, v1, st0, st1]
    return np.ascontiguousarray(np.concatenate([np.asarray(p, np.float32) for p in parts], axis=1))


def _prep_core(inp, b):
    f = np.float32
    x = np.asarray(inp["x"][b], f)
    ctx = np.asarray(inp["ctx"][b], f)
    seq = np.concatenate([ctx, x], axis=0)
    xT = np.ascontiguousarray(seq.T.reshape(8, 128, NT).transpose(1, 0, 2))
    cond = np.stack([np.asarray(inp["c"][b], f).reshape(8, 128).T,
                     np.asarray(inp["c_ctx"], f).reshape(8, 128).T], axis=-1)
    return {"xT": xT, "cond": np.ascontiguousarray(cond)}


def run(inputs, nlayers=DEPTH, stage="full"):
    nc = build_nc(nlayers, stage)
    sh = _prep_shared(inputs, nlayers)
    in_maps = []
    for b in range(8):
        m = dict(sh)
        m.update(_prep_core(inputs, b))
        in_maps.append(m)
    res = run_bass_kernel_spmd(nc, in_maps, core_ids=list(range(8)))
    out = np.stack([r["outT"].transpose(2, 1, 0).reshape(SEQ, D) for r in res.results], axis=0)
    return out.astype(np.float32)


def kernel(**inputs):
    return run(inputs)
```

```python
import numpy as np
from contextlib import ExitStack
import concourse.bass as bass
import concourse.mybir as mybir
from concourse.bass_utils import run_bass_kernel_spmd

F32 = mybir.dt.float32
BF16 = mybir.dt.bfloat16
ALU = mybir.AluOpType
AF = mybir.ActivationFunctionType

D = 1024
DEPTH = 4
SEQ = 2048
CTX = 256
NT = SEQ + CTX
DFF = 2816
NJ = DFF // 128
EPS = 1e-6
import os
CUT = int(os.environ.get('KCUT', '0'))
KCH = int(os.environ.get('KCH', '99'))
KSUB = int(os.environ.get('KSUB', '9'))
EVE = os.environ.get('EVE', None)
TT = [(0, 256), (256, 512), (768, 512), (1280, 512), (1792, 512)]
GROUPS = [[0, 1, 2], [3, 4]]


class Tl:
    __slots__ = ("w", "r", "name", "excl")

    def __init__(self, name=""):
        self.w = None
        self.r = {}
        self.name = name
        self.excl = False


class Prog:
    NSLOT = 12

    def __init__(self, nc, es):
        self.nc = nc
        self.eng = {"pe": nc.tensor, "act": nc.scalar, "dve": nc.vector, "pool": nc.gpsimd, "sp": nc.sync}
        self.banks = [{e: es.enter_context(nc.semaphore("s%d_%s" % (b, e))) for e in ("pe", "act", "dve", "pool")} for b in range(2)]
        self.bank = 0
        self.sem = self.banks[0]
        self.epoch_sem = es.enter_context(nc.semaphore("s_epoch"))
        self.epoch = 0
        self.cnt = {e: 0 for e in self.eng}
        self.known = {e: {} for e in self.eng}
        self.dsem = {}
        self.dtarget = {}
        self.drr = {}
        for q in ("sp", "pool"):
            self.dsem[q] = [es.enter_context(nc.semaphore("d_%s%d" % (q, i))) for i in range(self.NSLOT)]
            self.dtarget[q] = [0] * self.NSLOT
            self.drr[q] = 0
        self.tiles = {}
        self.nops = 0

    def T(self, *key):
        t = self.tiles.get(key)
        if t is None:
            t = Tl(str(key))
            self.tiles[key] = t
        return t

    def _wait(self, e, ref):
        if ref[0] == "dma":
            _, q, slot, target = ref
            k = ("dma", q, slot)
            if self.known[e].get(k, 0) >= target:
                return
            self.eng[e].wait_ge(self.dsem[q][slot], target)
            self.known[e][k] = target
        else:
            f, idx = ref
            if self.known[e].get(f, 0) >= idx:
                return
            self.eng[e].wait_ge(self.sem[f], idx)
            self.known[e][f] = idx

    def _deps(self, e, r, w):
        for t in r:
            if t.w is not None:
                ref = t.w
                if ref[0] == e and e == "pe":
                    continue
                self._wait(e, ref)
            if t.excl:
                for f, ref in t.r.items():
                    if ref[0] != e:
                        self._wait(e, ref)
        for t in w:
            if t.w is not None:
                ref = t.w
                if not (ref[0] == e and e == "pe"):
                    self._wait(e, ref)
            for f, ref in t.r.items():
                if not (ref[0] == e and e == "pe"):
                    self._wait(e, ref)

    def _commit(self, ref, r, w):
        for t in w:
            t.w = ref
            t.r = {}
        for t in r:
            key = ref[0] if ref[0] != "dma" else ref[:3]
            t.r[key] = ref

    def op(self, e, fn, r=(), w=()):
        self._deps(e, r, w)
        ins = fn(self.eng[e])
        self.cnt[e] += 1
        ins.then_inc(self.sem[e], 1)
        ref = (e, self.cnt[e])
        self._commit(ref, r, w)
        self.nops += 1
        return ref

    def dma(self, q, out, in_, r=(), w=(), **kw):
        self._deps(q, r, w)
        slot = self.drr[q]
        self.drr[q] = (slot + 1) % self.NSLOT
        if self.dtarget[q][slot] > 0:
            self._wait(q, ("dma", q, slot, self.dtarget[q][slot]))
        self.dtarget[q][slot] += 16
        self.eng[q].dma_start(out=out, in_=in_, **kw).then_inc(self.dsem[q][slot], 16)
        ref = ("dma", q, slot, self.dtarget[q][slot])
        self._commit(ref, r, w)
        self.nops += 1
        return ref

    def full_barrier(self):
        for e in self.eng:
            self.barrier_all(e)
        old = 1 - self.bank
        for e in ("pe", "act", "dve", "pool"):
            self.nc.vector.sem_clear(self.banks[old][e])
        self.nc.vector.sem_inc(self.epoch_sem, 1)
        self.epoch += 1
        for e in ("pe", "act", "pool", "sp"):
            self.eng[e].wait_ge(self.epoch_sem, self.epoch)
        self.bank = old
        self.sem = self.banks[old]
        for e in ("pe", "act", "dve", "pool"):
            self.cnt[e] = 0
        for e in self.known:
            self.known[e] = {k: v for k, v in self.known[e].items() if isinstance(k, tuple)}
        for t in self.tiles.values():
            t.w = None
            t.r = {}

    def barrier_all(self, e_final="sp"):
        for f in ("pe", "act", "dve", "pool"):
            if self.cnt[f] > 0 and f != e_final:
                self._wait(e_final, (f, self.cnt[f]))
        for q in ("sp", "pool"):
            for slot in range(self.NSLOT):
                if self.dtarget[q][slot] > 0:
                    self._wait(e_final, ("dma", q, slot, self.dtarget[q][slot]))


def build_nc(nlayers=DEPTH, stage="full"):
    nc = bass.Bass("TRN2", target_bir_lowering=False)
    dt_in = {}

    def din(name, shape, dt=F32):
        dt_in[name] = nc.dram_tensor(name, list(shape), dt, kind="ExternalInput").ap()
        return dt_in[name]

    xT = din("xT", [128, 8, NT])
    cond = din("cond", [128, 8, 2])
    w_ada = din("w_ada", [nlayers, 9, 128, 8, 1024])
    b_ada = din("b_ada", [nlayers, 128, 72])
    norm_w = din("norm_w", [nlayers, 128, 3, 8])
    ffn_up = din("ffn_up", [nlayers, 2, NJ, 128, 8, 256])
    ffn_dn = din("ffn_dn", [nlayers, 2, 8, 128, NJ, 128])
    fin_w = din("fin_w", [128, 8])
    consts = din("consts", [128, 9 * 128])
    w_ssd = din("w_ssd", [nlayers, 128, 8, 1296])
    lp_ssd = din("lp_ssd", [nlayers, 128, 588])
    w_gla = din("w_gla", [nlayers, 2, 128, 8, 768])
    w_glr = din("w_glr", [nlayers, 128, 8, 128])
    w_hy = din("w_hy", [nlayers, 2, 128, 8, 768])
    lp_hy = din("lp_hy", [nlayers, 128, 436])
    hy_wout = din("hy_wout", [nlayers, 128, 2048])
    hy_biasr = din("hy_biasr", [nlayers, 128, 1024])
    hy_zl = din("hy_zl", [128, 2048])
    hy_zc = din("hy_zc", [128, 256])
    hy_fc_l = din("hy_fc_l", [17, 128, 16, 128], BF16)
    hy_fs_l = din("hy_fs_l", [17, 128, 16, 128], BF16)
    hy_ic_l = din("hy_ic_l", [16, 128, 17, 128], BF16)
    hy_is_l = din("hy_is_l", [16, 128, 17, 128], BF16)
    hy_fc_c = din("hy_fc_c", [3, 128, 2, 128], BF16)
    hy_fs_c = din("hy_fs_c", [3, 128, 2, 128], BF16)
    hy_ic_c = din("hy_ic_c", [2, 128, 3, 128], BF16)
    hy_is_c = din("hy_is_c", [2, 128, 3, 128], BF16)
    hy_win_l = din("hy_win_l", [16, 128, 512])
    hy_win0_l = din("hy_win0_l", [128, 512])
    hy_win_c = din("hy_win_c", [2, 128, 512])
    hy_win0_c = din("hy_win0_c", [128, 512])
    w_gdn = din("w_gdn", [nlayers, 4, 128, 8, 512])
    w_gbd = din("w_gbd", [nlayers, 128, 8, 16])
    lp_gdn = din("lp_gdn", [nlayers, 128, 588])
    lp_gla = din("lp_gla", [nlayers, 128, 1536])
    h1s = nc.dram_tensor("h1s", [128, 8, NT], BF16, kind="Internal").ap()
    w_mg = din("w_mg", [nlayers, 4, 128, 8, 1024])
    w_br = din("w_br", [nlayers, 4, 128, 4, 1024])
    w_o = din("w_o", [nlayers, 128, 8, 1024])
    dbg = nc.dram_tensor("dbg", [128, 4, NT], BF16, kind="ExternalOutput").ap() if stage.startswith("dbg") else None
    outT = nc.dram_tensor("outT", [128, 8, SEQ], F32, kind="ExternalOutput").ap()

    _uc = [0]

    def sbt(name, shape, dt):
        _uc[0] += 1
        return nc.sbuf_tensor("%s_%d" % (name, _uc[0]), shape, dt)

    with ExitStack() as es:
        P = Prog(nc, es)

        def sb(name, shape, dt=F32):
            return es.enter_context(sbt(name, list(shape), dt))

        xs = sb("xs", [128, 8, NT])
        ones_bf = sb("ones_bf", [128, 128], BF16)
        condf = sb("condf", [128, 8, 2])
        condb = sb("condb", [128, 8, 2], BF16)
        mod = sb("mod", [128, 72, 2])
        bada = sb("bada", [128, 72])
        nw = sb("nw", [128, 3, 8])
        seff = sb("seff", [128, 3, 8, 2])
        gate = sb("gate", [128, 3, 8, 2])
        finw = sb("finw", [128, 8])
        sq = sb("sq", [128, 8, 512], BF16)
        lnv = sb("lnv", [128, 512])
        rstd = sb("rstd", [128, 512])
        tmpn = [sb("tmpn%d" % i, [128, 512]) for i in range(2)]
        epsb = sb("epsb", [128, 1])
        psum = [es.enter_context(nc.psum_tensor("ps%d" % i, [128, 512], F32)) for i in range(8)]
        pst = [P.T("ps", i) for i in range(8)]
        for t_ in pst:
            t_.excl = True
        prr = [0]

        def next_ps():
            i = prr[0]
            prr[0] = (i + 1) % 8
            return psum[i], pst[i]

        P.op("dve", lambda e: e.memset(ones_bf[:, :], 1.0), w=[P.T("ones_bf")])
        P.op("dve", lambda e: e.memset(epsb[:, :], EPS), w=[P.T("epsb")])
        for d in range(8):
            P.dma("sp", xs[:, d, :], xT[:, d, :], w=[P.T("xs", d, t) for t in range(5)])
        P.dma("sp", condf[:, :, :], cond[:, :, :], w=[P.T("condf")])
        P.dma("sp", finw[:, :], fin_w[:, :], w=[P.T("finw")])
        P.op("act", lambda e: e.activation(out=condb[:, :, :], in_=condf[:, :, :], func=AF.Silu),
             r=[P.T("condf")], w=[P.T("condb")])

        def ada_layer(l):
          with ExitStack() as ph:
            adar = [ph.enter_context(sbt("adar%d" % i, [128, 8, 1024], BF16)) for i in range(2)]
            ada_layer_(l, adar)
            P.full_barrier()

        def ada_layer_(l, adar):
            P.dma("sp", bada[:, :], b_ada[l], w=[P.T("bada")])
            P.dma("sp", nw[:, :, :], norm_w[l], w=[P.T("nw")])
            for j in range(9):
                slab = adar[j % 2]
                st = P.T("adar", j % 2)
                P.dma("pool", slab[:, :, :], w_ada[l, j], w=[st])
                ps, pt = next_ps()
                for dch in range(8):
                    for k in range(8):
                        P.op("pe", lambda e, dch=dch, k=k: e.matmul(
                            ps[:, dch * 2:dch * 2 + 2], slab[:, k, dch * 128:(dch + 1) * 128], condb[:, k, :],
                            start=(k == 0), stop=(k == 7)), r=[st, P.T("condb")], w=[pt])
                P.op("dve", lambda e, j=j: e.tensor_tensor(
                    out=mod[:, j * 8:(j + 1) * 8, :],
                    in0=ps[:, 0:16].rearrange("p (d c) -> p d c", c=2),
                    in1=bada[:, j * 8:(j + 1) * 8].unsqueeze(2).to_broadcast([128, 8, 2]),
                    op=ALU.add), r=[pt, P.T("bada")], w=[P.T("mod", j)])
            for i in range(3):
                P.op("dve", lambda e, i=i: e.scalar_tensor_tensor(
                    out=seff[:, i, :, :], in0=mod[:, (3 * i + 1) * 8:(3 * i + 2) * 8, :], scalar=1.0,
                    in1=nw[:, i, :].unsqueeze(2).to_broadcast([128, 8, 2]), op0=ALU.add, op1=ALU.mult),
                    r=[P.T("mod", 3 * i + 1), P.T("nw")], w=[P.T("seff", i)])
                gsc = 1.0 if i == 1 else 0.5
                P.op("dve", lambda e, i=i, gsc=gsc: e.tensor_scalar(
                    out=gate[:, i, :, :], in0=mod[:, (3 * i + 2) * 8:(3 * i + 3) * 8, :], scalar1=gsc,
                    scalar2=None, op0=ALU.mult), r=[P.T("mod", 3 * i + 2)], w=[P.T("gate", i)])

        def norm_tile(i, tt, dst, dst_off, dst_key, key_tt=True):
            t0, n = TT[tt]
            c = 1 if tt == 0 else 0
            for d in range(8):
                P.op("act", lambda e, d=d: e.activation(out=sq[:, d, :n], in_=xs[:, d, t0:t0 + n], func=AF.Square),
                     r=[P.T("xs", d, tt)], w=[P.T("sq", d)])
            ps, pt = next_ps()
            for d in range(8):
                P.op("pe", lambda e, d=d: e.matmul(ps[:, :n], ones_bf[:, :], sq[:, d, :n], start=(d == 0), stop=(d == 7)),
                     r=[P.T("ones_bf"), P.T("sq", d)], w=[pt])
            P.op("act", lambda e: e.activation(out=lnv[:, :n], in_=ps[:, :n], func=AF.Ln, bias=epsb[:, 0:1], scale=1.0 / D),
                 r=[pt, P.T("epsb")], w=[P.T("lnv")])
            P.op("act", lambda e: e.activation(out=rstd[:, :n], in_=lnv[:, :n], func=AF.Exp, scale=-0.5),
                 r=[P.T("lnv")], w=[P.T("rstd")])
            for d in range(8):
                tb = tmpn[d % 2]
                tk = P.T("tmpn", d % 2)
                P.op("dve", lambda e, d=d, tb=tb: e.tensor_tensor(out=tb[:, :n], in0=xs[:, d, t0:t0 + n], in1=rstd[:, :n], op=ALU.mult),
                     r=[P.T("xs", d, tt), P.T("rstd")], w=[tk])
                if i is None:
                    P.op("act", lambda e, d=d, tb=tb: e.activation(
                        out=dst[:, d, dst_off:dst_off + n], in_=tb[:, :n], func=AF.Identity, scale=finw[:, d:d + 1]),
                        r=[tk, P.T("finw")], w=[P.T(dst_key, d, tt) if key_tt else P.T(dst_key, d)])
                else:
                    P.op("act", lambda e, d=d, tb=tb: e.activation(
                        out=dst[:, d, dst_off:dst_off + n], in_=tb[:, :n], func=AF.Identity,
                        bias=mod[:, 3 * i * 8 + d, c:c + 1], scale=seff[:, i, d, c:c + 1]),
                        r=[tk, P.T("mod", 3 * i), P.T("seff", i)], w=[P.T(dst_key, d, tt) if key_tt else P.T(dst_key, d)])

        def ffn(l, s, i):
          with ExitStack() as ph:
            hb = ph.enter_context(sbt("hb", [128, 8, 1280], BF16))
            actb = ph.enter_context(sbt("actb", [128, NJ, 1280], BF16))
            upr = [ph.enter_context(sbt("upr%d" % i, [128, 8, 256], BF16)) for i in range(3)]
            dnr = [ph.enter_context(sbt("dnr%d" % i, [128, NJ, 128], BF16)) for i in range(2)]
            tmps = [ph.enter_context(sbt("tmps%d" % i, [128, 512], F32)) for i in range(2)]
            ffn_(l, s, i, hb, actb, upr, dnr, tmps)
            P.full_barrier()

        def ffn_(l, s, i, hb, actb, upr, dnr, tmps):
            for grp in GROUPS:
                offs = {}
                o = 0
                for tt in grp:
                    offs[tt] = o
                    o += TT[tt][1]
                for tt in grp:
                    norm_tile(i, tt, hb, offs[tt], "hb")
                for j in range(NJ):
                    slab = upr[j % 3]
                    st = P.T("upr", j % 3)
                    P.dma("pool", slab[:, :, :], ffn_up[l, s, j], w=[st])
                    for tt in grp:
                        n = TT[tt][1]
                        o = offs[tt]
                        pa, pat = next_ps()
                        for k in range(8):
                            P.op("pe", lambda e, k=k, pa=pa: e.matmul(pa[:, :n], slab[:, k, 0:128], hb[:, k, o:o + n],
                                                                  start=(k == 0), stop=(k == 7)),
                                 r=[st, P.T("hb", k, tt)], w=[pat])
                        pg, pgt = next_ps()
                        for k in range(8):
                            P.op("pe", lambda e, k=k, pg=pg: e.matmul(pg[:, :n], slab[:, k, 128:256], hb[:, k, o:o + n],
                                                                  start=(k == 0), stop=(k == 7)),
                                 r=[st, P.T("hb", k, tt)], w=[pgt])
                        tb = tmps[(j + tt) % 2]
                        tk = P.T("tmps", (j + tt) % 2)
                        P.op("act", lambda e, pa=pa, tb=tb: e.activation(out=tb[:, :n], in_=pa[:, :n], func=AF.Silu),
                             r=[pat], w=[tk])
                        P.op("dve", lambda e, pg=pg, tb=tb, j=j: e.tensor_tensor(out=actb[:, j, o:o + n], in0=pg[:, :n], in1=tb[:, :n], op=ALU.mult),
                             r=[pgt, tk], w=[P.T("actb", j, tt)])
                for d in range(8):
                    slab = dnr[d % 2]
                    st = P.T("dnr", d % 2)
                    P.dma("pool", slab[:, :, :], ffn_dn[l, s, d], w=[st])
                    for tt in grp:
                        t0, n = TT[tt]
                        o = offs[tt]
                        c = 1 if tt == 0 else 0
                        py, pyt = next_ps()
                        for j in range(NJ):
                            P.op("pe", lambda e, j=j, py=py: e.matmul(py[:, :n], slab[:, j, :], actb[:, j, o:o + n],
                                                                  start=(j == 0), stop=(j == NJ - 1)),
                                 r=[st, P.T("actb", j, tt)], w=[pyt])
                        P.op("dve", lambda e, py=py, d=d: e.scalar_tensor_tensor(
                            out=xs[:, d, t0:t0 + n], in0=py[:, :n], scalar=gate[:, i, d, c:c + 1], in1=xs[:, d, t0:t0 + n],
                            op0=ALU.mult, op1=ALU.add), r=[pyt, P.T("gate", i), P.T("xs", d, tt)], w=[P.T("xs", d, tt)])


        T = P.T
        consts_sb = sb("consts_sb", [128, 9 * 128])
        P.dma("sp", consts_sb[:, :], consts[:, :], w=[T("consts")])
        ident = consts_sb[:, 0:128]
        tri = [consts_sb[:, 128:256], consts_sb[:, 256:384]]
        negm = [consts_sb[:, 384:512], consts_sb[:, 512:640]]
        msk = [consts_sb[:, 640:768], consts_sb[:, 768:896]]
        smsk = [consts_sb[:, 896:1024], consts_sb[:, 1024:1152]]
        ones_f = sb("ones_f", [128, 128])
        one_col = sb("one_col", [128, 1])
        P.op("dve", lambda e: e.memset(ones_f[:, :], 1.0), w=[T("ones_f")])
        P.op("dve", lambda e: e.memset(one_col[:, :], 1.0), w=[T("one_col")])
        CK = [T("consts"), T("ones_f"), T("one_col"), T("epsb")]
        evt = [0]

        def evac(dst, src, r, w, func=None, eng=None):
            if func is not None or eng == "act" or (eng is None and evt[0] % 2 == 0 and eng != "dve"):
                P.op("act", lambda e: e.activation(out=dst, in_=src, func=(func or AF.Copy)), r=r, w=w)
            else:
                P.op("dve", lambda e: e.tensor_copy(out=dst, in_=src), r=r, w=w)
            evt[0] += 1

        def store_h1():
            with ExitStack() as p0:
                hst = p0.enter_context(sbt("hst", [128, 8, 512], BF16))
                for tt in range(5):
                    t0, n = TT[tt]
                    norm_tile(1, tt, hst, 0, "hst", key_tt=False)
                    P.dma("sp", h1s[:, :, t0:t0 + n], hst[:, :, :n], r=[T("hst", d) for d in range(8)], w=[T("h1s", tt)])
                P.full_barrier()

        def load_h1(tt, hb):
            t0, n = TT[tt]
            P.dma("sp", hb[:, :, :n], h1s[:, :, t0:t0 + n], r=[T("h1s", tt)], w=[T("hbm", k) for k in range(8)])

        def bc(ap, axis, shape):
            return ap.unsqueeze(axis).to_broadcast(list(shape))

        def conv_fm(src, dst, taps, bias, K, rk, wk, segs=((0, CTX), (CTX, NT))):
            left = (K - 1) // 2
            for (a, b) in segs:
                if bias is not None:
                    P.op("dve", lambda e: e.tensor_scalar(out=dst[:, a:b], in0=src[:, a:b], scalar1=taps[left], scalar2=bias,
                                                       op0=ALU.mult, op1=ALU.add), r=rk, w=wk)
                else:
                    P.op("dve", lambda e: e.tensor_scalar(out=dst[:, a:b], in0=src[:, a:b], scalar1=taps[left], scalar2=None,
                                                       op0=ALU.mult), r=rk, w=wk)
                for i in range(K):
                    if i == left:
                        continue
                    sft = i - left
                    lo = max(a, a - sft)
                    hi = min(b, b - sft)
                    P.op("dve", lambda e: e.scalar_tensor_tensor(out=dst[:, lo:hi], in0=src[:, lo + sft:hi + sft], scalar=taps[i],
                                                              in1=dst[:, lo:hi], op0=ALU.mult, op1=ALU.add), r=rk + wk, w=wk)

        def tr_to_tok(src_fm, rk, dstfn, wk, nchunks=18, c0=0):
            for g0 in range(0, nchunks, 4):
                nn = min(4, nchunks - g0)
                ps, pt = next_ps()
                for q in range(nn):
                    P.op("pe", lambda e: e.transpose(out=ps[:, q * 128:(q + 1) * 128],
                                                     in_=src_fm[:, (g0 + q) * 128:(g0 + q + 1) * 128], identity=ident),
                         r=rk + [T("consts")], w=[pt])
                evac(dstfn(c0 + g0, nn), ps[:, :nn * 128].rearrange("p (q c) -> p q c", c=128), r=[pt], w=wk)

        def proj_fm(wsrc, ncols, dst, dstkey, hb, slab, slabkey):
            P.dma("pool", slab[:, :, :ncols], wsrc, w=[slabkey])
            for tt in range(5):
                t0, n = TT[tt]
                load_h1(tt, hb)
                for cc in range(ncols // 128):
                    ps, pt = next_ps()
                    for k in range(8):
                        P.op("pe", lambda e: e.matmul(ps[:, :n], slab[:, k, cc * 128:(cc + 1) * 128], hb[:, k, :n],
                                                      start=(k == 0), stop=(k == 7)), r=[slabkey, T("hbm", k)], w=[pt])
                    evac(dst[:, cc, t0:t0 + n], ps[:, :n], r=[pt], w=[T(dstkey, cc)])

        def proj_tm(wsrc, ncols, hb, slab, slabkey, consume):
            P.dma("pool", slab[:, :, :ncols], wsrc, w=[slabkey])
            for tt in range(5):
                t0, n = TT[tt]
                load_h1(tt, hb)
                for sub in range(n // 128):
                    tc = t0 // 128 + sub
                    ps, pt = next_ps()
                    for k in range(8):
                        P.op("pe", lambda e: e.matmul(ps[:, :ncols], hb[:, k, sub * 128:(sub + 1) * 128], slab[:, k, :ncols],
                                                      start=(k == 0), stop=(k == 7)), r=[slabkey, T("hbm", k)], w=[pt])
                    consume(tc, ps, pt)

        def softplus_inplace(buf, key, tmp, tmpkey):
            P.op("act", lambda e: e.activation(out=tmp, in_=buf, func=AF.Exp), r=[key], w=[tmpkey])
            P.op("act", lambda e: e.activation(out=buf, in_=tmp, func=AF.Ln, bias=one_col[:, 0:1], scale=1.0),
                 r=[tmpkey, T("one_col")], w=[key])

        def finish(l, wg_src, oaccfn, ybr, pre_gate, ngrp, nrm, lpkey, ph, skipfn=None, W=512, ch0=0):
            al = lambda n_, s_, dt=F32: ph.enter_context(sbt(n_, list(s_), dt))
            hb = al("fin_hb", [128, 8, 512], BF16)
            wg = al("fin_wg", [128, 8, W], BF16)
            zs = tmpn[1]
            tq = al("fin_t", [128, W])
            sqv = lnv
            ssq = al("fin_ssq", [128, 8])
            gs = W // ngrp

            def consume(tc, ps, pt):
                evac(zs[:, :W], ps[:, :W], r=[pt], w=[T("tmpn", 1)], func=AF.Silu)
                src, rk = oaccfn(tc)
                if skipfn is not None:
                    skipfn(tc, tq)
                    src = tq[:, :]
                    rk = [T("fin_t")]
                if pre_gate:
                    P.op("dve", lambda e: e.tensor_tensor(out=tq[:, :], in0=src, in1=zs[:, :W], op=ALU.mult),
                         r=rk + [T("tmpn", 1)], w=[T("fin_t")])
                    src = tq[:, :]
                    rk = [T("fin_t")]
                P.op("dve", lambda e: e.tensor_tensor(out=sqv[:, :W], in0=src, in1=src, op=ALU.mult), r=rk, w=[T("lnv")])
                P.op("dve", lambda e: e.tensor_reduce(out=ssq[:, :ngrp], in_=sqv[:, :W].rearrange("p (g c) -> p g c", c=gs),
                                                   axis=mybir.AxisListType.X, op=ALU.add), r=[T("lnv")], w=[T("fin_ssq")])
                P.op("act", lambda e: e.activation(out=ssq[:, :ngrp], in_=ssq[:, :ngrp], func=AF.Ln, bias=epsb[:, 0:1], scale=1.0 / gs),
                     r=[T("fin_ssq"), T("epsb")], w=[T("fin_ssq")])
                P.op("act", lambda e: e.activation(out=ssq[:, :ngrp], in_=ssq[:, :ngrp], func=AF.Exp, scale=-0.5),
                     r=[T("fin_ssq")], w=[T("fin_ssq")])
                P.op("dve", lambda e: e.tensor_tensor(out=tq[:, :].rearrange("p (g c) -> p g c", c=gs),
                                                   in0=src.rearrange("p (g c) -> p g c", c=gs),
                                                   in1=bc(ssq[:, :ngrp], 2, [128, ngrp, gs]), op=ALU.mult),
                     r=rk + [T("fin_ssq")], w=[T("fin_t")])
                P.op("dve", lambda e: e.tensor_tensor(out=tq[:, :], in0=tq[:, :], in1=nrm, op=ALU.mult),
                     r=[T("fin_t"), lpkey], w=[T("fin_t")])
                if not pre_gate:
                    P.op("dve", lambda e: e.tensor_tensor(out=tq[:, :], in0=tq[:, :], in1=zs[:, :W], op=ALU.mult),
                         r=[T("fin_t"), T("tmpn", 1)], w=[T("fin_t")])
                ps2, pt2 = next_ps()
                nq = W // 128
                for cc in range(nq):
                    P.op("pe", lambda e: e.transpose(out=ps2[:, cc * 128:(cc + 1) * 128], in_=tq[:, cc * 128:(cc + 1) * 128],
                                                     identity=ident), r=[T("fin_t"), T("consts")], w=[pt2])
                evac(ybr[:, ch0:ch0 + nq, tc * 128:(tc + 1) * 128], ps2[:, :W].rearrange("p (q c) -> p q c", c=128), r=[pt2],
                     w=[T("ybr", tc)])

            proj_tm(wg_src, W, hb, wg, T("fin_wg"), consume)

        def decay_mats(a_c, d, H, wk, bufs, rk):
            X, gca, egl, ewl, Dm, E = bufs
            P.op("dve", lambda e: e.tensor_tensor(out=X[:, :H, :], in0=bc(tri[d], 1, [128, H, 128]), in1=bc(a_c, 2, [128, H, 128]),
                                               op=ALU.mult), r=rk + [T("consts")], w=[T(wk, "X")])
            pg, pgt = next_ps()
            P.op("pe", lambda e: e.matmul(pg[:, 0:H], tri[d], a_c, start=True, stop=True), r=rk + [T("consts")], w=[pgt])
            P.op("pe", lambda e: e.matmul(pg[:, H:2 * H], ones_f[:, :], a_c, start=True, stop=True), r=rk + [T("ones_f")], w=[pgt])
            evac(gca[:, :2 * H], pg[:, :2 * H], r=[pgt], w=[T(wk, "gca")], eng="act")
            P.op("act", lambda e: e.activation(out=egl[:, :2 * H], in_=gca[:, :2 * H], func=AF.Exp), r=[T(wk, "gca")], w=[T(wk, "egl")])
            P.op("dve", lambda e: e.tensor_tensor(out=ewl[:, :H], in0=gca[:, H:2 * H], in1=gca[:, 0:H], op=ALU.subtract),
                 r=[T(wk, "gca")], w=[T(wk, "ewl")])
            P.op("act", lambda e: e.activation(out=ewl[:, :H], in_=ewl[:, :H], func=AF.Exp), r=[T(wk, "ewl")], w=[T(wk, "ewl")])
            hb_ = min(4, H)
            for h0 in range(0, H, hb_):
                pr, prt = next_ps()
                P.op("pe", lambda e: e.matmul(pr[:, :hb_ * 128], ones_f[:, :], X[:, h0:h0 + hb_, :].rearrange("p h c -> p (h c)"),
                                              start=True, stop=True), r=[T(wk, "X"), T("ones_f")], w=[prt])
                P.op("dve", lambda e: e.tensor_tensor(out=Dm[:, h0:h0 + hb_, :], in0=pr[:, :hb_ * 128].rearrange("p (h c) -> p h c", c=128),
                                                   in1=bc(gca[:, h0:h0 + hb_], 2, [128, hb_, 128]), op=ALU.subtract),
                     r=[prt, T(wk, "gca")], w=[T(wk, "Dm", h0), T(wk, "E", h0)])
                P.op("dve", lambda e: e.scalar_tensor_tensor(out=Dm[:, h0:h0 + hb_, :], in0=Dm[:, h0:h0 + hb_, :], scalar=0.0,
                                                          in1=bc(negm[d], 1, [128, hb_, 128]), op0=ALU.min, op1=ALU.add),
                     r=[T(wk, "Dm", h0), T("consts")], w=[T(wk, "Dm", h0)])
                P.op("act", lambda e: e.activation(out=E[:, h0:h0 + hb_, :], in_=Dm[:, h0:h0 + hb_, :], func=AF.Exp),
                     r=[T(wk, "Dm", h0)], w=[T(wk, "E", h0), T(wk, "Dm", h0)])

        def chunk_order(d):
            return list(range(18)) if d == 0 else [1, 0] + list(range(17, 1, -1))

        def ssd_branch(l, ybr):
            with ExitStack() as ph:
                al = lambda n_, s_, dt=F32: ph.enter_context(sbt(n_, list(s_), dt))
                lp = al("lp_ssd", [128, 588])
                LK = T("lp_ssd")
                P.dma("sp", lp[:, :], lp_ssd[l], w=[LK])
                xtok = al("xtok", [128, 18, 512], BF16)
                btok = al("btok", [128, 18, 128], BF16)
                bcT = al("bcT", [128, 3, NT], BF16)
                dtt = al("dtt", [128, 18, 16])
                att = al("att", [128, 18, 16])
                with ExitStack() as p2:
                    al2 = lambda n_, s_, dt=F32: p2.enter_context(sbt(n_, list(s_), dt))
                    craw = al2("craw", [128, 2, NT])
                    cvo = al2("cvo", [128, NT])
                    slab = al2("pslab", [128, 8, 256], BF16)
                    hb = al2("hbm", [128, 8, 512], BF16)
                    for pss in range(3):
                        proj_fm(w_ssd[l, :, :, pss * 256:(pss + 1) * 256], 256, craw, "craw", hb, slab, T("pslab"))
                        for cc in range(2):
                            ch = 2 * pss + cc
                            conv_fm(craw[:, cc, :], cvo, [lp[:, ch * 5 + i:ch * 5 + i + 1] for i in range(5)], lp[:, 30 + ch:31 + ch], 5,
                                    [T("craw", cc), LK], [T("cvo")])
                            P.op("act", lambda e: e.activation(out=cvo[:, :], in_=cvo[:, :], func=AF.Silu), r=[T("cvo")], w=[T("cvo")])
                            if ch < 4:
                                tr_to_tok(cvo, [T("cvo")], lambda g0, nn: xtok[:, g0:g0 + nn, ch * 128:(ch + 1) * 128], [T("xtok", ch)])
                            elif ch == 4:
                                evac(bcT[:, 0, :], cvo[:, :], r=[T("cvo")], w=[T("bcT", 0)])
                                tr_to_tok(cvo, [T("cvo")], lambda g0, nn: btok[:, g0:g0 + nn, :], [T("btok")])
                            else:
                                P.op("dve", lambda e: e.memset(bcT[:, 1:3, :], 0.0), w=[T("bcT", 1)])
                                evac(bcT[0:64, 1, :], cvo[0:64, :], r=[T("cvo")], w=[T("bcT", 1)])
                                evac(bcT[64:128, 2, :], cvo[64:128, :], r=[T("cvo")], w=[T("bcT", 1)])
                    proj_tm(w_ssd[l, :, :, 1280:1296], 16, hb, slab, T("pslab"),
                            lambda tc, ps, pt: evac(dtt[:, tc, :], ps[:, :16], r=[pt], w=[T("dtt")]))
                    P.full_barrier()
                if CUT == 1:
                    return
                oacc = al("oacc", [128, 18, 512])
                with ExitStack() as p3:
                    al3 = lambda n_, s_, dt=F32: p3.enter_context(sbt(n_, list(s_), dt))
                    tmpd = al3("tmpd", [128, 18, 16])
                    negA = al3("negA", [128, 16])
                    P.op("dve", lambda e: e.tensor_tensor(out=dtt[:, :, :], in0=dtt[:, :, :], in1=bc(lp[:, 36:52], 1, [128, 18, 16]), op=ALU.add),
                         r=[T("dtt"), LK], w=[T("dtt")])
                    softplus_inplace(dtt[:, :, :], T("dtt"), tmpd[:, :, :], T("tmpd"))
                    P.op("act", lambda e: e.activation(out=negA[:, :], in_=lp[:, 52:68], func=AF.Exp), r=[LK], w=[T("negA")])
                    P.op("dve", lambda e: e.scalar_tensor_tensor(out=att[:, :, :], in0=dtt[:, :, :], scalar=-1.0, in1=bc(negA[:, :], 1, [128, 18, 16]),
                                                              op0=ALU.mult, op1=ALU.mult), r=[T("dtt"), T("negA")], w=[T("att")])
                    P.full_barrier()
                if CUT == 2:
                    return
                with ExitStack() as p4:
                    al4 = lambda n_, s_, dt=F32: p4.enter_context(sbt(n_, list(s_), dt))
                    X = al4("dX", [128, 8, 128])
                    gca = al4("gca", [128, 16])
                    egl = al4("egl", [128, 16])
                    ewl = al4("ewl", [128, 8])
                    Dm = al4("Dm", [128, 8, 128])
                    E = Dm
                    AT = al4("AT", [128, 8, 128], BF16)
                    xdt = al4("xdt", [128, 8, 64], BF16)
                    xw = al4("xw", [128, 8, 64], BF16)
                    t1 = al4("t1", [128, 8, 64])
                    S = al4("S", [128, 8, 64])
                    Sb = al4("Sb", [128, 512], BF16)
                    for d in range(2):
                        P.op("dve", lambda e: e.memset(S[:, :, :], 0.0), w=[T("S")])
                        P.op("dve", lambda e: e.memset(Sb[:, :], 0.0), w=[T("Sb")])
                        for tc in chunk_order(d)[:KCH]:
                            tok = slice(tc * 128, (tc + 1) * 128)
                            a_c = att[:, tc, d * 8:(d + 1) * 8]
                            dt_c = dtt[:, tc, d * 8:(d + 1) * 8]
                            decay_mats(a_c, d, 8, "ssd", (X, gca, egl, ewl, Dm, E), [T("att")])
                            if KSUB == 1:
                                continue
                            psc, psct = next_ps()
                            for g in range(2):
                                P.op("pe", lambda e: e.matmul(psc[:, g * 128:(g + 1) * 128], bcT[:, 0, tok],
                                                              bcT[:, 1 + g, tok], start=True, stop=True),
                                     r=[T("bcT", 0), T("bcT", 1)], w=[psct])
                            if KSUB == 11:
                                continue
                            for g in range(2):
                                P.op("dve", lambda e: e.tensor_tensor(out=AT[:, g * 4:(g + 1) * 4, :], in0=E[:, g * 4:(g + 1) * 4, :],
                                                                   in1=bc(psc[:, g * 128:(g + 1) * 128], 1, [128, 4, 128]), op=ALU.mult),
                                     r=[T("ssd", "E", g * 4), psct], w=[T("AT", g)])
                            if KSUB == 12:
                                continue
                            P.op("dve", lambda e: e.tensor_tensor(out=xdt[:, :, :], in0=xtok[:, tc, :].rearrange("p (h q) -> p h q", q=64),
                                                               in1=bc(dt_c, 2, [128, 8, 64]), op=ALU.mult),
                                 r=[T("xtok", i) for i in range(4)] + [T("dtt")], w=[T("xdt")])
                            P.op("dve", lambda e: e.tensor_tensor(out=xw[:, :, :], in0=xdt[:, :, :], in1=bc(ewl[:, :8], 2, [128, 8, 64]), op=ALU.mult),
                                 r=[T("xdt"), T("ssd", "ewl")], w=[T("xw")])
                            if KSUB == 2:
                                continue
                            po, pot = next_ps()
                            for h in range(8):
                                P.op("pe", lambda e: e.matmul(po[:, h * 64:(h + 1) * 64], AT[:, h, :], xdt[:, h, :], start=True, stop=True),
                                     r=[T("AT", h // 4), T("xdt")], w=[pot])
                            pf, pft = next_ps()
                            for g in range(2):
                                P.op("pe", lambda e: e.matmul(pf[:, g * 256:(g + 1) * 256], bcT[:, 1 + g, tok],
                                                              Sb[:, g * 256:(g + 1) * 256], start=True, stop=True),
                                     r=[T("bcT", 1), T("Sb")], w=[pft])
                            P.op("dve", lambda e: e.tensor_tensor(out=t1[:, :, :], in0=pf[:, :].rearrange("p (h q) -> p h q", q=64),
                                                               in1=bc(egl[:, 0:8], 2, [128, 8, 64]), op=ALU.mult),
                                 r=[pft, T("ssd", "egl")], w=[T("t1")])
                            if d == 0:
                                P.op("dve", lambda e: e.tensor_tensor(out=oacc[:, tc, :], in0=po[:, :], in1=t1[:, :, :].rearrange("p h q -> p (h q)"),
                                                                   op=ALU.add), r=[pot, T("t1")], w=[T("oacc", tc)])
                            else:
                                P.op("dve", lambda e: e.tensor_tensor(out=t1[:, :, :].rearrange("p h q -> p (h q)"), in0=po[:, :],
                                                                   in1=t1[:, :, :].rearrange("p h q -> p (h q)"), op=ALU.add),
                                     r=[pot, T("t1")], w=[T("t1")])
                                P.op("dve", lambda e: e.tensor_tensor(out=oacc[:, tc, :], in0=oacc[:, tc, :],
                                                                   in1=t1[:, :, :].rearrange("p h q -> p (h q)"), op=ALU.add),
                                     r=[T("oacc", tc), T("t1")], w=[T("oacc", tc)])
                            if KSUB == 3:
                                continue
                            pS, pSt = next_ps()
                            P.op("pe", lambda e: e.matmul(pS[:, :], btok[:, tc, :], xw[:, :, :].rearrange("p h q -> p (h q)"), start=True, stop=True),
                                 r=[T("btok"), T("xw")], w=[pSt])
                            P.op("dve", lambda e: e.tensor_tensor(out=S[:, :, :], in0=S[:, :, :], in1=bc(egl[:, 8:16], 2, [128, 8, 64]), op=ALU.mult),
                                 r=[T("S"), T("ssd", "egl")], w=[T("S")])
                            P.op("dve", lambda e: e.tensor_tensor(out=S[:, :, :], in0=S[:, :, :], in1=pS[:, :].rearrange("p (h q) -> p h q", q=64),
                                                               op=ALU.add), r=[T("S"), pSt], w=[T("S")])
                            evac(Sb[:, :], S[:, :, :].rearrange("p h q -> p (h q)"), r=[T("S")], w=[T("Sb")], eng="act")
                    P.full_barrier()
                if CUT == 3:
                    return
                with ExitStack() as p5:
                    def skipfn(tc, tq):
                        P.op("dve", lambda e: e.tensor_tensor(out=tq[:, :].rearrange("p (h q) -> p h q", q=64),
                                                           in0=xtok[:, tc, :].rearrange("p (h q) -> p h q", q=64),
                                                           in1=bc(lp[:, 68:76], 2, [128, 8, 64]), op=ALU.mult),
                             r=[T("xtok", i) for i in range(4)] + [LK], w=[T("fin_t")])
                        P.op("dve", lambda e: e.tensor_tensor(out=tq[:, :], in0=tq[:, :], in1=oacc[:, tc, :], op=ALU.add),
                             r=[T("fin_t"), T("oacc", tc)], w=[T("fin_t")])
                    finish(l, w_ssd[l, :, :, 768:1280], lambda tc: (oacc[:, tc, :], [T("oacc", tc)]), ybr, True, 2, lp[:, 76:588], LK, p5, skipfn)
                    P.full_barrier()


        def gla_branch(l, ybr):
            for pair in range(2):
                gla_pair(l, ybr, pair)

        def gla_pair(l, ybr, pair):
            with ExitStack() as ph:
                al = lambda n_, s_, dt=F32: ph.enter_context(sbt(n_, list(s_), dt))
                lp = al("lp_gla", [128, 1536])
                LK = T("lp_gla")
                P.dma("sp", lp[:, :], lp_gla[l], w=[LK])
                qk = al("g_qk", [128, 18, 256], BF16)
                vt = al("g_v", [128, 18, 256], BF16)
                lrT = al("g_lrT", [128, NT])
                wp = w_gla[l, pair]
                with ExitStack() as p2:
                    al2 = lambda n_, s_, dt=F32: p2.enter_context(sbt(n_, list(s_), dt))
                    slab = al2("pslab", [128, 8, 512], BF16)
                    hb = al2("hbm", [128, 8, 512], BF16)

                    def cons(tc, ps, pt):
                        evac(qk[:, tc, :], ps[:, 0:256], r=[pt], w=[T("g_qk", tc)], eng=EVE)
                        evac(vt[:, tc, :], ps[:, 256:512], r=[pt], w=[T("g_v", tc)], eng=EVE)
                    if KSUB == 21:
                        cons = lambda tc, ps, pt: None
                    proj_tm(wp[:, :, 0:512], 512, hb, slab, T("pslab"), cons)
                    if KSUB in (21, 22):
                        P.full_barrier()
                        return
                    P.dma("pool", slab[:, :, :128], w_glr[l], w=[T("pslab")])
                    for tt in range(5):
                        t0, n = TT[tt]
                        load_h1(tt, hb)
                        ps, pt = next_ps()
                        for k in range(8):
                            P.op("pe", lambda e: e.matmul(ps[:, :n], slab[:, k, 0:128], hb[:, k, :n], start=(k == 0), stop=(k == 7)),
                                 r=[T("pslab"), T("hbm", k)], w=[pt])
                        evac(lrT[:, t0:t0 + n], ps[:, :n], r=[pt], w=[T("g_lrT")])
                    P.full_barrier()
                if CUT == 1:
                    return
                oacc = al("oacc", [128, 18, 256])
                with ExitStack() as p4:
                    al4 = lambda n_, s_, dt=F32: p4.enter_context(sbt(n_, list(s_), dt))
                    P.op("dve", lambda e: e.memset(oacc[:, :, :], 0.0), w=[T("oacc", tc) for tc in range(18)])
                    SC = 64.0 ** -0.5

                    def scan_dir(d):
                        sfx = "%d" % d
                        G_ = lambda n_: T("g" + sfx, n_)
                        al4d = lambda n_, s_, dt=F32: al4(n_ + sfx, s_, dt)
                        sp_ = al4d("g_sp", [128, 128])
                        glsb = al4d("g_gl", [128, 128])
                        e1 = al4d("g_e1", [128, 128])
                        e2 = al4d("g_e2", [128, 128])
                        e3 = al4d("g_e3", [128, 128])
                        qd = al4d("g_qd", [128, 128])
                        qr = al4d("g_qr", [128, 128])
                        kdf = al4d("g_kdf", [128, 128])
                        kdb = al4d("g_kdb", [128, 128], BF16)
                        kdTz = al4d("g_kdTz", [128, 2, 128], BF16)
                        qdTz = al4d("g_qdTz", [128, 2, 128], BF16)
                        qrT = al4d("g_qrT", [128, 128], BF16)
                        AT = al4d("g_AT", [128, 2, 128], BF16)
                        t1 = al4d("g_t1", [128, 256])
                        dl = al4d("g_dl", [128, 1])
                        S = al4d("g_S", [128, 256])
                        Sb = al4d("g_Sb", [128, 256], BF16)
                        P.op("dve", lambda e: e.memset(S[:, :], 0.0), w=[G_("S")])
                        P.op("dve", lambda e: e.memset(Sb[:, :], 0.0), w=[G_("Sb")])
                        P.op("dve", lambda e: e.memset(kdTz[:, :, :], 0.0), w=[G_("kdTz")])
                        P.op("dve", lambda e: e.memset(qdTz[:, :, :], 0.0), w=[G_("qdTz")])
                        gkw = lp[:, 1024 + d * 256 + pair * 128:1024 + d * 256 + (pair + 1) * 128]
                        gkb = lp[:, d * 256 + pair * 128:d * 256 + (pair + 1) * 128]
                        for tc in chunk_order(d)[:KCH]:
                            tok = slice(tc * 128, (tc + 1) * 128)
                            pgk, pgkt = next_ps()
                            P.op("pe", lambda e: e.matmul(pgk[:, :128], lrT[:, tok], gkw, start=True, stop=True), r=[G_("lrT"), LK], w=[pgkt])
                            P.op("dve", lambda e: e.tensor_tensor(out=sp_[:, :], in0=pgk[:, :128], in1=gkb, op=ALU.add), r=[pgkt, LK], w=[G_("sp")])
                            P.op("act", lambda e: e.activation(out=sp_[:, :], in_=sp_[:, :], func=AF.Exp, scale=-1.0), r=[G_("sp")], w=[G_("sp")])
                            P.op("act", lambda e: e.activation(out=sp_[:, :], in_=sp_[:, :], func=AF.Ln, bias=one_col[:, 0:1], scale=1.0),
                                 r=[G_("sp"), T("one_col")], w=[G_("sp")])
                            pc, pct = next_ps()
                            P.op("pe", lambda e: e.matmul(pc[:, 0:128], tri[d], sp_[:, :], start=True, stop=True), r=[G_("sp"), T("consts")], w=[pct])
                            P.op("pe", lambda e: e.matmul(pc[:, 128:256], ones_f[:, :], sp_[:, :], start=True, stop=True), r=[G_("sp"), T("ones_f")], w=[pct])
                            P.op("pe", lambda e: e.matmul(pc[:, 256:257], sp_[:, :], ones_f[:, 0:1], start=True, stop=True),
                                 r=[G_("sp"), T("ones_f")], w=[pct])
                            P.op("act", lambda e: e.activation(out=dl[:, :], in_=pc[:, 256:257], func=AF.Exp, scale=-1.0 / 16), r=[pct], w=[G_("dl")])
                            P.op("act", lambda e: e.activation(out=e1[:, :], in_=pc[:, 0:128], func=AF.Exp, scale=-1.0 / 16), r=[pct], w=[G_("e1")])
                            evac(glsb[:, :], pc[:, 128:256], r=[pct], w=[G_("gl")], eng="act")
                            P.op("dve", lambda e: e.tensor_tensor(out=glsb[:, :], in0=pc[:, 0:128], in1=glsb[:, :], op=ALU.subtract),
                                 r=[pct, G_("gl")], w=[G_("gl")])
                            P.op("act", lambda e: e.activation(out=e2[:, :], in_=glsb[:, :], func=AF.Exp, scale=1.0 / 16), r=[G_("gl")], w=[G_("e2")])
                            P.op("act", lambda e: e.activation(out=e3[:, :], in_=glsb[:, :], func=AF.Exp, scale=-1.0 / 16), r=[G_("gl")], w=[G_("e3")])
                            yield
                            P.op("dve", lambda e: e.scalar_tensor_tensor(out=qd[:, :], in0=qk[:, tc, 0:128], scalar=SC, in1=e1[:, :], op0=ALU.mult, op1=ALU.mult),
                                 r=[T("g_qk", tc), G_("e1")], w=[G_("qd")])
                            P.op("dve", lambda e: e.scalar_tensor_tensor(out=qr[:, :], in0=qk[:, tc, 0:128], scalar=SC, in1=e3[:, :], op0=ALU.mult, op1=ALU.mult),
                                 r=[T("g_qk", tc), G_("e3")], w=[G_("qr")])
                            P.op("dve", lambda e: e.tensor_tensor(out=kdf[:, :], in0=qk[:, tc, 128:256], in1=e2[:, :], op=ALU.mult),
                                 r=[T("g_qk", tc), G_("e2")], w=[G_("kdf")])
                            evac(kdb[:, :], kdf[:, :], r=[G_("kdf")], w=[G_("kdb")], eng="act")
                            ptr, ptrt = next_ps()
                            P.op("pe", lambda e: e.transpose(out=ptr[:, 0:128], in_=kdf[:, :], identity=ident), r=[G_("kdf"), T("consts")], w=[ptrt])
                            P.op("pe", lambda e: e.transpose(out=ptr[:, 128:256], in_=qd[:, :], identity=ident), r=[G_("qd"), T("consts")], w=[ptrt])
                            P.op("pe", lambda e: e.transpose(out=ptr[:, 256:384], in_=qr[:, :], identity=ident), r=[G_("qr"), T("consts")], w=[ptrt])
                            for hh in range(2):
                                rows = slice(hh * 64, (hh + 1) * 64)
                                evac(kdTz[rows, hh, :], ptr[rows, 0:128], r=[ptrt], w=[G_("kdTz")])
                                evac(qdTz[rows, hh, :], ptr[rows, 128:256], r=[ptrt], w=[G_("qdTz")])
                            evac(qrT[:, :], ptr[:, 256:384], r=[ptrt], w=[G_("qrT")])
                            yield
                            psc, psct = next_ps()
                            for hh in range(2):
                                P.op("pe", lambda e: e.matmul(psc[:, hh * 128:(hh + 1) * 128], kdTz[:, hh, :], qrT[:, :], start=True, stop=True),
                                     r=[G_("kdTz"), G_("qrT")], w=[psct])
                            P.op("dve", lambda e: e.tensor_tensor(out=AT[:, :, :], in0=psc[:, 0:256].rearrange("p (h c) -> p h c", c=128),
                                                               in1=bc(msk[d], 1, [128, 2, 128]), op=ALU.mult), r=[psct, T("consts")], w=[G_("AT")])
                            po, pot = next_ps()
                            for hh in range(2):
                                P.op("pe", lambda e: e.matmul(po[:, hh * 128:(hh + 1) * 128], AT[:, hh, :], vt[:, tc, hh * 128:(hh + 1) * 128], start=True, stop=True),
                                     r=[G_("AT"), T("g_v", tc)], w=[pot])
                            pf, pft = next_ps()
                            for hh in range(2):
                                P.op("pe", lambda e: e.matmul(pf[:, hh * 128:(hh + 1) * 128], qdTz[:, hh, :], Sb[:, hh * 128:(hh + 1) * 128],
                                                              start=True, stop=True), r=[G_("qdTz"), G_("Sb")], w=[pft])
                            evac(t1[:, :], pf[:, 0:256], r=[pft], w=[G_("t1")], eng="act")
                            P.op("dve", lambda e: e.tensor_tensor(out=t1[:, :], in0=po[:, 0:256], in1=t1[:, :], op=ALU.add), r=[pot, G_("t1")], w=[G_("t1")])
                            P.op("dve", lambda e: e.tensor_tensor(out=oacc[:, tc, :], in0=oacc[:, tc, :], in1=t1[:, :], op=ALU.add),
                                 r=[T("oacc", tc), G_("t1")], w=[T("oacc", tc)])
                            yield
                            pS, pSt = next_ps()
                            P.op("pe", lambda e: e.matmul(pS[:, :256], kdb[:, :], vt[:, tc, :], start=True, stop=True),
                                 r=[G_("kdb"), T("g_v", tc)], w=[pSt])
                            P.op("dve", lambda e: e.scalar_tensor_tensor(out=S[:, :], in0=S[:, :], scalar=dl[:, 0:1], in1=pS[:, :256],
                                                                      op0=ALU.mult, op1=ALU.add), r=[G_("S"), G_("dl"), pSt], w=[G_("S")])
                            evac(Sb[:, :], S[:, :], r=[G_("S")], w=[G_("Sb")], eng="act")
                            yield

                    gens = [scan_dir(0), scan_dir(1)]
                    while gens:
                        for g_ in list(gens):
                            try:
                                next(g_)
                            except StopIteration:
                                gens.remove(g_)
                    P.full_barrier()
                if CUT == 3:
                    return
                with ExitStack() as p5:
                    finish(l, wp[:, :, 512:768], lambda tc: (oacc[:, tc, :], [T("oacc", tc)]), ybr, False, 2, lp[:, 512:768], LK, p5,
                           W=256, ch0=pair * 2)
                    P.full_barrier()

        def gdn_branch(l, ybr):
            with ExitStack() as ph:
                al = lambda n_, s_, dt=F32: ph.enter_context(sbt(n_, list(s_), dt))
                lp = al("lp_gdn", [128, 588])
                LK = T("lp_gdn")
                P.dma("sp", lp[:, :], lp_gdn[l], w=[LK])
                beta = al("d_beta", [128, 18, 8])
                nbeta = al("d_nbeta", [128, 18, 8])
                gg = al("d_gg", [128, 18, 8])
                with ExitStack() as p2:
                    al2 = lambda n_, s_, dt=F32: p2.enter_context(sbt(n_, list(s_), dt))
                    slab = al2("pslab", [128, 8, 16], BF16)
                    hb = al2("hbm", [128, 8, 512], BF16)
                    tmpd = al2("tmpd", [128, 18, 8])
                    negA = al2("negA", [128, 8])

                    def cons(tc, ps, pt):
                        evac(beta[:, tc, :], ps[:, 0:8], r=[pt], w=[T("d_beta")], eng="act")
                        evac(gg[:, tc, :], ps[:, 8:16], r=[pt], w=[T("d_gg")], eng="act")
                    proj_tm(w_gbd[l], 16, hb, slab, T("pslab"), cons)
                    P.op("act", lambda e: e.activation(out=beta[:, :, :], in_=beta[:, :, :], func=AF.Sigmoid), r=[T("d_beta")], w=[T("d_beta")])
                    P.op("dve", lambda e: e.tensor_scalar(out=nbeta[:, :, :], in0=beta[:, :, :], scalar1=-1.0, scalar2=None, op0=ALU.mult),
                         r=[T("d_beta")], w=[T("d_nbeta")])
                    P.op("dve", lambda e: e.tensor_tensor(out=gg[:, :, :], in0=gg[:, :, :], in1=bc(lp[:, 68:76], 1, [128, 18, 8]), op=ALU.add),
                         r=[T("d_gg"), LK], w=[T("d_gg")])
                    softplus_inplace(gg[:, :, :], T("d_gg"), tmpd[:, :, :], T("tmpd"))
                    P.op("act", lambda e: e.activation(out=negA[:, :], in_=lp[:, 60:68], func=AF.Exp), r=[LK], w=[T("negA")])
                    P.op("dve", lambda e: e.scalar_tensor_tensor(out=gg[:, :, :], in0=gg[:, :, :], scalar=-1.0, in1=bc(negA[:, :], 1, [128, 18, 8]),
                                                              op0=ALU.mult, op1=ALU.mult), r=[T("d_gg"), T("negA")], w=[T("d_gg")])
                    P.full_barrier()
                for h in range(4):
                    gdn_head(l, ybr, h, lp, LK, beta, nbeta, gg)

        def gdn_head(l, ybr, h, lp, LK, beta, nbeta, gg):
            with ExitStack() as ph:
                al = lambda n_, s_, dt=F32: ph.enter_context(sbt(n_, list(s_), dt))
                QT = al("d_QT", [128, NT], BF16)
                KT = al("d_KT", [128, NT], BF16)
                Ktok = al("d_Ktok", [128, 18, 128], BF16)
                Vtok = al("d_Vtok", [128, 18, 128])
                wh = w_gdn[l, h]
                with ExitStack() as p2:
                    al2 = lambda n_, s_, dt=F32: p2.enter_context(sbt(n_, list(s_), dt))
                    craw = al2("craw", [128, 3, NT])
                    cvo = al2("cvo", [128, NT])
                    slab = al2("pslab", [128, 8, 384], BF16)
                    hb = al2("hbm", [128, 8, 512], BF16)
                    proj_fm(wh[:, :, 0:384], 384, craw, "craw", hb, slab, T("pslab"))
                    for cc in range(3):
                        ch = cc * 4 + h
                        conv_fm(craw[:, cc, :], cvo, [lp[:, ch * 5 + i:ch * 5 + i + 1] for i in range(5)], None, 5, [T("craw", cc), LK], [T("cvo")])
                        P.op("act", lambda e: e.activation(out=cvo[:, :], in_=cvo[:, :], func=AF.Silu), r=[T("cvo")], w=[T("cvo")])
                        if cc == 2:
                            tr_to_tok(cvo, [T("cvo")], lambda g0, nn: Vtok[:, g0:g0 + nn, :], [T("d_Vtok")])
                            continue
                        for tt in range(5):
                            t0, n = TT[tt]
                            P.op("dve", lambda e: e.tensor_tensor(out=tmpn[0][:, :n], in0=cvo[:, t0:t0 + n], in1=cvo[:, t0:t0 + n], op=ALU.mult),
                                 r=[T("cvo")], w=[T("tmpn", 0)])
                            ps, pt = next_ps()
                            P.op("pe", lambda e: e.matmul(ps[:, :n], ones_f[:, :], tmpn[0][:, :n], start=True, stop=True),
                                 r=[T("tmpn", 0), T("ones_f")], w=[pt])
                            P.op("act", lambda e: e.activation(out=rstd[:, :n], in_=ps[:, :n], func=AF.Ln, bias=epsb[:, 0:1], scale=1.0),
                                 r=[pt, T("epsb")], w=[T("rstd")])
                            P.op("act", lambda e: e.activation(out=rstd[:, :n], in_=rstd[:, :n], func=AF.Exp, scale=-0.5), r=[T("rstd")], w=[T("rstd")])
                            if cc == 0:
                                P.op("dve", lambda e: e.scalar_tensor_tensor(out=QT[:, t0:t0 + n], in0=cvo[:, t0:t0 + n], scalar=128.0 ** -0.5,
                                                                          in1=rstd[:, :n], op0=ALU.mult, op1=ALU.mult),
                                     r=[T("cvo"), T("rstd")], w=[T("d_QT")])
                            else:
                                P.op("dve", lambda e: e.tensor_tensor(out=cvo[:, t0:t0 + n], in0=cvo[:, t0:t0 + n], in1=rstd[:, :n], op=ALU.mult),
                                     r=[T("cvo"), T("rstd")], w=[T("cvo")])
                        if cc == 1:
                            evac(KT[:, :], cvo[:, :], r=[T("cvo")], w=[T("d_KT")])
                            tr_to_tok(cvo, [T("cvo")], lambda g0, nn: Ktok[:, g0:g0 + nn, :], [T("d_Ktok")])
                    P.full_barrier()
                oacc = al("oacc", [128, 18, 128])
                with ExitStack() as p4:
                    al4 = lambda n_, s_, dt=F32: p4.enter_context(sbt(n_, list(s_), dt))
                    P.op("dve", lambda e: e.memset(oacc[:, :, :], 0.0), w=[T("oacc", tc) for tc in range(18)])

                    def scan_dir(d):
                        sfx = "%d" % d
                        K_ = lambda n_: T("d" + sfx, n_)
                        wk = "gdn" + sfx
                        X = al4("dX" + sfx, [128, 1, 128])
                        gca = al4("gca" + sfx, [128, 2])
                        egl = al4("egl" + sfx, [128, 2])
                        ewl = al4("ewl" + sfx, [128, 1])
                        neg = al4("d_neg" + sfx, [128, 1])
                        Dm = al4("Dm" + sfx, [128, 1, 128])
                        E = Dm
                        tmpA = al4("d_tmpA" + sfx, [128, 128])
                        MT = al4("d_MT" + sfx, [128, 128])
                        Ab = [al4("d_A%d" % i + sfx, [128, 128]) for i in range(2)]
                        ATb = [al4("d_AT%d" % i + sfx, [128, 128]) for i in range(2)]
                        TTm = al4("d_TT" + sfx, [128, 128])
                        Aqk = al4("d_Aqk" + sfx, [128, 128], BF16)
                        r0 = al4("d_r0" + sfx, [128, 128])
                        vn = al4("d_vn" + sfx, [128, 128], BF16)
                        kgt = al4("d_kgt" + sfx, [128, 128], BF16)
                        t1 = al4("d_t1" + sfx, [128, 128])
                        S = al4("d_S" + sfx, [128, 128])
                        Sb = al4("d_Sb" + sfx, [128, 128], BF16)
                        col = d * 4 + h
                        P.op("dve", lambda e: e.memset(S[:, :], 0.0), w=[K_("S")])
                        P.op("dve", lambda e: e.memset(Sb[:, :], 0.0), w=[K_("Sb")])
                        for tc in chunk_order(d)[:KCH]:
                            tok = slice(tc * 128, (tc + 1) * 128)
                            a_c = gg[:, tc, col:col + 1]
                            decay_mats(a_c, d, 1, wk, (X, gca, egl, ewl, Dm, E), [T("d_gg")])
                            yield
                            pkk, pkkt = next_ps()
                            P.op("pe", lambda e: e.matmul(pkk[:, 0:128], KT[:, tok], KT[:, tok], start=True, stop=True), r=[T("d_KT")], w=[pkkt])
                            P.op("pe", lambda e: e.matmul(pkk[:, 128:256], KT[:, tok], QT[:, tok], start=True, stop=True), r=[T("d_KT"), T("d_QT")], w=[pkkt])
                            P.op("dve", lambda e: e.tensor_tensor(out=tmpA[:, :], in0=pkk[:, 0:128], in1=E[:, 0, :], op=ALU.mult),
                                 r=[pkkt, T(wk, "E", 0)], w=[K_("tmpA")])
                            P.op("dve", lambda e: e.scalar_tensor_tensor(out=MT[:, :], in0=tmpA[:, :], scalar=nbeta[:, tc, col:col + 1], in1=smsk[d],
                                                                      op0=ALU.mult, op1=ALU.mult), r=[K_("tmpA"), T("d_nbeta"), T("consts")], w=[K_("MT")])
                            P.op("dve", lambda e: e.tensor_tensor(out=Aqk[:, :], in0=pkk[:, 128:256], in1=E[:, 0, :], op=ALU.mult),
                                 r=[pkkt, T(wk, "E", 0)], w=[K_("Aqk")])
                            yield
                            pA, pAt = next_ps()
                            P.op("pe", lambda e: e.transpose(out=pA[:, 0:128], in_=MT[:, :], identity=ident), r=[K_("MT"), T("consts")], w=[pAt])
                            evac(Ab[0][:, :], pA[:, 0:128], r=[pAt], w=[K_("A0")])
                            P.op("dve", lambda e: e.tensor_tensor(out=TTm[:, :], in0=MT[:, :], in1=ident, op=ALU.add), r=[K_("MT"), T("consts")], w=[K_("TT")])
                            yield
                            Ap, ATp, Apk, ATpk = Ab[0], MT, K_("A0"), K_("MT")
                            for j in range(1, 7):
                                An, ATn = Ab[j % 2], ATb[j % 2]
                                Ank, ATnk = K_("A%d" % (j % 2)), K_("AT%d" % (j % 2))
                                p1, p1t = next_ps()
                                P.op("pe", lambda e: e.matmul(p1[:, 0:128], ATp[:, :], Ap[:, :], start=True, stop=True), r=[ATpk, Apk], w=[p1t])
                                if j < 6:
                                    P.op("pe", lambda e: e.matmul(p1[:, 128:256], Ap[:, :], ATp[:, :], start=True, stop=True), r=[ATpk, Apk], w=[p1t])
                                evac(An[:, :], p1[:, 0:128], r=[p1t], w=[Ank], eng="act")
                                if j < 6:
                                    evac(ATn[:, :], p1[:, 128:256], r=[p1t], w=[ATnk], eng="act")
                                yield
                                p2_, p2t = next_ps()
                                P.op("pe", lambda e: e.matmul(p2_[:, 0:128], An[:, :], TTm[:, :], start=True, stop=True), r=[Ank, K_("TT")], w=[p2t])
                                P.op("dve", lambda e: e.tensor_tensor(out=TTm[:, :], in0=TTm[:, :], in1=p2_[:, 0:128], op=ALU.add),
                                     r=[K_("TT"), p2t], w=[K_("TT")])
                                Ap, ATp, Apk, ATpk = An, ATn, Ank, ATnk
                                yield
                            pks, pkst = next_ps()
                            P.op("pe", lambda e: e.matmul(pks[:, 0:128], KT[:, tok], Sb[:, :], start=True, stop=True), r=[T("d_KT"), K_("Sb")], w=[pkst])
                            P.op("pe", lambda e: e.matmul(pks[:, 128:256], QT[:, tok], Sb[:, :], start=True, stop=True), r=[T("d_QT"), K_("Sb")], w=[pkst])
                            P.op("dve", lambda e: e.tensor_scalar(out=neg[:, :], in0=egl[:, 0:1], scalar1=-1.0, scalar2=None, op0=ALU.mult),
                                 r=[T(wk, "egl")], w=[K_("neg")])
                            P.op("dve", lambda e: e.scalar_tensor_tensor(out=r0[:, :], in0=pks[:, 0:128], scalar=neg[:, 0:1], in1=Vtok[:, tc, :],
                                                                      op0=ALU.mult, op1=ALU.add), r=[pkst, K_("neg"), T("d_Vtok")], w=[K_("r0")])
                            P.op("dve", lambda e: e.tensor_scalar(out=t1[:, :], in0=pks[:, 128:256], scalar1=egl[:, 0:1], scalar2=None, op0=ALU.mult),
                                 r=[pkst, T(wk, "egl")], w=[K_("t1")])
                            yield
                            pX, pXt = next_ps()
                            P.op("pe", lambda e: e.matmul(pX[:, 0:128], TTm[:, :], r0[:, :], start=True, stop=True), r=[K_("TT"), K_("r0")], w=[pXt])
                            P.op("dve", lambda e: e.tensor_scalar(out=vn[:, :], in0=pX[:, 0:128], scalar1=beta[:, tc, col:col + 1], scalar2=None, op0=ALU.mult),
                                 r=[pXt, T("d_beta")], w=[K_("vn")])
                            P.op("dve", lambda e: e.tensor_scalar(out=kgt[:, :], in0=Ktok[:, tc, :], scalar1=ewl[:, 0:1], scalar2=None, op0=ALU.mult),
                                 r=[T("d_Ktok"), T(wk, "ewl")], w=[K_("kgt")])
                            yield
                            po, pot = next_ps()
                            P.op("pe", lambda e: e.matmul(po[:, 0:128], Aqk[:, :], vn[:, :], start=True, stop=True), r=[K_("Aqk"), K_("vn")], w=[pot])
                            pS, pSt = next_ps()
                            P.op("pe", lambda e: e.matmul(pS[:, 0:128], kgt[:, :], vn[:, :], start=True, stop=True), r=[K_("kgt"), K_("vn")], w=[pSt])
                            P.op("dve", lambda e: e.tensor_tensor(out=t1[:, :], in0=po[:, 0:128], in1=t1[:, :], op=ALU.add), r=[pot, K_("t1")], w=[K_("t1")])
                            P.op("dve", lambda e: e.tensor_tensor(out=oacc[:, tc, :], in0=oacc[:, tc, :], in1=t1[:, :], op=ALU.add),
                                 r=[T("oacc", tc), K_("t1")], w=[T("oacc", tc)])
                            P.op("dve", lambda e: e.scalar_tensor_tensor(out=S[:, :], in0=S[:, :], scalar=egl[:, 1:2], in1=pS[:, 0:128],
                                                                      op0=ALU.mult, op1=ALU.add), r=[K_("S"), T(wk, "egl"), pSt], w=[K_("S")])
                            evac(Sb[:, :], S[:, :], r=[K_("S")], w=[K_("Sb")], eng="act")
                            yield

                    gens = [scan_dir(0), scan_dir(1)]
                    while gens:
                        for g_ in list(gens):
                            try:
                                next(g_)
                            except StopIteration:
                                gens.remove(g_)
                    P.full_barrier()
                if CUT == 3:
                    return
                with ExitStack() as p5:
                    finish(l, wh[:, :, 384:512], lambda tc: (oacc[:, tc, :], [T("oacc", tc)]), ybr, False, 1, lp[:, 76:204], LK, p5,
                           W=128, ch0=h)
                    P.full_barrier()

        TWO_PI = 2.0 * np.pi

        def hyena_branch(l, ybr):
            with ExitStack() as ph:
                al = lambda n_, s_, dt=F32: ph.enter_context(sbt(n_, list(s_), dt))
                lp = al("lp_hy", [128, 436])
                LK = T("lp_hy")
                P.dma("sp", lp[:, :], lp_hy[l], w=[LK])
                h3 = {2048: al("h3l", [128, 2048], BF16), 256: al("h3c", [128, 256], BF16)}
                negpi = al("negpi", [128, 1])
                P.op("dve", lambda e: e.memset(negpi[:, :], -float(np.pi)), w=[T("negpi")])
                with ExitStack() as p1:
                    al1 = lambda n_, s_, dt=F32: p1.enter_context(sbt(n_, list(s_), dt))
                    zT = al1("hy_zT", [128, 2048])
                    hA = al1("hy_hA", [128, 2048])
                    hB = al1("hy_hB", [128, 2048])
                    zr = al1("hy_zr", [128, 512])
                    for L, zsrc in ((2048, hy_zl), (256, hy_zc)):
                        P.dma("sp", zT[:, :L], zsrc[:, :], w=[T("hy_zT")])
                        cur, curk = zT, T("hy_zT")
                        for li in range(3):
                            dst, dstk = (hA, T("hy_hA")) if li % 2 == 0 else (hB, T("hy_hB"))
                            wap = lp[:, li * 128:(li + 1) * 128]
                            bcol = lp[:, 384 + li:385 + li]
                            fcol = lp[:, 387:388]
                            for c0 in range(0, L, 512):
                                n = min(512, L - c0)
                                ps, pt = next_ps()
                                P.op("pe", lambda e: e.matmul(ps[:, :n], wap, cur[:, c0:c0 + n], start=True, stop=True), r=[LK, curk], w=[pt])
                                P.op("dve", lambda e: e.tensor_scalar(out=dst[:, c0:c0 + n], in0=ps[:, :n], scalar1=bcol, scalar2=fcol,
                                                                   op0=ALU.add, op1=ALU.mult), r=[pt, LK], w=[dstk])
                                MAGIC = 12582912.0
                                P.op("dve", lambda e: e.tensor_scalar(out=zr[:, :n], in0=dst[:, c0:c0 + n], scalar1=float(1.0 / TWO_PI), scalar2=MAGIC,
                                                                   op0=ALU.mult, op1=ALU.add), r=[dstk], w=[T("hy_zr")])
                                P.op("dve", lambda e: e.tensor_scalar(out=zr[:, :n], in0=zr[:, :n], scalar1=-MAGIC, scalar2=None, op0=ALU.add),
                                     r=[T("hy_zr")], w=[T("hy_zr")])
                                P.op("dve", lambda e: e.scalar_tensor_tensor(out=dst[:, c0:c0 + n], in0=zr[:, :n], scalar=float(-TWO_PI), in1=dst[:, c0:c0 + n],
                                                                          op0=ALU.mult, op1=ALU.add), r=[T("hy_zr"), dstk], w=[dstk])
                                P.op("dve", lambda e: e.tensor_scalar(out=dst[:, c0:c0 + n], in0=dst[:, c0:c0 + n], scalar1=3.1415925, scalar2=-3.1415925,
                                                                   op0=ALU.min, op1=ALU.max), r=[dstk], w=[dstk])
                                if li < 2:
                                    P.op("act", lambda e: e.activation(out=dst[:, c0:c0 + n], in_=dst[:, c0:c0 + n], func=AF.Sin),
                                         r=[dstk], w=[dstk])
                                else:
                                    P.op("act", lambda e: e.activation(out=h3[L][:, c0:c0 + n], in_=dst[:, c0:c0 + n], func=AF.Sin),
                                         r=[dstk], w=[T("h3", L)])
                            cur, curk = dst, dstk
                    P.full_barrier()
                for half in range(2):
                    hyena_half(l, ybr, half, lp, LK, h3)

        def hyena_half(l, ybr, half, lp, LK, h3):
            with ExitStack() as ph:
                al = lambda n_, s_, dt=F32: ph.enter_context(sbt(n_, list(s_), dt))
                yb = al("hy_y", [128, 18, 256], BF16)
                uu = [None, al("hy_u1", [128, 18, 256], BF16), al("hy_u2", [128, 18, 256], BF16)]
                ukeys = [T("hy_y"), T("hy_u1"), T("hy_u2")]
                ubufs = [yb, uu[1], uu[2]]
                with ExitStack() as p2:
                    al2 = lambda n_, s_, dt=F32: p2.enter_context(sbt(n_, list(s_), dt))
                    craw = al2("craw", [128, 3, NT])
                    cvo = al2("cvo", [128, NT])
                    slab = al2("pslab", [128, 8, 384], BF16)
                    hb = al2("hbm", [128, 8, 512], BF16)
                    for pss in range(2):
                        P.dma("pool", slab[:, :, :], w_hy[l, half, :, :, pss * 384:(pss + 1) * 384], w=[T("pslab")])
                        for tt in range(5):
                            t0, n = TT[tt]
                            load_h1(tt, hb)
                            for c3 in range(3):
                                ps, pt = next_ps()
                                for k in range(8):
                                    P.op("pe", lambda e: e.matmul(ps[:, :n], slab[:, k, c3 * 128:(c3 + 1) * 128], hb[:, k, :n],
                                                                  start=(k == 0), stop=(k == 7)), r=[T("pslab"), T("hbm", k)], w=[pt])
                                if half == 1 and tt > 0:
                                    r0_ = (t0 - CTX) // 64
                                    dstv = craw[:, c3, CTX:NT].rearrange("p (c r) -> p r c", r=32)[:, r0_:r0_ + 8, :]
                                    evac(dstv, ps[:, :512].rearrange("p (r c) -> p r c", c=64), r=[pt], w=[T("craw", c3)])
                                else:
                                    evac(craw[:, c3, t0:t0 + n], ps[:, :n], r=[pt], w=[T("craw", c3)])
                        for c3 in range(3):
                            cc = pss * 3 + c3
                            grp, sub = cc // 2, cc % 2
                            ch = grp * 4 + half * 2 + sub
                            conv_fm(craw[:, c3, :], cvo, [lp[:, 388 + ch * 3 + i:388 + ch * 3 + i + 1] for i in range(3)], lp[:, 424 + ch:425 + ch], 3,
                                    [T("craw", c3), LK], [T("cvo")])
                            tr_to_tok(cvo, [T("cvo")], lambda g0, nn: ubufs[grp][:, g0:g0 + nn, sub * 128:(sub + 1) * 128], [ukeys[grp]])
                    P.full_barrier()
                for (L, c0, nN, nK, fc, fs, ic, isn, wn, wn0) in ((2048, 2, 16, 17, hy_fc_l, hy_fs_l, hy_ic_l, hy_is_l, hy_win_l, hy_win0_l),
                                                                (256, 0, 2, 3, hy_fc_c, hy_fs_c, hy_ic_c, hy_is_c, hy_win_c, hy_win0_c)):
                    with ExitStack() as p3:
                        al3 = lambda n_, s_, dt=F32: p3.enter_context(sbt(n_, list(s_), dt))
                        Fp = al3("hy_Fp", [128, nN, 256], BF16)
                        Fm = al3("hy_Fm", [128, nN, 256], BF16)
                        Zc = al3("hy_Zc", [128, nK, 256], BF16)
                        Zs = al3("hy_Zs", [128, nK, 256], BF16)
                        wo = al3("hy_wo", [128, 512], BF16)
                        bia = al3("hy_bias", [128, 256])
                        win = al3("hy_win", [128, 256])
                        winb = al3("hy_winb", [128, 256])
                        cs = al3("hy_cs", [128, nK, 128], BF16)
                        sn = al3("hy_sn", [128, nK, 128], BF16)
                        gsb = al3("hy_gsb", [128, 512])
                        ta = al3("hy_ta", [128, 256])
                        tb = al3("hy_tb", [128, 256])
                        yn = al3("hy_yn", [128, 256])
                        for order in range(2):
                            P.dma("pool", wo[:, 0:256], hy_wout[l, :, order * 1024 + half * 256:order * 1024 + half * 256 + 256], w=[T("hy_wo")])
                            P.dma("pool", wo[:, 256:512], hy_wout[l, :, order * 1024 + 512 + half * 256:order * 1024 + 512 + half * 256 + 256], w=[T("hy_wo")])
                            P.dma("sp", bia[:, :], hy_biasr[l, :, order * 512 + half * 256:order * 512 + half * 256 + 256], w=[T("hy_bias")])
                            for n_ in range(nN):
                                P.dma("sp", win[:, :], wn[n_, :, half * 256:(half + 1) * 256], w=[T("hy_win")])
                                if n_ == 0:
                                    P.dma("sp", winb[:, :], wn0[:, half * 256:(half + 1) * 256], w=[T("hy_winb")])
                                ps, pt = next_ps()
                                P.op("pe", lambda e: e.matmul(ps[:, :], h3[L][:, n_ * 128:(n_ + 1) * 128], wo[:, :], start=True, stop=True),
                                     r=[T("h3", L), T("hy_wo")], w=[pt])
                                P.op("dve", lambda e: e.tensor_tensor(out=ta[:, :], in0=ps[:, 0:256], in1=win[:, :], op=ALU.mult), r=[pt, T("hy_win")], w=[T("hy_ta")])
                                if n_ == 0:
                                    P.op("dve", lambda e: e.tensor_tensor(out=tb[:, :], in0=ps[:, 256:512], in1=winb[:, :], op=ALU.mult),
                                         r=[pt, T("hy_winb")], w=[T("hy_tb")])
                                else:
                                    P.op("dve", lambda e: e.tensor_tensor(out=tb[:, :], in0=ps[:, 256:512], in1=win[:, :], op=ALU.mult),
                                         r=[pt, T("hy_win")], w=[T("hy_tb")])
                                P.op("dve", lambda e: e.tensor_tensor(out=Fp[:, n_, :], in0=ta[:, :], in1=tb[:, :], op=ALU.add), r=[T("hy_ta"), T("hy_tb")], w=[T("hy_Fp")])
                                P.op("dve", lambda e: e.tensor_tensor(out=Fm[:, n_, :], in0=ta[:, :], in1=tb[:, :], op=ALU.subtract), r=[T("hy_ta"), T("hy_tb")], w=[T("hy_Fm")])
                            for kc in range(nK):
                                P.dma("sp", cs[:, :nN, :], fc[kc], w=[T("hy_cs")])
                                P.dma("sp", sn[:, :nN, :], fs[kc], w=[T("hy_sn")])
                                py, pyt = next_ps()
                                pg, pgt = next_ps()
                                for (pp, ppt, c_, mat, mk, rhs_, rk_) in ((py, pyt, 0, cs, "hy_cs", yb, T("hy_y")), (py, pyt, 256, sn, "hy_sn", yb, T("hy_y")),
                                                                         (pg, pgt, 0, cs, "hy_cs", Fp, T("hy_Fp")), (pg, pgt, 256, sn, "hy_sn", Fm, T("hy_Fm"))):
                                    for n_ in range(nN):
                                        rr = rhs_[:, c0 + n_, :] if rhs_ is yb else rhs_[:, n_, :]
                                        P.op("pe", lambda e: e.matmul(pp[:, c_:c_ + 256], mat[:, n_, :], rr, start=(n_ == 0), stop=(n_ == nN - 1)),
                                             r=[T(mk), rk_], w=[ppt])
                                evac(gsb[:, :], pg[:, :], r=[pgt], w=[T("hy_gsb")], eng="act")
                                P.op("dve", lambda e: e.tensor_tensor(out=ta[:, :], in0=py[:, 0:256], in1=gsb[:, 0:256], op=ALU.mult), r=[pyt, T("hy_gsb")], w=[T("hy_ta")])
                                P.op("dve", lambda e: e.tensor_tensor(out=tb[:, :], in0=py[:, 256:512], in1=gsb[:, 256:512], op=ALU.mult), r=[pyt, T("hy_gsb")], w=[T("hy_tb")])
                                P.op("dve", lambda e: e.tensor_tensor(out=Zc[:, kc, :], in0=ta[:, :], in1=tb[:, :], op=ALU.subtract), r=[T("hy_ta"), T("hy_tb")], w=[T("hy_Zc")])
                                P.op("dve", lambda e: e.tensor_tensor(out=ta[:, :], in0=py[:, 0:256], in1=gsb[:, 256:512], op=ALU.mult), r=[pyt, T("hy_gsb")], w=[T("hy_ta")])
                                P.op("dve", lambda e: e.tensor_tensor(out=tb[:, :], in0=py[:, 256:512], in1=gsb[:, 0:256], op=ALU.mult), r=[pyt, T("hy_gsb")], w=[T("hy_tb")])
                                P.op("dve", lambda e: e.tensor_tensor(out=Zs[:, kc, :], in0=ta[:, :], in1=tb[:, :], op=ALU.add), r=[T("hy_ta"), T("hy_tb")], w=[T("hy_Zs")])
                            for tn in range(nN):
                                P.dma("sp", cs[:, :, :], ic[tn], w=[T("hy_cs")])
                                P.dma("sp", sn[:, :, :], isn[tn], w=[T("hy_sn")])
                                pv, pvt = next_ps()
                                for kc in range(nK):
                                    P.op("pe", lambda e: e.matmul(pv[:, 0:256], cs[:, kc, :], Zc[:, kc, :], start=(kc == 0), stop=False), r=[T("hy_cs"), T("hy_Zc")], w=[pvt])
                                for kc in range(nK):
                                    P.op("pe", lambda e: e.matmul(pv[:, 0:256], sn[:, kc, :], Zs[:, kc, :], start=False, stop=(kc == nK - 1)), r=[T("hy_sn"), T("hy_Zs")], w=[pvt])
                                P.op("dve", lambda e: e.tensor_tensor(out=ta[:, :], in0=yb[:, c0 + tn, :], in1=bia[:, :], op=ALU.mult), r=[T("hy_y"), T("hy_bias")], w=[T("hy_ta")])
                                P.op("dve", lambda e: e.tensor_tensor(out=ta[:, :], in0=ta[:, :], in1=pv[:, 0:256], op=ALU.add), r=[T("hy_ta"), pvt], w=[T("hy_ta")])
                                if order == 0:
                                    P.op("dve", lambda e: e.tensor_tensor(out=yb[:, c0 + tn, :], in0=ta[:, :], in1=uu[1][:, c0 + tn, :], op=ALU.mult),
                                         r=[T("hy_ta"), T("hy_u1")], w=[T("hy_y")])
                                else:
                                    P.op("dve", lambda e: e.tensor_tensor(out=yn[:, :], in0=ta[:, :], in1=uu[2][:, c0 + tn, :], op=ALU.mult),
                                         r=[T("hy_ta"), T("hy_u2")], w=[T("hy_yn")])
                                    pt_, ptt = next_ps()
                                    for q in range(2):
                                        P.op("pe", lambda e: e.transpose(out=pt_[:, q * 128:(q + 1) * 128], in_=yn[:, q * 128:(q + 1) * 128], identity=ident),
                                             r=[T("hy_yn"), T("consts")], w=[ptt])
                                    if L == 2048 and half == 1:
                                        for q in range(2):
                                            dstv = ybr[:, half * 2 + q, CTX:NT].rearrange("p (r c) -> p c r", c=64)[:, 4 * tn:4 * tn + 4, :]
                                            evac(dstv, pt_[:, q * 128:(q + 1) * 128].rearrange("p (c r) -> p c r", r=32), r=[ptt], w=[T("ybr", 0)])
                                    else:
                                        tg = c0 + tn
                                        evac(ybr[:, half * 2:half * 2 + 2, tg * 128:(tg + 1) * 128], pt_[:, 0:256].rearrange("p (q c) -> p q c", c=128),
                                             r=[ptt], w=[T("ybr", 0)])
                        P.full_barrier()

        def merge_branch(l, br, ybr):
            with ExitStack() as ph:
                al = lambda n_, s_, dt=F32: ph.enter_context(sbt(n_, list(s_), dt))
                wg = al("m_wg", [128, 8, 1024], BF16)
                wb = al("m_wb", [128, 4, 1024], BF16)
                wo = al("m_wo", [128, 8, 1024], BF16)
                hb = al("m_hb", [128, 8, 512], BF16)
                sg = al("m_sg", [128, 512])
                mg = al("m_mg", [128, 8, 512], BF16)
                P.dma("pool", wg[:, :, :], w_mg[l, br], w=[T("m_wg")])
                P.dma("pool", wb[:, :, :], w_br[l, br], w=[T("m_wb")])
                P.dma("pool", wo[:, :, :], w_o[l], w=[T("m_wo")])
                for tt in range(5):
                    t0, n = TT[tt]
                    c = 1 if tt == 0 else 0
                    load_h1(tt, hb)
                    for d in range(8):
                        pg, pgt = next_ps()
                        for k in range(8):
                            P.op("pe", lambda e: e.matmul(pg[:, :n], wg[:, k, d * 128:(d + 1) * 128], hb[:, k, :n], start=(k == 0), stop=(k == 7)),
                                 r=[T("m_wg"), T("hbm", k)], w=[pgt])
                        P.op("act", lambda e: e.activation(out=sg[:, :n], in_=pg[:, :n], func=AF.Sigmoid), r=[pgt], w=[T("m_sg")])
                        pp, ppt = next_ps()
                        for cc in range(4):
                            P.op("pe", lambda e: e.matmul(pp[:, :n], wb[:, cc, d * 128:(d + 1) * 128], ybr[:, cc, t0:t0 + n], start=(cc == 0), stop=(cc == 3)),
                                 r=[T("m_wb")] + [T("ybr", tc) for tc in range(t0 // 128, (t0 + n) // 128)] + [T("ybr", 0)], w=[ppt])
                        P.op("dve", lambda e: e.tensor_tensor(out=mg[:, d, :n], in0=pp[:, :n], in1=sg[:, :n], op=ALU.mult),
                             r=[ppt, T("m_sg")], w=[T("m_mg", d)])
                    for d2 in range(8):
                        po, pot = next_ps()
                        for d in range(8):
                            P.op("pe", lambda e: e.matmul(po[:, :n], wo[:, d, d2 * 128:(d2 + 1) * 128], mg[:, d, :n], start=(d == 0), stop=(d == 7)),
                                 r=[T("m_wo"), T("m_mg", d)], w=[pot])
                        P.op("dve", lambda e: e.scalar_tensor_tensor(out=xs[:, d2, t0:t0 + n], in0=po[:, :n], scalar=gate[:, 1, d2, c:c + 1],
                                                                  in1=xs[:, d2, t0:t0 + n], op0=ALU.mult, op1=ALU.add),
                             r=[pot, T("gate", 1), T("xs", d2, tt)], w=[T("xs", d2, tt)])
                P.full_barrier()

        def mixer(l, which):
            store_h1()
            with ExitStack() as mp:
                ybr = mp.enter_context(sbt("ybr", [128, 4, NT], BF16))
                dbgmode = which.startswith("dbg_") or which.startswith("dbgn_")
                for br, (nm, fn) in enumerate((("gdn", gdn_branch), ("hy", hyena_branch), ("ssd", ssd_branch), ("gla", gla_branch))):
                    if dbgmode and nm not in which:
                        continue
                    fn(l, ybr)
                    if not dbgmode:
                        merge_branch(l, br, ybr)
                if dbgmode:
                    P.dma("sp", dbg[:, :, :], ybr[:, :, :], r=[T("ybr", tc) for tc in range(18)])
                    P.full_barrier()

        for l in range(nlayers):
            ada_layer(l)
            if not stage.startswith("dbgn"):
                ffn(l, 0, 0)
            if stage == "ffn0":
                break
            if stage.startswith("dbg"):
                mixer(l, stage)
                break
            mixer(l, "all")
            if stage == "mix":
                break
            ffn(l, 1, 2)

        if stage == "full":
            for tt in range(1, 5):
                t0, n = TT[tt]
                norm_tile(None, tt, xs, t0, "xs")
        for d in range(8):
            P.dma("sp", outT[:, d, :], xs[:, d, CTX:NT], r=[P.T("xs", d, t) for t in range(1, 5)])
        P.barrier_all("sp")
    return nc


def _prep_shared(inp, NL_=DEPTH):
    inp = {k: (np.asarray(v)[:NL_] if k in _LAYERED else v) for k, v in inp.items()}
    f = np.float32
    sh = {}
    w_ada = np.asarray(inp["w_ada"], f)
    sh["w_ada"] = np.ascontiguousarray(w_ada.reshape(NL_, 8, 128, 9, 1024).transpose(0, 3, 2, 1, 4))
    sh["b_ada"] = np.ascontiguousarray(np.asarray(inp["b_ada"], f).reshape(NL_, 72, 128).transpose(0, 2, 1))
    sh["norm_w"] = np.ascontiguousarray(np.asarray(inp["norm_w"], f).reshape(NL_, 3, 8, 128).transpose(0, 3, 1, 2))
    up = np.asarray(inp["ffn_up"], f).reshape(NL_, 2, 8, 128, 2, NJ, 128)
    sh["ffn_up"] = np.ascontiguousarray(up.transpose(0, 1, 5, 3, 2, 4, 6)).reshape(NL_, 2, NJ, 128, 8, 256)
    dn = np.asarray(inp["ffn_down"], f).reshape(NL_, 2, NJ, 128, 8, 128)
    sh["ffn_dn"] = np.ascontiguousarray(dn.transpose(0, 1, 4, 3, 2, 5))
    sh["fin_w"] = np.ascontiguousarray(np.asarray(inp["final_norm"], f).reshape(8, 128).T)
    wbr = np.asarray(inp["w_branch"], f).reshape(NL_, 4, 4, 128, 1024)
    sh["w_br"] = np.ascontiguousarray(wbr.transpose(0, 1, 3, 2, 4))
    sh["w_o"] = np.ascontiguousarray(np.asarray(inp["w_out"], f).reshape(NL_, 8, 128, 1024).transpose(0, 2, 1, 3))
    sh["consts"] = _consts()
    w_in = np.asarray(inp["w_in"], f)

    def wslab(cols):
        return np.ascontiguousarray(w_in[:, :, cols].reshape(NL_, 8, 128, len(cols)).transpose(0, 2, 1, 3))

    def rep(a):
        a = np.asarray(a, f).reshape(NL_, -1)
        return np.broadcast_to(a[:, None, :], (NL_, 128, a.shape[1]))

    def chanmajor(a, nch):
        a = np.asarray(a, f)
        K = a.shape[1]
        return a.reshape(NL_, K, nch, 128).transpose(0, 3, 2, 1).reshape(NL_, 128, nch * K)

    ar = np.arange
    sh["w_ssd"] = wslab(np.concatenate([4112 + ar(768), 3600 + ar(512), 4880 + ar(16)]))
    sh["w_gla"] = np.ascontiguousarray(np.stack([wslab(np.concatenate([4896 + pr * 128 + ar(128), 5152 + pr * 128 + ar(128),
                                                                      5408 + pr * 256 + ar(256), 5920 + pr * 256 + ar(256)]))
                                                  for pr in range(2)], axis=1))
    _glr = np.zeros((NL_, 128, 8, 128), f)
    _glr[:, :, :, 0:32] = wslab(6432 + ar(32))
    sh["w_glr"] = _glr
    gkw = np.asarray(inp["gla_gk_w"], f)
    gkw_pad = np.zeros((NL_, 128, 2, 256), f)
    gkw_pad[:, 0:16, 0, :] = gkw[:, 0]
    gkw_pad[:, 16:32, 1, :] = gkw[:, 1]
    sh["lp_gla"] = np.ascontiguousarray(np.concatenate([
        rep(inp["gla_gk_b"]), rep(np.tile(np.asarray(inp["gla_norm"], f), (1, 4))), gkw_pad.reshape(NL_, 128, 512)], axis=2))
    sh.update(_hy_consts())
    sh["w_hy"] = np.ascontiguousarray(np.stack([wslab(np.concatenate([2064 + g_ * 512 + hf * 256 + ar(256) for g_ in range(3)]))
                                                 for hf in range(2)], axis=1))
    lph = np.zeros((NL_, 128, 436), f)
    lph[:, 0:33, 0:64] = np.asarray(inp["hy_w1"], f)
    w2 = np.asarray(inp["hy_w2"], f)
    lph[:, 0:64, 128:192] = w2[:, 0]
    lph[:, 0:64, 256:320] = w2[:, 1]
    lph[:, 0:64, 384] = np.asarray(inp["hy_b1"], f)
    b2 = np.asarray(inp["hy_b2"], f)
    lph[:, 0:64, 385] = b2[:, 0]
    lph[:, 0:64, 386] = b2[:, 1]
    lph[:, 0:64, 387] = np.asarray(inp["hy_freq"], f)
    lph[:, :, 388:424] = chanmajor(np.asarray(inp["hy_conv_w"], f).reshape(NL_, 3, 1536), 12)
    lph[:, :, 424:436] = chanmajor(np.asarray(inp["hy_conv_b"], f).reshape(NL_, 1, 1536), 12)
    sh["lp_hy"] = lph
    wo = np.zeros((NL_, 128, 2048), f)
    wo[:, 0:64, :] = np.asarray(inp["hy_wout"], f)
    sh["hy_wout"] = wo
    sh["hy_biasr"] = np.ascontiguousarray(rep(inp["hy_bias"]))
    sh["w_gdn"] = np.ascontiguousarray(np.stack([wslab(np.concatenate([hh * 128 + ar(128), 512 + hh * 128 + ar(128),
                                                                      1024 + hh * 128 + ar(128), 1536 + hh * 128 + ar(128)]))
                                                  for hh in range(4)], axis=1))
    sh["w_gbd"] = wslab(2048 + ar(16))
    sh["lp_gdn"] = np.ascontiguousarray(np.concatenate([
        chanmajor(inp["gdn_conv"], 12), rep(inp["gdn_a_log"]), rep(inp["gdn_dt_bias"]),
        rep(np.tile(np.asarray(inp["gdn_norm"], f), (1, 4)))], axis=2))
    sh["w_mg"] = np.ascontiguousarray(np.stack([wslab(6464 + b_ * 1024 + ar(1024)) for b_ in range(4)], axis=1))
    sh["lp_ssd"] = np.ascontiguousarray(np.concatenate([
        chanmajor(inp["ssd_conv_w"], 6), chanmajor(np.asarray(inp["ssd_conv_b"], f)[:, None, :], 6),
        rep(inp["ssd_dt_bias"]), rep(inp["ssd_a_log"]), rep(inp["ssd_d"]), rep(inp["ssd_norm"])], axis=2))
    return sh


_LAYERED = ("w_ada", "b_ada", "norm_w", "ffn_up", "ffn_down", "w_in", "gdn_conv", "gdn_a_log", "gdn_dt_bias", "gdn_norm",
            "hy_conv_w", "hy_conv_b", "hy_w1", "hy_b1", "hy_w2", "hy_b2", "hy_wout", "hy_freq", "hy_bias",
            "ssd_conv_w", "ssd_conv_b", "ssd_a_log", "ssd_dt_bias", "ssd_d", "ssd_norm",
            "gla_gk_w", "gla_gk_b", "gla_norm", "w_branch", "w_out")


_HYC = {}


def _hy_consts():
    if _HYC:
        return _HYC
    import ml_dtypes
    bf = ml_dtypes.bfloat16
    out = {}
    for L, tag in ((2048, "l"), (256, "c")):
        N = 2 * L
        nN = L // 128
        nK = (L + 1 + 127) // 128
        t = np.linspace(0.0, 1.0, L, dtype=np.float32)[:, None].astype(np.float64)
        ang = 2.0 * np.pi * np.arange(L, dtype=np.float64)[:, None] / L
        fr = np.linspace(1e-4, 15, 16, dtype=np.float32)[None, :].astype(np.float64)
        z = np.concatenate([t, np.cos(fr * ang), -np.sin(fr * ang)], axis=-1)
        zT = np.zeros((128, L), np.float32)
        zT[0:33] = z.T
        out["hy_z" + tag] = zT
        max_decay = np.log(1e-2) / 0.3
        min_decay = np.log(1e-2) / 1.5
        deltas = np.abs(np.linspace(min_decay, max_decay, 512, dtype=np.float32)).astype(np.float64)
        win = (np.exp(-t * deltas[None, :]) + 0.05).astype(np.float32)
        out["hy_win_" + tag] = np.ascontiguousarray(win.reshape(nN, 128, 512))
        w0 = win[0:128].copy()
        w0[0, :] = 0.0
        out["hy_win0_" + tag] = w0
        n = np.arange(L, dtype=np.float64)
        k = np.arange(nK * 128, dtype=np.float64)
        th = 2.0 * np.pi / N
        valid = (k <= L)
        ph_ = th * np.outer(n, k)
        C = np.cos(ph_) * valid[None, :]
        S_ = np.sin(ph_) * valid[None, :]
        out["hy_fc_" + tag] = np.ascontiguousarray(C.reshape(nN, 128, nK, 128).transpose(2, 1, 0, 3)).astype(bf)
        out["hy_fs_" + tag] = np.ascontiguousarray(S_.reshape(nN, 128, nK, 128).transpose(2, 1, 0, 3)).astype(bf)
        wk = np.where((k == 0) | (k == L), 1.0 / N, 2.0 / N) * valid
        Ci = (C * wk[None, :]).T
        Si = (S_ * wk[None, :]).T
        out["hy_ic_" + tag] = np.ascontiguousarray(Ci.reshape(nK, 128, nN, 128).transpose(2, 1, 0, 3)).astype(bf)
        out["hy_is_" + tag] = np.ascontiguousarray(Si.reshape(nK, 128, nN, 128).transpose(2, 1, 0, 3)).astype(bf)
    _HYC.update(out)
    return _HYC


def _consts():
    i = np.arange(128)
    t, s_ = i[:, None], i[None, :]
    ident = (t == s_)
    tri0 = (t <= s_)
    tri1 = (t >= s_)
    v0 = (s_ >= t)
    v1 = (s_ <= t)
    st0 = (s_ > t)
    st1 = (s_ < t)
    parts = [ident, tri0, tri1, np.where(v0, 0.0, -1e5), np.where(v1, 0.0, -1e5), v0, v1, st0, st1]
    return np.ascontiguousarray(np.concatenate([np.asarray(p, np.float32) for p in parts], axis=1))


def _prep_core(inp, b):
    f = np.float32
    x = np.asarray(inp["x"][b], f)
    ctx = np.asarray(inp["ctx"][b], f)
    seq = np.concatenate([ctx, x], axis=0)
    xT = np.ascontiguousarray(seq.T.reshape(8, 128, NT).transpose(1, 0, 2))
    cond = np.stack([np.asarray(inp["c"][b], f).reshape(8, 128).T,
                     np.asarray(inp["c_ctx"], f).reshape(8, 128).T], axis=-1)
    return {"xT": xT, "cond": np.ascontiguousarray(cond)}


def run(inputs, nlayers=DEPTH, stage="full"):
    nc = build_nc(nlayers, stage)
    sh = _prep_shared(inputs, nlayers)
    in_maps = []
    for b in range(8):
        m = dict(sh)
        m.update(_prep_core(inputs, b))
        in_maps.append(m)
    res = run_bass_kernel_spmd(nc, in_maps, core_ids=list(range(8)))
    out = np.stack([r["outT"].transpose(2, 1, 0).reshape(SEQ, D) for r in res.results], axis=0)
    return out.astype(np.float32)


def kernel(**inputs):
    return run(inputs)
```
